# Optimizing a Trainium2 kernel written in Bass

```python
import math
import numpy as np
import jax, jax.numpy as jnp
from jax import lax

D_MODEL = 1024
BATCH = 2
SEQ = 8192
DEPTH = 2

N_EVEN = (DEPTH + 1) // 2
N_ODD = DEPTH // 2

MLSTM_HEADS = 4
MLSTM_HEAD_DIM = 128
MLSTM_W = MLSTM_HEADS * MLSTM_HEAD_DIM
MLSTM_CHUNK = 128
CONV_K = 4
NSA_HEADS = 8
NSA_GROUPS = 2
NSA_HPG = NSA_HEADS // NSA_GROUPS
NSA_HEAD_DIM = 64
NSA_W = NSA_HEADS * NSA_HEAD_DIM
NSA_KV_W = NSA_GROUPS * NSA_HEAD_DIM
CMP_LEN = 32
CMP_STRIDE = 16
CMP_HIDDEN = 128
SEL_BLOCK = 64
SEL_TOPK = 16
WINDOW = 512
Q_BLOCK = 128
ROPE_THETA = 500000.0
ROPE_DIM = NSA_HEAD_DIM // 4
EVEN_MIX_W = MLSTM_W + NSA_W
MLSTM_SPLITS = (MLSTM_W,) * 5 + (MLSTM_HEADS, MLSTM_HEADS)
NSA_SPLITS = (NSA_W,) + (NSA_KV_W,) * 6 + (3 * NSA_HEADS, NSA_W)
MLSTM_COLS = sum(MLSTM_SPLITS)
NSA_COLS = sum(NSA_SPLITS)
EVEN_IN_COLS = MLSTM_COLS + NSA_COLS
HGRN_HEADS = 8
HGRN_HEAD_DIM = 128
HGRN_W = HGRN_HEADS * HGRN_HEAD_DIM
HGRN_CHUNK = 64
ODD_IN_COLS = 4 * HGRN_W
EPS = 1e-6
NEG = -1e30

kernel_name = "hybrid_mlstm_nsa_hgrn2_trunk"


def split_cols(a, sizes):
    idx = np.cumsum(sizes)[:-1].tolist()
    return jnp.split(a, idx, axis=-1)


def rms_norm(x, w):
    xf = x.astype(jnp.float32)
    return xf * lax.rsqrt(jnp.mean(xf * xf, axis=-1, keepdims=True) + EPS) * w.astype(jnp.float32)


def heads(a, n_heads):
    b, s, _ = a.shape
    return a.reshape(b, s, n_heads, -1).transpose(0, 2, 1, 3)


def partial_rope(x, pos):
    half = ROPE_DIM // 2
    inv_freq = ROPE_THETA ** (-jnp.arange(half, dtype=jnp.float32) / half)
    ang = pos.astype(jnp.float32)[:, None] * inv_freq
    cos, sin = jnp.cos(ang), jnp.sin(ang)
    x1, x2, rest = x[..., :half], x[..., half:ROPE_DIM], x[..., ROPE_DIM:]
    return jnp.concatenate([x1 * cos - x2 * sin, x1 * sin + x2 * cos, rest], axis=-1)


def masked_softmax(s, mask):
    p = jax.nn.softmax(jnp.where(mask, s, NEG), axis=-1)
    return jnp.where(mask, p, 0.0)


def causal_conv(x, w, b):
    c = x.shape[-1]
    out = lax.conv_general_dilated(
        x, w.astype(jnp.float32).reshape(CONV_K, 1, c), window_strides=(1,),
        padding=[(CONV_K - 1, 0)], dimension_numbers=('NWC', 'WIO', 'NWC'),
        feature_group_count=c)
    return out + b


def mlstm_chunkwise(q, k, v, i_pre, f_pre):
    b_, h_, s_, d = q.shape
    L = MLSTM_CHUNK
    nc = s_ // L
    logf = jax.nn.log_sigmoid(f_pre)

    def to_chunks(a):
        return jnp.moveaxis(a.reshape(b_, h_, nc, L, *a.shape[3:]), 2, 0)

    causal = jnp.tril(jnp.ones((L, L), dtype=bool))

    def step(carry, xs):
        C, n, m = carry
        qj, kj, vj, ij, fj = xs
        bcum = jnp.cumsum(fj, axis=-1)
        D = jnp.where(causal, bcum[..., :, None] - bcum[..., None, :] + ij[..., None, :], -jnp.inf)
        inter = bcum + m[..., None]
        m_row = jnp.maximum(jnp.max(D, axis=-1), inter)
        Smat = jnp.einsum('bhjd,bhsd->bhjs', qj, kj) * jnp.exp(D - m_row[..., None])
        w_inter = jnp.exp(inter - m_row)
        num = jnp.einsum('bhjs,bhse->bhje', Smat, vj) + w_inter[..., None] * jnp.einsum('bhjd,bhde->bhje', qj, C)
        den = Smat.sum(-1) + w_inter * jnp.einsum('bhjd,bhd->bhj', qj, n)
        h = num / jnp.maximum(jnp.abs(den), jnp.exp(-m_row))[..., None]
        bL = bcum[..., -1]
        dl = bL[..., None] - bcum + ij
        m_new = jnp.maximum(bL + m, jnp.max(dl, axis=-1))
        wk = jnp.exp(dl - m_new[..., None])
        decay = jnp.exp(bL + m - m_new)
        C_new = decay[..., None, None] * C + jnp.einsum('bhs,bhsd,bhse->bhde', wk, kj, vj)
        n_new = decay[..., None] * n + jnp.einsum('bhs,bhsd->bhd', wk, kj)
        return (C_new, n_new, m_new), h

    init = (jnp.zeros((b_, h_, d, d), jnp.float32), jnp.zeros((b_, h_, d), jnp.float32),
            jnp.zeros((b_, h_), jnp.float32))
    _, hs = lax.scan(step, init, (to_chunks(q), to_chunks(k), to_chunks(v), to_chunks(i_pre), to_chunks(logf)))
    return jnp.moveaxis(hs, 0, 2).reshape(b_, h_, s_, d)


def mlstm_branch(cols, f_bias, conv_w, conv_b, head_norm):
    b_, s_, _ = cols.shape
    q, k, v, o, z, ig, fg = split_cols(cols, MLSTM_SPLITS)
    qk = jax.nn.silu(causal_conv(jnp.concatenate([q, k], axis=-1), conv_w, conv_b))
    q, k = jnp.split(qk, 2, axis=-1)
    qh = heads(q, MLSTM_HEADS)
    kh = heads(k, MLSTM_HEADS) * (MLSTM_HEAD_DIM ** -0.5)
    vh = heads(v, MLSTM_HEADS)
    i_pre = ig.transpose(0, 2, 1)
    f_pre = (fg + f_bias).transpose(0, 2, 1)
    h = mlstm_chunkwise(qh, kh, vh, i_pre, f_pre).transpose(0, 2, 1, 3)
    h = rms_norm(h, head_norm.reshape(MLSTM_HEADS, MLSTM_HEAD_DIM)).reshape(b_, s_, MLSTM_W)
    return h * jax.nn.sigmoid(o) * jax.nn.silu(z)


def compress(blocks, w1, w2):
    flat = blocks.reshape(*blocks.shape[:3], CMP_LEN * NSA_HEAD_DIM)
    return jax.nn.gelu(flat @ w1) @ w2


def nsa_branch(cols, q_norm, k_norm, ck_pos, ck_w1, ck_w2, cv_pos, cv_w1, cv_w2):
    b_, s_, _ = cols.shape
    G, R, d = NSA_GROUPS, NSA_HPG, NSA_HEAD_DIM
    q, kc, vc, ks, vs, kw, vw, gt, z = split_cols(cols, NSA_SPLITS)
    pos = jnp.arange(s_)
    qh = partial_rope(rms_norm(heads(q, NSA_HEADS), q_norm), pos) * (d ** -0.5)
    qh = qh.reshape(b_, G, R, s_, d)
    ks_h = partial_rope(rms_norm(heads(ks, G), k_norm[1]), pos)
    vs_h = heads(vs, G)
    kw_h = partial_rope(rms_norm(heads(kw, G), k_norm[2]), pos)
    vw_h = heads(vw, G)

    n_cmp = (s_ - CMP_LEN) // CMP_STRIDE + 1
    cmp_idx = np.arange(n_cmp)[:, None] * CMP_STRIDE + np.arange(CMP_LEN)[None, :]
    cmp_end = jnp.asarray(cmp_idx[:, -1])
    k_cmp = compress(heads(kc, G)[:, :, cmp_idx] + ck_pos, ck_w1, ck_w2)
    k_cmp = partial_rope(rms_norm(k_cmp, k_norm[0]), cmp_end)
    v_cmp = compress(heads(vc, G)[:, :, cmp_idx] + cv_pos, cv_w1, cv_w2)

    n_sel = s_ // SEL_BLOCK
    cs = np.arange(n_cmp) * CMP_STRIDE
    ss = np.arange(n_sel) * SEL_BLOCK
    ov = np.clip(np.minimum(cs[:, None] + CMP_LEN, ss[None, :] + SEL_BLOCK)
                 - np.maximum(cs[:, None], ss[None, :]), 0, None).astype(np.float32)
    ov = jnp.asarray(ov)
    k_top = min(SEL_TOPK, n_sel)

    gates = jax.nn.sigmoid(gt.reshape(b_, s_, NSA_HEADS, 3)).transpose(0, 2, 1, 3).reshape(b_, G, R, s_, 3)
    kw_pad = jnp.pad(kw_h, ((0, 0), (0, 0), (WINDOW, 0), (0, 0)))
    vw_pad = jnp.pad(vw_h, ((0, 0), (0, 0), (WINDOW, 0), (0, 0)))
    bi = jnp.arange(b_)[:, None, None, None]
    gi = jnp.arange(G)[None, :, None, None]
    nqb = s_ // Q_BLOCK

    def to_blocks(a):
        a = a.reshape(*a.shape[:-2], nqb, Q_BLOCK, a.shape[-1])
        return jnp.moveaxis(a, -3, 0)

    def block_fn(args):
        qb, q_blk, g_blk = args
        t = qb * Q_BLOCK + jnp.arange(Q_BLOCK)
        s_c = jnp.einsum('bgrqd,bgnd->bgrqn', q_blk, k_cmp)
        p_c = masked_softmax(s_c, cmp_end[None, :] <= t[:, None])
        o_c = jnp.einsum('bgrqn,bgnd->bgrqd', p_c, v_cmp)
        imp = jnp.einsum('bgrqn,nm->bgqm', p_c, ov)
        blk = jnp.arange(n_sel)[None, :]
        cur = (t // SEL_BLOCK)[:, None]
        forced = (blk == 0) | (blk == cur) | (blk == cur - 1)
        imp = jnp.where(blk > cur, NEG, jnp.where(forced, -NEG, imp))
        _, sel = lax.top_k(imp, k_top)
        valid = sel <= cur[None, None]
        tok = (sel[..., None] * SEL_BLOCK + jnp.arange(SEL_BLOCK)).reshape(b_, G, Q_BLOCK, k_top * SEL_BLOCK)
        tok_ok = jnp.repeat(valid, SEL_BLOCK, axis=-1) & (tok <= t[:, None])
        ks_g = ks_h[bi, gi, tok]
        vs_g = vs_h[bi, gi, tok]
        s_s = jnp.einsum('bgrqd,bgqnd->bgrqn', q_blk, ks_g)
        p_s = masked_softmax(s_s, tok_ok[:, :, None])
        o_s = jnp.einsum('bgrqn,bgqnd->bgrqd', p_s, vs_g)
        kwb = lax.dynamic_slice_in_dim(kw_pad, qb * Q_BLOCK, Q_BLOCK + WINDOW, axis=2)
        vwb = lax.dynamic_slice_in_dim(vw_pad, qb * Q_BLOCK, Q_BLOCK + WINDOW, axis=2)
        sp = qb * Q_BLOCK - WINDOW + jnp.arange(Q_BLOCK + WINDOW)
        wmask = (sp[None, :] >= 0) & (sp[None, :] <= t[:, None]) & (t[:, None] - sp[None, :] < WINDOW)
        s_w = jnp.einsum('bgrqd,bgnd->bgrqn', q_blk, kwb)
        p_w = masked_softmax(s_w, wmask)
        o_w = jnp.einsum('bgrqn,bgnd->bgrqd', p_w, vwb)
        return g_blk[..., 0:1] * o_c + g_blk[..., 1:2] * o_s + g_blk[..., 2:3] * o_w

    out = lax.map(block_fn, (jnp.arange(nqb), to_blocks(qh), to_blocks(gates)))
    out = jnp.moveaxis(out, 0, 3).reshape(b_, NSA_HEADS, s_, d).transpose(0, 2, 1, 3).reshape(b_, s_, NSA_W)
    return out * jax.nn.silu(z)


def hgrn2_chunkwise(q, k, v, logf):
    b_, h_, s_, dk = q.shape
    dv = v.shape[-1]
    L = HGRN_CHUNK
    nc = s_ // L

    def to_chunks(a):
        return jnp.moveaxis(a.reshape(b_, h_, nc, L, a.shape[-1]), 2, 0)

    causal = jnp.tril(jnp.ones((L, L), dtype=bool))[..., None]

    def step(Sst, xs):
        qj, kj, vj, gj = xs
        bcum = jnp.cumsum(gj, axis=-2)
        diff = bcum[..., :, None, :] - bcum[..., None, :, :]
        dec = jnp.exp(jnp.where(causal, diff, -jnp.inf))
        A = jnp.einsum('bhjc,bhsc,bhjsc->bhjs', qj, kj, dec)
        o = jnp.einsum('bhjs,bhsv->bhjv', A, vj) + jnp.einsum('bhjc,bhcv->bhjv', qj * jnp.exp(bcum), Sst)
        bL = bcum[..., -1, :]
        S_new = jnp.exp(bL)[..., None] * Sst + jnp.einsum('bhsc,bhsv->bhcv', kj * jnp.exp(bL[..., None, :] - bcum), vj)
        return S_new, o

    init = jnp.zeros((b_, h_, dk, dv), jnp.float32)
    _, os_ = lax.scan(step, init, (to_chunks(q), to_chunks(k), to_chunks(v), to_chunks(logf)))
    return jnp.moveaxis(os_, 0, 2).reshape(b_, h_, s_, dv)


def hgrn2_layer(h, w_in, b_in, lb, head_norm, w_out):
    b_, s_, _ = h.shape
    proj = h @ w_in + b_in
    fp, iv, qp, z = jnp.split(proj, 4, axis=-1)
    logf = jnp.log(lb + (1.0 - lb) * jax.nn.sigmoid(fp))
    k = (1.0 - lb) * jax.nn.sigmoid(-fp)
    q = jax.nn.silu(qp)
    o = hgrn2_chunkwise(heads(q, HGRN_HEADS), heads(k, HGRN_HEADS), heads(iv, HGRN_HEADS), heads(logf, HGRN_HEADS))
    o = rms_norm(o.transpose(0, 2, 1, 3), head_norm.reshape(HGRN_HEADS, HGRN_HEAD_DIM)).reshape(b_, s_, HGRN_W)
    return (o * jax.nn.silu(z)) @ w_out


def even_layer(h, w_in, b_in, f_bias, conv_w, conv_b, head_norm, q_norm, k_norm,
               ck_pos, ck_w1, ck_w2, cv_pos, cv_w1, cv_w2, w_out):
    proj = h @ w_in + b_in
    m_cols, n_cols = jnp.split(proj, [MLSTM_COLS], axis=-1)
    ya = mlstm_branch(m_cols, f_bias, conv_w, conv_b, head_norm)
    yb = nsa_branch(n_cols, q_norm, k_norm, ck_pos, ck_w1, ck_w2, cv_pos, cv_w1, cv_w2)
    return jnp.concatenate([ya, yb], axis=-1) @ w_out


def setup_inputs(seed: int = 0) -> dict:
    key = jax.random.key(seed)
    ks = jax.random.split(key, 24)

    def nrm(k, shape, scale):
        return scale * jax.random.normal(k, shape, jnp.float32)

    NE, NO = N_EVEN, N_ODD
    return {
        "x": nrm(ks[0], (BATCH, SEQ, D_MODEL), 1.0),
        "norm_w": 1.0 + nrm(ks[1], (DEPTH, D_MODEL), 0.02),
        "ev_w_in": nrm(ks[2], (NE, D_MODEL, EVEN_IN_COLS), D_MODEL ** -0.5),
        "ev_b_in": nrm(ks[3], (NE, EVEN_IN_COLS), 0.02),
        "mlstm_f_bias": jnp.linspace(3.0, 6.0, MLSTM_HEADS, dtype=jnp.float32)[None, :] + nrm(ks[4], (NE, MLSTM_HEADS), 0.1),
        "mlstm_conv_w": nrm(ks[5], (NE, CONV_K, 2 * MLSTM_W), CONV_K ** -0.5),
        "mlstm_conv_b": nrm(ks[6], (NE, 2 * MLSTM_W), 0.02),
        "mlstm_head_norm": 1.0 + nrm(ks[7], (NE, MLSTM_W), 0.02),
        "nsa_q_norm": 1.0 + nrm(ks[8], (NE, NSA_HEAD_DIM), 0.02),
        "nsa_k_norm": 1.0 + nrm(ks[9], (NE, 3, NSA_HEAD_DIM), 0.02),
        "cmp_k_pos": nrm(ks[10], (NE, CMP_LEN, NSA_HEAD_DIM), 0.1),
        "cmp_k_w1": nrm(ks[11], (NE, CMP_LEN * NSA_HEAD_DIM, CMP_HIDDEN), (CMP_LEN * NSA_HEAD_DIM) ** -0.5),
        "cmp_k_w2": nrm(ks[12], (NE, CMP_HIDDEN, NSA_HEAD_DIM), CMP_HIDDEN ** -0.5),
        "cmp_v_pos": nrm(ks[13], (NE, CMP_LEN, NSA_HEAD_DIM), 0.1),
        "cmp_v_w1": nrm(ks[14], (NE, CMP_LEN * NSA_HEAD_DIM, CMP_HIDDEN), (CMP_LEN * NSA_HEAD_DIM) ** -0.5),
        "cmp_v_w2": nrm(ks[15], (NE, CMP_HIDDEN, NSA_HEAD_DIM), CMP_HIDDEN ** -0.5),
        "ev_w_out": nrm(ks[16], (NE, EVEN_MIX_W, D_MODEL), EVEN_MIX_W ** -0.5),
        "od_w_in": nrm(ks[17], (NO, D_MODEL, ODD_IN_COLS), D_MODEL ** -0.5),
        "od_b_in": nrm(ks[18], (NO, ODD_IN_COLS), 0.02),
        "hgrn_lb_logits": nrm(ks[19], (NO + 1, HGRN_W), 0.5),
        "hgrn_head_norm": 1.0 + nrm(ks[20], (NO, HGRN_W), 0.02),
        "od_w_out": nrm(ks[21], (NO, HGRN_W, D_MODEL), HGRN_W ** -0.5),
    }


def reference(x, norm_w, ev_w_in, ev_b_in, mlstm_f_bias, mlstm_conv_w, mlstm_conv_b, mlstm_head_norm,
              nsa_q_norm, nsa_k_norm, cmp_k_pos, cmp_k_w1, cmp_k_w2, cmp_v_pos, cmp_v_w1, cmp_v_w2,
              ev_w_out, od_w_in, od_b_in, hgrn_lb_logits, hgrn_head_norm, od_w_out):
    lbs = jnp.cumsum(jax.nn.softmax(hgrn_lb_logits.astype(jnp.float32), axis=0), axis=0)
    for layer in range(DEPTH):
        h = rms_norm(x, norm_w[layer])
        j = layer // 2
        if layer % 2 == 0:
            y = even_layer(h, ev_w_in[j], ev_b_in[j], mlstm_f_bias[j], mlstm_conv_w[j], mlstm_conv_b[j],
                           mlstm_head_norm[j], nsa_q_norm[j], nsa_k_norm[j], cmp_k_pos[j], cmp_k_w1[j],
                           cmp_k_w2[j], cmp_v_pos[j], cmp_v_w1[j], cmp_v_w2[j], ev_w_out[j])
        else:
            y = hgrn2_layer(h, od_w_in[j], od_b_in[j], lbs[j], hgrn_head_norm[j], od_w_out[j])
        x = x + y.astype(x.dtype)
    return x
```

```python
import numpy as np
import ml_dtypes
from contextlib import ExitStack
import threading
import concourse.bass as bass
import concourse.mybir as mybir
from concourse.bass_utils import run_bass_kernel_spmd

F32 = mybir.dt.float32
BF16 = mybir.dt.bfloat16
AF = mybir.ActivationFunctionType
ALU = mybir.AluOpType
AX = mybir.AxisListType
NPBF = ml_dtypes.bfloat16

SEM_LIMIT = 30000
FUSE_WAIT = True
NO_SES = set()
AW = 0.75
BW1 = 1
BW2 = 2


class Sem:
    __slots__ = ("h", "v")

    def __init__(self, h):
        self.h = h
        self.v = 0


class Buf:
    __slots__ = ("name", "w", "r", "dsem", "excl")

    def __init__(self, name=""):
        self.name = name
        self.excl = False
        self.w = None
        self.r = {}
        self.dsem = None


class Tile:
    def __init__(self, t, b):
        self.t = t
        self.b = b

    def __getitem__(self, k):
        return self.t[k]


class Queue:
    def __init__(self, name, eng):
        self.name = name
        self.eng = eng
        self.sem = None
        self.known = {}
        self.own = set()


class Sched:
    def __init__(self, nc, same_engine_sync=True):
        self.nc = nc
        self.es = ExitStack()
        self.nsem = 0
        self.same_engine_sync = same_engine_sync
        self.q = {}
        for name, eng in [("pe", nc.tensor), ("dve", nc.vector), ("act", nc.scalar),
                          ("pool", nc.gpsimd), ("sp", nc.sync)]:
            q = Queue(name, eng)
            self.q[name] = q
        for name in ("pe", "dve", "act", "pool"):
            q = self.q[name]
            q.sem = self.new_sem(name)
            q.own.add(q.sem)
        self.ninst = 0
        self.nwait = 0
        self.out_toks = []
        self.scopes = []
        self.dsems = []
        self._coop = None
        self._flags = set()
        self._tls = threading.local()

    def new_sem(self, name):
        self.nsem += 1
        h = self.es.enter_context(self.nc.semaphore(f"s{self.nsem}_{name}"))
        return Sem(h)

    def push_scope(self):
        self.scopes.append(ExitStack())

    def pop_scope(self):
        self.scopes.pop().close()

    def _stack(self):
        return self.scopes[-1] if self.scopes else self.es

    def barrier(self):
        toks = []
        for qn in ("pe", "dve", "act", "pool"):
            for sem in self.q[qn].own:
                if sem.v > 0:
                    toks.append((sem, sem.v))
        for sem in self.dsems:
            if sem.v > 0:
                toks.append((sem, sem.v))
        for qn in ("pe", "dve", "act", "pool", "sp"):
            q = self.q[qn]
            for (sem, val) in toks:
                self._wait(q, sem, val)

    def sbuf(self, name, shape, dtype):
        t = self._stack().enter_context(self.nc.sbuf_tensor(name, list(shape), dtype))
        return Tile(t, Buf(name))

    def psum(self, name, shape, dtype):
        t = self._stack().enter_context(self.nc.psum_tensor(name, list(shape), dtype))
        b = Buf(name)
        b.excl = True
        return Tile(t, b)

    @staticmethod
    def _bufs(lst):
        out = []
        for x in lst:
            if x is None:
                continue
            out.append(x.b if isinstance(x, Tile) else x)
        return out

    def _wait(self, q, sem, val):
        if q.known.get(sem, 0) >= val:
            return
        q.eng.wait_ge(sem.h, val)
        q.known[sem] = val
        self.nwait += 1

    def op(self, qname, fn, r=(), w=(), dma=None):
        q = self.q[qname]
        rb = self._bufs(r)
        wb = self._bufs(w)
        ex = [b for b in rb if b.excl and b not in wb]
        if ex:
            wb = wb + ex
            rb = [b for b in rb if not b.excl]
        deps = []
        for b in rb:
            if b.w is not None:
                deps.append(b.w)
        for b in wb:
            if b.w is not None:
                deps.append(b.w)
            deps.extend(b.r.items())
        need = {}
        for (sem, val) in deps:
            if sem in q.own and dma is None:
                if qname == "pe" or not self.same_engine_sync or qname in NO_SES:
                    continue
            if q.known.get(sem, 0) >= val:
                continue
            if need.get(sem, 0) < val:
                need[sem] = val
        need = list(need.items())
        fused = None
        if FUSE_WAIT and need and dma is None:
            fused = need.pop()
        for (sem, val) in need:
            self._wait(q, sem, val)
        ins = fn(q.eng)
        if fused is not None:
            ins._wait_ge(fused[0].h, fused[1])
            q.known[fused[0]] = fused[1]
        self.ninst += 1
        if dma is not None:
            ob = dma.b if isinstance(dma, Tile) else dma
            if ob.dsem is None or ob.dsem.v + 16 > SEM_LIMIT:
                ob.dsem = self.new_sem("d_" + ob.name)
                self.dsems.append(ob.dsem)
            ob.dsem.v += 16
            ins.then_inc(ob.dsem.h, 16)
            tok = (ob.dsem, ob.dsem.v)
        else:
            if q.sem.v + 1 > SEM_LIMIT:
                q.sem = self.new_sem(qname)
                q.own.add(q.sem)
            q.sem.v += 1
            ins.then_inc(q.sem.h, 1)
            tok = (q.sem, q.sem.v)
            q.known[q.sem] = max(q.known.get(q.sem, 0), 0)
        for b in wb:
            b.w = tok
            b.r = {}
        for b in rb:
            if b in wb:
                continue
            if b.r.get(tok[0], 0) < tok[1]:
                b.r[tok[0]] = tok[1]
        if self._coop is not None:
            st = self._coop
            i = self._tls.idx
            st["cnt"][i] += 1
            if st["cnt"][i] >= st["w"][i]:
                st["cnt"][i] = 0
                self._yield()
        return tok

    def parallel(self, fns, weights=None):
        assert self._coop is None
        n = len(fns)
        if n == 1:
            fns[0]()
            return
        st = {"turn": 0, "alive": [True] * n, "cv": threading.Condition(), "err": None,
              "w": list(weights) if weights else [1] * n, "cnt": [0] * n}
        self._coop = st

        def nxt_alive(i):
            for k in range(1, n + 1):
                c = (i + k) % n
                if st["alive"][c]:
                    return c
            return None

        def runner(i, fn):
            self._tls.idx = i
            with st["cv"]:
                while st["turn"] != i:
                    st["cv"].wait()
            try:
                if st["err"] is None:
                    fn()
            except BaseException as e:
                st["err"] = e
            finally:
                with st["cv"]:
                    st["alive"][i] = False
                    c = nxt_alive(i)
                    st["turn"] = c if c is not None else -1
                    st["cv"].notify_all()

        st["nxt"] = nxt_alive
        ths = [threading.Thread(target=runner, args=(i, f)) for i, f in enumerate(fns)]
        for t in ths:
            t.start()
        for t in ths:
            t.join()
        self._coop = None
        if st["err"] is not None:
            raise st["err"]

    def flag_set(self, name):
        self._flags.add(name)

    def flag_wait(self, name):
        assert self._coop is not None, "flag_wait outside parallel section would deadlock"
        while name not in self._flags:
            if self._coop["err"] is not None:
                raise RuntimeError("sibling emitter failed")
            self._yield()

    def _yield(self):
        st = self._coop
        i = self._tls.idx
        if st["err"] is not None:
            raise RuntimeError("sibling emitter failed")
        with st["cv"]:
            c = st["nxt"](i)
            if c is not None and c != i:
                st["turn"] = c
                st["cv"].notify_all()
                while st["turn"] != i:
                    st["cv"].wait()

    def finish(self, toks):
        q = self.q["sp"]
        for (sem, val) in toks:
            self._wait(q, sem, val)

    def close(self):
        self.es.close()


D = 1024
SEQ = 8192
CS = 64
NCH = 128 // CS
NB = SEQ // 128
EPS = 1e-6


def dram_in(nc, name, shape, dtype):
    return nc.dram_tensor(name, list(shape), dtype, kind="ExternalInput").ap()


def dram_out(nc, name, shape, dtype):
    return nc.dram_tensor(name, list(shape), dtype, kind="ExternalOutput").ap()


def load_cast_weight(S, w_dram, w_bf, ncols, stage, scale_cols=None, nk=8, qname="dve"):
    wv = w_dram.rearrange("(c p) n -> p c n", p=128)
    for kc in range(nk):
        st = stage[kc % len(stage)]
        S.op("sp", lambda e: e.dma_start(out=st[:, 0:ncols], in_=wv[:, kc, :]), w=[st], dma=st)
        if scale_cols is not None:
            S.op(qname, lambda e: e.tensor_scalar(out=w_bf[:, kc, :], in0=st[:, 0:ncols], scalar1=scale_cols[:, kc:kc + 1],
                                                  scalar2=None, op0=ALU.mult), r=[st, scale_cols], w=[w_bf])
        else:
            S.op(qname, lambda e: e.tensor_copy(out=w_bf[:, kc, :], in_=st[:, 0:ncols]), r=[st], w=[w_bf])


def rms_rstd(S, ss, n, npart, ncol, tmp):
    S.op("act", lambda e: e.activation(out=tmp[0:npart, 0:ncol], in_=ss[0:npart, 0:ncol], func=AF.Ln, scale=1.0 / n, bias=EPS),
         r=[ss], w=[tmp])
    S.op("act", lambda e: e.activation(out=ss[0:npart, 0:ncol], in_=tmp[0:npart, 0:ncol], func=AF.Exp, scale=-0.5),
         r=[tmp], w=[ss])


def build_BC(S, io, nblk=NB):
    x, wout0, normw1, win1, bcol1, lbl, hn1 = (io[k] for k in ("xr", "wout0", "normw1", "win1", "bcol1", "lbl", "hn1"))
    identd, mask32d = io["ident"], io["mask32"]
    x1s = io["x1s"]
    out_toks = []

    ident = S.sbuf("c_ident", [128, 128], BF16)
    mask32 = S.sbuf("c_mask32", [CS, 128], F32)
    normw = S.sbuf("c_normw", [128, 8], F32)
    bcol = S.sbuf("c_bcol", [128, 8], F32)
    nbcol = S.sbuf("c_nbcol", [128, 8], F32)
    lblt = S.sbuf("c_lbl", [128, 4], F32)
    lb = S.sbuf("c_lb", [128, 2], F32)
    oml = S.sbuf("c_oml", [128, 2], F32)
    hn = S.sbuf("c_hn", [128, 2], F32)
    ones = S.sbuf("c_ones", [128, 128], F32)
    for (t, d) in ((ident, identd), (mask32, mask32d), (normw, normw1), (bcol, bcol1), (lblt, lbl), (hn, hn1)):
        S.op("sp", lambda e: e.dma_start(out=t[:, :], in_=d[:, :]), w=[t], dma=t)
    S.op("pool", lambda e: e.memset(ones[:, :], 1.0), w=[ones])
    S.op("dve", lambda e: e.tensor_scalar(out=nbcol[:, :], in0=bcol[:, :], scalar1=-1.0, scalar2=None, op0=ALU.mult), r=[bcol], w=[nbcol])
    for hd in range(2):
        S.op("dve", lambda e: e.tensor_tensor(out=lb[:, hd:hd + 1], in0=lblt[:, 2 * hd + 1:2 * hd + 2], in1=lblt[:, 2 * hd:2 * hd + 1], op=ALU.subtract),
             r=[lblt], w=[lb])
    S.op("act", lambda e: e.activation(out=lb[:, :], in_=lb[:, :], func=AF.Exp), r=[lb], w=[lb])
    S.op("dve", lambda e: e.tensor_scalar(out=lb[:, :], in0=lb[:, :], scalar1=1.0, scalar2=None, op0=ALU.add), r=[lb], w=[lb])
    S.op("dve", lambda e: e.reciprocal(out=lb[:, :], in_=lb[:, :]), r=[lb], w=[lb])
    S.op("dve", lambda e: e.tensor_scalar(out=oml[:, :], in0=lb[:, :], scalar1=-1.0, scalar2=1.0, op0=ALU.mult, op1=ALU.add), r=[lb], w=[oml])

    if "bc_weights" in io:
        wout0_bf, win1_bf = io["bc_weights"]
    else:
        stage = [S.sbuf(f"wstage{i}", [128, 1024], F32) for i in range(2)]
        wout0_bf = S.sbuf("wout0_bf", [128, 8, 1024], BF16)
        win1_bf = S.sbuf("win1_bf", [128, 8, 1024], BF16)
        load_cast_weight(S, wout0, wout0_bf, 1024, stage)
        load_cast_weight(S, win1, win1_bf, 1024, stage, scale_cols=normw)

    x_t = [S.sbuf(f"x_t{i}", [128, 1024], F32) for i in range(2)]
    y0_t = [S.sbuf(f"y0_t{i}", [128, 8, 128], BF16) for i in range(2)]
    x1_t = [S.sbuf(f"x1_t{i}", [128, 1024], F32) for i in range(2)]
    junk = S.sbuf("junk", [128, 1024], BF16)
    ssx = S.sbuf("ssx", [128, 1], F32)
    sstmp = S.sbuf("sstmp", [128, 4], F32)
    h_bf = S.sbuf("h_bf", [128, 1024], BF16)
    hT_l = [S.sbuf(f"hT{i}", [128, 1024], BF16) for i in range(2)]
    t1 = [S.sbuf(f"t1_{i}", [128, 128], F32) for i in range(2)]
    t2 = [S.sbuf(f"t2_{i}", [128, 128], F32) for i in range(2)]
    t3 = [S.sbuf(f"t3_{i}", [128, 128], F32) for i in range(2)]
    fT = [S.sbuf(f"fT{i}", [128, 128], F32) for i in range(2)]
    lfT = [S.sbuf(f"lfT{i}", [128, 128], F32) for i in range(2)]
    kT = [S.sbuf(f"kT{i}", [128, 128], F32) for i in range(2)]
    qT = [S.sbuf(f"qT{i}", [128, 128], F32) for i in range(2)]
    szT_l = [[S.sbuf(f"szT{p}_{i}", [128, 128], F32) for i in range(2)] for p in range(2)]
    szT = szT_l[0]
    vT_l = [[S.sbuf(f"vT{p}_{i}", [128, 128], BF16) for i in range(2)] for p in range(2)]
    pre_l = [[S.sbuf(f"pre{p}_{g}", [128, 128], F32) for g in range(8)] for p in range(2)]
    G = [S.sbuf(f"G{i}", [128, 128], F32) for i in range(2)]
    eq = [S.sbuf(f"eq{i}", [128, 128], F32) for i in range(2)]
    ek = [S.sbuf(f"ek{i}", [128, 128], F32) for i in range(2)]
    dec_l = [[S.sbuf(f"dec{p}_{i}", [128, NCH], F32) for i in range(2)] for p in range(2)]
    dec = dec_l[0]
    qt_bf_l = [[S.sbuf(f"qt_bf{p}_{i}", [128, 128], BF16) for i in range(2)] for p in range(2)]
    kt_bf = [S.sbuf(f"kt_bf{i}", [128, 128], BF16) for i in range(2)]
    kh_bf = [S.sbuf(f"kh_bf{i}", [128, 128], BF16) for i in range(2)]
    AT_bf_l = [[S.sbuf(f"AT_bf{p}_{i}", [CS, 128], BF16) for i in range(2)] for p in range(2)]
    kh_tok_l = [[S.sbuf(f"kh_tok{p}_{i}", [CS, NCH * 128], BF16) for i in range(2)] for p in range(2)]
    v_tok_l = [[S.sbuf(f"v_tok{p}_{i}", [CS, NCH * 128], BF16) for i in range(2)] for p in range(2)]
    S32 = [S.sbuf(f"S32_{i}", [128, 128], F32) for i in range(2)]
    S_bf = [S.sbuf(f"S_bf{i}", [128, 128], BF16) for i in range(2)]
    osq_l = [S.sbuf(f"osq{i}", [CS, NCH * 128], F32) for i in range(2)]
    oss_l = [S.sbuf(f"oss{i}", [CS, NCH], F32) for i in range(2)]
    on_bf_l = [S.sbuf(f"on_bf{i}", [CS, NCH * 128], BF16) for i in range(2)]
    sstmp_l = [S.sbuf(f"sstmp_h{i}", [128, 4], F32) for i in range(2)]
    y_bf = [S.sbuf(f"y_bf{i}", [128, 128], BF16) for i in range(4)]

    ps_a = [S.psum(f"ps_a{i}", [128, 512], F32) for i in range(2)]
    ps_hT = S.psum("ps_hT", [128, 1024], BF16)
    ps_m1 = S.psum("ps_m1", [128, 512], F32)
    ps_m2_l = [S.psum(f"ps_m2_{i}", [128, 1024], BF16) for i in range(2)]
    ps_o_l = [S.psum(f"ps_o_{i}", [128, 512], F32) for i in range(2)]

    for hd in range(2):
        S.op("pool", lambda e: e.memset(S32[hd][:, :], 0.0), w=[S32[hd]])
        S.op("pool", lambda e: e.memset(S_bf[hd][:, :], 0.0), w=[S_bf[hd]])

    def load(j):
        sl = j % 2
        S.op("sp", lambda e: e.dma_start(out=x_t[sl][:, :], in_=x[j * 128:(j + 1) * 128, :]), w=[x_t[sl]], dma=x_t[sl])
        S.op("sp", lambda e: e.dma_start(out=y0_t[sl][:, :, :], in_=io["y0src"](j)), r=[io["y0buf"](j)], w=[y0_t[sl]], dma=y0_t[sl])

    load(0)

    def common(j):
        sl = j % 2
        if j + 1 < nblk:
            load(j + 1)
        xt, yt, x1 = x_t[sl], y0_t[sl], x1_t[sl]
        hT = hT_l[sl]
        for n in range(2):
            for c in range(8):
                S.op("pe", lambda e: e.matmul(ps_a[n][:, :], lhsT=yt[:, c, :], rhs=wout0_bf[:, c, n * 512:(n + 1) * 512],
                                              start=(c == 0), stop=(c == 7)), r=[yt, wout0_bf], w=[ps_a[n]])
        for n in range(2):
            S.op("dve", lambda e: e.tensor_tensor(out=x1[:, n * 512:(n + 1) * 512], in0=ps_a[n][:, :], in1=xt[:, n * 512:(n + 1) * 512], op=ALU.add),
                 r=[ps_a[n], xt], w=[x1])
        S.flag_set(("pa", j))
        out_toks.append(S.op("pool", lambda e: e.dma_start(out=x1s[j * 128:(j + 1) * 128, :], in_=x1[:, 0:256]), r=[x1], dma=x1))
        S.op("act", lambda e: e.activation(out=junk[:, :], in_=x1[:, :], func=AF.Square, accum_out=ssx[:, 0:1]), r=[x1], w=[junk, ssx])
        rms_rstd(S, ssx, 1024, 128, 1, sstmp)
        S.op("act", lambda e: e.activation(out=h_bf[:, :], in_=x1[:, :], func=AF.Copy, scale=ssx[:, 0:1]), r=[x1, ssx], w=[h_bf])
        for hf in range(2):
            for c in range(4):
                cc = hf * 4 + c
                S.op("pe", lambda e: e.transpose(out=ps_hT[:, c * 128:(c + 1) * 128], in_=h_bf[:, cc * 128:(cc + 1) * 128], identity=ident[:, :]),
                     r=[h_bf, ident], w=[ps_hT])
            S.op("dve", lambda e: e.tensor_copy(out=hT[:, hf * 512:(hf + 1) * 512], in_=ps_hT[:, 0:512]), r=[ps_hT], w=[hT])


    def common2(j):
        sl = j % 2
        hT = hT_l[sl]
        S.flag_wait(("pa", j + 1))
        for o in range(8):
            pt = ps_a[o // 4]
            for c in range(8):
                S.op("pe", lambda e: e.matmul(pt[:, (o % 4) * 128:(o % 4 + 1) * 128], lhsT=win1_bf[:, c, o * 128:(o + 1) * 128],
                                              rhs=hT[:, c * 128:(c + 1) * 128], start=(c == 0), stop=(c == 7)), r=[win1_bf, hT], w=[pt])
        for g in range(8):
            dst = vT_l[sl][g - 4] if g in (4, 5) else pre_l[sl][g]
            srcp = ps_a[g // 4][:, (g % 4) * 128:(g % 4 + 1) * 128]
            if g % 2 == 0:
                S.op("act", lambda e: e.activation(out=dst[:, :], in_=srcp, func=AF.Identity, bias=bcol[:, g:g + 1]), r=[ps_a[g // 4], bcol], w=[dst])
            else:
                S.op("dve", lambda e: e.tensor_scalar(out=dst[:, :], in0=srcp, scalar1=bcol[:, g:g + 1], scalar2=None, op0=ALU.add), r=[ps_a[g // 4], bcol], w=[dst])

    if True:
        def head(hd, j, part):
            sl = j % 2
            vT = vT_l[sl]
            pre = pre_l[sl]
            ps_m2, ps_o, osq, oss, on_bf, sstmp_h = ps_m2_l[hd], ps_o_l[hd], osq_l[hd], oss_l[hd], on_bf_l[hd], sstmp_l[hd]
            ma = hd * 256
            ms = hd * 256 + 128
            ps_f = ps_a[0][:, hd * 128:(hd + 1) * 128]
            ps_q = ps_a[0][:, (2 + hd) * 128:(3 + hd) * 128]
            ps_v = ps_a[1][:, hd * 128:(hd + 1) * 128]
            ps_z = ps_a[1][:, (2 + hd) * 128:(3 + hd) * 128]
            T1, FT, LF, KT, QT, SZ, VT, GG, EQ, EK, DEC = t1[hd], fT[hd], lfT[hd], kT[hd], qT[hd], szT_l[sl][hd], vT[hd], G[hd], eq[hd], ek[hd], dec_l[sl][hd]
            qtb, ATb, khk, vtk = qt_bf_l[sl][hd], AT_bf_l[sl][hd], kh_tok_l[sl][hd], v_tok_l[sl][hd]
            T2, T3 = t2[hd], t3[hd]
            if part == 0:
                S.op("act", lambda e: e.activation(out=T1[:, :], in_=pre[hd][:, :], func=AF.Exp, scale=-1.0), r=[pre[hd]], w=[T1])
                S.op("act", lambda e: e.activation(out=T1[:, :], in_=T1[:, :], func=AF.Ln, bias=1.0), r=[T1], w=[T1])
                S.op("act", lambda e: e.activation(out=T1[:, :], in_=T1[:, :], func=AF.Exp, scale=-1.0), r=[T1], w=[T1])
                S.op("dve", lambda e: e.tensor_scalar(out=FT[:, :], in0=T1[:, :], scalar1=oml[:, hd:hd + 1], scalar2=lb[:, hd:hd + 1], op0=ALU.mult, op1=ALU.add),
                     r=[T1, oml, lb], w=[FT])
                S.op("act", lambda e: e.activation(out=LF[:, :], in_=FT[:, :], func=AF.Ln), r=[FT], w=[LF])
                S.op("dve", lambda e: e.tensor_scalar(out=KT[:, :], in0=FT[:, :], scalar1=-1.0, scalar2=1.0, op0=ALU.mult, op1=ALU.add), r=[FT], w=[KT])
                S.op("act", lambda e: e.activation(out=T2[:, :], in_=pre[2 + hd][:, :], func=AF.Exp, scale=-1.0), r=[pre[2 + hd]], w=[T2])
                S.op("act", lambda e: e.activation(out=T2[:, :], in_=T2[:, :], func=AF.Ln, bias=1.0), r=[T2], w=[T2])
                S.op("act", lambda e: e.activation(out=T2[:, :], in_=T2[:, :], func=AF.Exp, scale=-1.0), r=[T2], w=[T2])
                S.op("pool", lambda e: e.tensor_tensor(out=QT[:, :], in0=pre[2 + hd][:, :], in1=T2[:, :], op=ALU.mult), r=[pre[2 + hd], T2], w=[QT])
                S.op("act", lambda e: e.activation(out=T3[:, :], in_=pre[6 + hd][:, :], func=AF.Exp, scale=-1.0), r=[pre[6 + hd]], w=[T3])
                S.op("act", lambda e: e.activation(out=T3[:, :], in_=T3[:, :], func=AF.Ln, bias=1.0), r=[T3], w=[T3])
                S.op("act", lambda e: e.activation(out=T3[:, :], in_=T3[:, :], func=AF.Exp, scale=-1.0), r=[T3], w=[T3])
                S.op("pool", lambda e: e.tensor_tensor(out=SZ[:, :], in0=pre[6 + hd][:, :], in1=T3[:, :], op=ALU.mult), r=[pre[6 + hd], T3], w=[SZ])
                for ch in range(NCH):
                    S.op("pe", lambda e: e.transpose(out=ps_m2[0:CS, 512 + ch * 128:512 + (ch + 1) * 128], in_=VT[:, ch * CS:(ch + 1) * CS], identity=ident[:, :]),
                         r=[VT, ident], w=[ps_m2])
                S.op("act", lambda e: e.copy(out=vtk[:, :], in_=ps_m2[0:CS, 512:512 + NCH * 128]), r=[ps_m2], w=[vtk])
                for ch in range(NCH):
                    S.op("dve", lambda e: e.tensor_tensor_scan(out=GG[:, ch * CS:(ch + 1) * CS], data0=ones[:, 0:CS], data1=LF[:, ch * CS:(ch + 1) * CS],
                                                               initial=0.0, op0=ALU.mult, op1=ALU.add), r=[ones, LF], w=[GG])
                S.op("act", lambda e: e.activation(out=EQ[:, :], in_=GG[:, :], func=AF.Exp), r=[GG], w=[EQ])
                S.op("act", lambda e: e.activation(out=EK[:, :], in_=GG[:, :], func=AF.Exp, scale=-1.0), r=[GG], w=[EK])
                S.op("act", lambda e: e.activation(out=DEC[:, :], in_=GG[:, :].rearrange("p (a b) -> p a b", a=NCH)[:, :, CS - 1], func=AF.Exp), r=[GG], w=[DEC])
                S.op("dve", lambda e: e.tensor_tensor(out=qtb[:, :], in0=QT[:, :], in1=EQ[:, :], op=ALU.mult), r=[QT, EQ], w=[qtb])
                S.op("dve", lambda e: e.tensor_tensor(out=kt_bf[hd][:, :], in0=KT[:, :], in1=EK[:, :], op=ALU.mult), r=[KT, EK], w=[kt_bf[hd]])
                S.op("dve", lambda e: e.tensor_tensor(out=kh_bf[hd][:, :].rearrange("p (a b) -> p a b", a=NCH), in0=kt_bf[hd][:, :].rearrange("p (a b) -> p a b", a=NCH),
                                                      in1=DEC[:, :].unsqueeze(2).to_broadcast([128, NCH, CS]), op=ALU.mult), r=[kt_bf[hd], DEC], w=[kh_bf[hd]])
                for ch in range(NCH):
                    S.op("pe", lambda e: e.matmul(ps_m1[0:CS, ma + ch * CS:ma + (ch + 1) * CS], lhsT=kt_bf[hd][:, ch * CS:(ch + 1) * CS], rhs=qtb[:, ch * CS:(ch + 1) * CS],
                                                  start=True, stop=True), r=[kt_bf[hd], qtb], w=[ps_m1])
                S.op("dve", lambda e: e.tensor_tensor(out=ATb[:, :], in0=ps_m1[0:CS, ma:ma + 128], in1=mask32[:, :], op=ALU.mult), r=[ps_m1, mask32], w=[ATb])
                for ch in range(NCH):
                    S.op("pe", lambda e: e.transpose(out=ps_m2[0:CS, ch * 128:(ch + 1) * 128], in_=kh_bf[hd][:, ch * CS:(ch + 1) * CS], identity=ident[:, :]),
                         r=[kh_bf[hd], ident], w=[ps_m2])
                S.op("act", lambda e: e.copy(out=khk[:, :], in_=ps_m2[0:CS, 0:NCH * 128]), r=[ps_m2], w=[khk])
                return
            for ch in range(NCH):
                S.op("pe", lambda e: e.matmul(ps_o[0:CS, ch * 128:(ch + 1) * 128], lhsT=ATb[:, ch * CS:(ch + 1) * CS], rhs=vtk[:, ch * 128:(ch + 1) * 128],
                                              start=True, stop=False), r=[ATb, vtk], w=[ps_o])
                S.op("pe", lambda e: e.matmul(ps_o[0:CS, ch * 128:(ch + 1) * 128], lhsT=qtb[:, ch * CS:(ch + 1) * CS], rhs=S_bf[hd][:, :],
                                              start=False, stop=True), r=[qtb, S_bf[hd]], w=[ps_o])
                S.op("pe", lambda e: e.matmul(ps_m1[:, ms:ms + 128], lhsT=khk[:, ch * 128:(ch + 1) * 128], rhs=vtk[:, ch * 128:(ch + 1) * 128],
                                              start=True, stop=True), r=[khk, vtk], w=[ps_m1])
                S.op("dve", lambda e: e.scalar_tensor_tensor(out=S32[hd][:, :], in0=S32[hd][:, :], scalar=DEC[:, ch:ch + 1], in1=ps_m1[:, ms:ms + 128],
                                                             op0=ALU.mult, op1=ALU.add), r=[S32[hd], DEC, ps_m1], w=[S32[hd]])
                S.op("act", lambda e: e.copy(out=S_bf[hd][:, :], in_=S32[hd][:, :]), r=[S32[hd]], w=[S_bf[hd]])
            S.op("act", lambda e: e.activation(out=osq[:, :], in_=ps_o[0:CS, 0:NCH * 128], func=AF.Square), r=[ps_o], w=[osq])
            S.op("dve", lambda e: e.tensor_reduce(out=oss[:, :], in_=osq[:, :].rearrange("p (a b) -> p a b", a=NCH), axis=AX.X, op=ALU.add), r=[osq], w=[oss])
            rms_rstd(S, oss, 128, CS, NCH, sstmp_h)
            S.op("dve", lambda e: e.tensor_tensor(out=on_bf[:, :].rearrange("p (a b) -> p a b", a=NCH), in0=ps_o[0:CS, 0:NCH * 128].rearrange("p (a b) -> p a b", a=NCH),
                                                  in1=oss[:, :].unsqueeze(2).to_broadcast([CS, NCH, 128]), op=ALU.mult), r=[ps_o, oss], w=[on_bf])
            for ch in range(NCH):
                S.op("pe", lambda e: e.transpose(out=ps_hT[:, 512 + hd * 128 + ch * CS:512 + hd * 128 + (ch + 1) * CS], in_=on_bf[:, ch * 128:(ch + 1) * 128], identity=ident[0:CS, 0:CS]),
                     r=[on_bf, ident], w=[ps_hT])
            yb = y_bf[(2 * j + hd) % 4]
            S.op("dve", lambda e: e.scalar_tensor_tensor(out=yb[:, :], in0=ps_hT[:, 512 + hd * 128:512 + (hd + 1) * 128], scalar=hn[:, hd:hd + 1], in1=SZ[:, :], op0=ALU.mult, op1=ALU.mult),
                 r=[ps_hT, hn, SZ], w=[yb])
            out_toks.append(S.op("pool", lambda e: e.dma_start(out=io["y1dst"](j, hd), in_=yb[:, :]), r=[yb], dma=yb))

    for i in range(nblk + 3):
        fns = []
        wts = []
        if 3 <= i:
            fns.append(lambda i=i: head(0, i - 3, 1))
            fns.append(lambda i=i: head(1, i - 3, 1))
            wts += [1, 1]
        if 2 <= i <= nblk + 1:
            fns.append(lambda i=i: head(0, i - 2, 0))
            fns.append(lambda i=i: head(1, i - 2, 0))
            wts += [1, 1]
        if 1 <= i <= nblk:
            fns.append(lambda i=i: common2(i - 1))
            wts += [BW2]
        if i < nblk:
            fns.append(lambda i=i: common(i))
            wts += [BW1]
        else:
            S.flag_set(("pa", i))
        S.parallel(fns, wts)
        if i >= 3 and "after_block" in io:
            io["after_block"](i - 3, out_toks)
    return out_toks


def build_D(S, io, nblk=NB):
    x1s, wout1, outs = io["x1s"], io["wout1"], io["outs"]
    out_toks = []
    stage = [S.sbuf(f"dwstage{i}", [128, 256], F32) for i in range(2)]
    w_bf = S.sbuf("wout1_bf", [128, 8, 256], BF16)
    load_cast_weight(S, wout1, w_bf, 256, stage)
    G = 4
    ng = nblk // G
    y_t = [S.sbuf(f"dy_t{i}", [128, 8, G * 128], BF16) for i in range(2)]
    x_t = [S.sbuf(f"dx_t{i}", [128, G, 256], F32) for i in range(2)]
    o_t = [S.sbuf(f"do_t{i}", [128, G, 256], F32) for i in range(2)]
    ps = [S.psum(f"dps{i}", [128, 512], F32) for i in range(4)]

    def load(g):
        sl = g % 2
        S.op("sp", lambda e: e.dma_start(out=y_t[sl][:, :, :], in_=io["y1src4"](g)), r=[io["y1buf"](g * G)], w=[y_t[sl]], dma=y_t[sl])
        S.op("sp", lambda e: e.dma_start(out=x_t[sl][:, :, :], in_=x1s[g * G * 128:(g + 1) * G * 128, :].rearrange("(b p) c -> p b c", p=128)), w=[x_t[sl]], dma=x_t[sl])

    load(0)
    for g in range(ng):
        sl = g % 2
        if g + 1 < ng:
            load(g + 1)
        for b in range(G):
            pt = ps[b]
            for c in range(8):
                S.op("pe", lambda e: e.matmul(pt[:, 0:256], lhsT=y_t[sl][:, c, b * 128:(b + 1) * 128], rhs=w_bf[:, c, :], start=(c == 0), stop=(c == 7)),
                     r=[y_t[sl], w_bf], w=[pt])
        for b in range(G):
            S.op("dve", lambda e: e.tensor_tensor(out=o_t[sl][:, b, :], in0=ps[b][:, 0:256], in1=x_t[sl][:, b, :], op=ALU.add), r=[ps[b], x_t[sl]], w=[o_t[sl]])
        out_toks.append(S.op("pool", lambda e: e.dma_start(out=outs[g * G * 128:(g + 1) * G * 128, :].rearrange("(b p) c -> p b c", p=128), in_=o_t[sl][:, :, :]), r=[o_t[sl]], dma=o_t[sl]))
    return out_toks


NCOL_A = 1416
WR = 8
BIGV = 1.0e30
NEGM = 30000.0


def sub(tile, name):
    return tile


def build_A(S, io, nblk=NB, do_m=True, do_n=True):
    nc = S.nc
    x = io["x"]
    out_toks = []

    def cload(name, shape, dtype, src):
        t = S.sbuf(name, shape, dtype)
        idx = tuple(slice(None) for _ in shape)
        S.op("sp", lambda e: e.dma_start(out=t[idx], in_=src[idx]), w=[t], dma=t)
        return t

    ident = cload("a_ident", [128, 128], BF16, io["ident"])
    identf = cload("a_identf", [128, 128], F32, io["identf"])
    normw = cload("a_normw", [128, 8], F32, io["normw0"])
    bfm = cload("a_bfm", [128, 2], F32, io["bfm"])
    btm = cload("a_btm", [128, 1160], F32, io["btm"])
    ones = S.sbuf("a_ones", [128, 128], F32)
    S.op("pool", lambda e: e.memset(ones[:, :], 1.0), w=[ones])
    stage = [S.sbuf(f"a_wstage{i}", [128, NCOL_A], F32) for i in range(2)]
    win_bf = S.sbuf("a_win_bf", [128, 8, NCOL_A], BF16)
    load_cast_weight(S, io["win0"], win_bf, NCOL_A, stage, scale_cols=normw)

    if do_m:
        convw = cload("m_convw", [128, 8], F32, io["convw"])
        convb = cload("m_convb", [128, 2], F32, io["convb"])
        fb = cload("m_fb", [1, 2], F32, io["fbias"])
        hnm = cload("m_hnm", [128, 128], F32, io["hnm"])
        mask128 = cload("m_mask128", [128, 128], F32, io["mask128"])
        nfb = S.sbuf("m_nfb", [1, 2], F32)
        S.op("dve", lambda e: e.tensor_scalar(out=nfb[:, :], in0=fb[:, :], scalar1=-1.0, scalar2=None, op0=ALU.mult), r=[fb], w=[nfb])
        qbuf_l = [S.sbuf(f"m_qbuf{i}", [128, 131], F32) for i in range(2)]
        kbuf_l = [S.sbuf(f"m_kbuf{i}", [128, 131], F32) for i in range(2)]
        for i in range(2):
            S.op("pool", lambda e: e.memset(qbuf_l[i][:, :], 0.0), w=[qbuf_l[i]])
            S.op("pool", lambda e: e.memset(kbuf_l[i][:, :], 0.0), w=[kbuf_l[i]])
        qacc = S.sbuf("m_qacc", [128, 128], F32)
        kacc = S.sbuf("m_kacc", [128, 128], F32)
        qsg = S.sbuf("m_qsg", [128, 128], F32)
        ksg = S.sbuf("m_ksg", [128, 128], F32)
        qT_bf = S.sbuf("m_qT_bf", [128, 128], BF16)
        kT_bf = S.sbuf("m_kT_bf", [128, 128], BF16)
        kt_tok = S.sbuf("m_kt_tok", [128, 128], BF16)
        v1 = S.sbuf("m_v1", [128, 129], BF16)
        S.op("pool", lambda e: e.memset(v1[:, :], 1.0), w=[v1])
        rows = [S.sbuf(f"m_rows{i}", [1, 1280], F32) for i in range(2)]
        for i in range(2):
            S.op("pool", lambda e: e.memset(rows[i][:, :], 0.0), w=[rows[i]])
        mcols = S.sbuf("m_cols", [128, 8], F32)
        ST_bf = S.sbuf("m_ST_bf", [128, 128], BF16)
        Cn32 = S.sbuf("m_Cn32", [128, 129], F32)
        Cn_bf = S.sbuf("m_Cn_bf", [128, 129], BF16)
        S.op("pool", lambda e: e.memset(Cn32[:, :], 0.0), w=[Cn32])
        S.op("pool", lambda e: e.memset(Cn_bf[:, :], 0.0), w=[Cn_bf])
        hs = S.sbuf("m_hs", [128, 128], F32)
        mss = S.sbuf("m_ss", [128, 2], F32)
        mtmp = S.sbuf("m_tmp", [128, 4], F32)
        eo = S.sbuf("m_eo", [128, 128], F32)
        ez = S.sbuf("m_ez", [128, 128], F32)
        gate = S.sbuf("m_gate", [128, 128], F32)
        ya_bf = S.sbuf("m_ya_bf", [128, 128], BF16)
        yaT = [S.sbuf(f"m_yaT{i}", [128, 128], BF16) for i in range(2)]

    if do_n:
        w6 = cload("n_w6", [128, 384], F32, io["w6"])
        kcw = cload("n_kcw", [8, 64], F32, io["kcw"])
        rcos = cload("n_rcos", [128, nblk, 8], F32, io["rcos"])
        rsin = cload("n_rsin", [128, nblk, 8], F32, io["rsin"])
        rnsin = S.sbuf("n_rnsin", [128, nblk, 8], F32)
        S.op("dve", lambda e: e.tensor_scalar(out=rnsin[:, :, :], in0=rsin[:, :, :], scalar1=-1.0, scalar2=None, op0=ALU.mult), r=[rsin], w=[rnsin])
        ccos = cload("n_ccos", [8, nblk, 8], F32, io["ccos"])
        csin = cload("n_csin", [8, nblk, 8], F32, io["csin"])
        bwide = cload("n_bwide", [128, 254], F32, io["bwide"])
        cmask = cload("n_cmask", [128, 16, 128], BF16, io["cmask"])
        causneg2 = cload("n_causneg2", [128, 256], BF16, io["causneg2"])
        anti2 = cload("n_anti2", [128, 256], BF16, io["anti2"])
        w2kv = S.sbuf("n_w2kv", [128, 128], BF16)
        posT = S.sbuf("n_posT", [128, 32], BF16)
        w1kv = [S.sbuf(f"n_w1kv{i}", [128, 16, 128], BF16) for i in range(2)]
        cnt = 0
        for i, nm in enumerate(("w1k", "w1v")):
            wsrc = io[nm].rearrange("(a l d) h -> a d l h", a=2, d=64)
            for m0 in range(0, 16, 8):
                st = stage[cnt % 2]
                cnt += 1
                stv = st[:, 0:1024].rearrange("p (a b) -> p a b", a=8)
                for a in range(2):
                    S.op("sp", lambda e: e.dma_start(out=stv[64 * a:64 * a + 64], in_=wsrc[a, :, m0:m0 + 8, :]), w=[st], dma=st)
                S.op("dve", lambda e: e.tensor_copy(out=w1kv[i][:, m0:m0 + 8, :], in_=stv), r=[st], w=[w1kv[i]])
        w2st = cload("n_w2st", [128, 128], F32, io["w2kv"])
        S.op("dve", lambda e: e.tensor_copy(out=w2kv[:, :], in_=w2st[:, :]), r=[w2st], w=[w2kv])
        posst = cload("n_posst", [128, 32], F32, io["posT"])
        S.op("dve", lambda e: e.tensor_copy(out=posT[:, :], in_=posst[:, :]), r=[posst], w=[posT])
        cbias = S.sbuf("n_cbias", [128, 2], F32)
        KST = S.sbuf("n_KST", [128, nblk * 128], BF16)
        S.op("sp", lambda e: e.dma_start(out=KST[64:128, :], in_=io["kind"][:, :]), w=[KST], dma=KST)
        KWT = S.sbuf("n_KWT", [64, WR * 128], BF16)
        KSTb = [Buf(f"kst{j}") for j in range(nblk)]
        KWTr = [Buf(f"kwt{j}") for j in range(WR)]
        KWTb = [KWTr[j % WR] for j in range(nblk)]
        VS1 = S.sbuf("n_VS1", [128, nblk, 65], BF16)
        VW1 = S.sbuf("n_VW1", [128, WR, 65], BF16)
        VS1b = [Buf(f"vs{j}") for j in range(nblk)]
        VW1r = [Buf(f"vw{j}") for j in range(WR)]
        VW1b = [VW1r[j % WR] for j in range(nblk)]
        S.op("pool", lambda e: e.memset(VS1[:, :, :], 1.0), w=[VS1] + VS1b)
        S.op("pool", lambda e: e.memset(VW1[:, :, :], 1.0), w=[VW1] + VW1r)
        KCT = S.sbuf("n_KCT", [64, 512], BF16)
        VCT = S.sbuf("n_VCT", [64, 512], BF16)
        S.op("pool", lambda e: e.memset(KCT[:, :], 0.0), w=[KCT])
        S.op("pool", lambda e: e.memset(VCT[:, :], 0.0), w=[VCT])
        VC1 = S.sbuf("n_VC1", [128, 4, 193], BF16)
        S.op("pool", lambda e: e.memset(VC1[:, :, :], 1.0), w=[VC1])
        S.op("pool", lambda e: e.memset(VC1[0:1, 0, 64:65], 0.0), w=[VC1])
        ovst = cload("n_ovst", [128, 4, 128], BF16, io["ov"])
        S.op("pool", lambda e: e.tensor_copy(out=VC1[:, :, 65:193], in_=ovst[:, :, :]), r=[ovst], w=[VC1])
        kcv = S.sbuf("n_kcv", [128, 288], BF16)
        S.op("pool", lambda e: e.memset(kcv[:, :], 0.0), w=[kcv])
        qk6 = S.sbuf("n_qk6", [128, 384], F32)
        nsq = S.sbuf("n_sq", [128, 384], F32)
        nss = S.sbuf("n_ss", [128, 8], F32)
        nstmp = S.sbuf("n_stmp", [128, 8], F32)
        rt = S.sbuf("n_rt", [128, 2, 96], F32)
        qk6_bf = S.sbuf("n_qk6_bf", [128, 384], BF16)
        qT_l = [S.sbuf(f"n_qT{i}", [64, 512], BF16) for i in range(2)]
        kcv_bf = S.sbuf("n_kcv_bf", [128, 256], BF16)
        cu = S.sbuf("n_cu", [128, 16], F32)
        cw = S.sbuf("n_cw", [128, 16], F32)
        cg = S.sbuf("n_cg", [128, 16], BF16)
        c2 = S.sbuf("n_c2", [8, 128], F32)
        c2b = S.sbuf("n_c2b", [8, 128], BF16)
        crt = S.sbuf("n_crt", [8, 8, 4], F32)
        Pc = [S.sbuf(f"n_Pc{i}", [128, 512], BF16) for i in range(4)]
        ocs = S.sbuf("n_ocs", [128, 4], F32)
        imp = S.sbuf("n_imp", [128, 128], F32)
        impw = S.sbuf("n_impw", [128, 128], F32)
        m8 = S.sbuf("n_m8", [128, 16], F32)
        negm = S.sbuf("n_negm", [128, 256], BF16)
        S.op("pool", lambda e: e.memset(negm[:, :], 0.0), w=[negm])
        qA_l = [[S.sbuf(f"n_qA{p}_{i}", [128, 256], BF16) for i in range(2)] for p in range(2)]
        Pa = [S.sbuf(f"n_Pa{i}", [128, 512], BF16) for i in range(3)]
        oT = S.sbuf("n_oT", [65, 512], F32)
        OB_l = [S.sbuf(f"n_OB{i}", [128, 6, 65], F32) for i in range(2)]
        c2n = S.sbuf("n_c2n", [8, 64], F32)
        css = S.sbuf("n_css", [8, 2], F32)
        gts_l = [S.sbuf(f"n_gts{i}", [128, 6], F32) for i in range(2)]
        coef = S.sbuf("n_coef", [128, 6], F32)
        sums = S.sbuf("n_sums", [128, 6], F32)
        ez2_l = [S.sbuf(f"n_ez2{i}", [128, 128], F32) for i in range(2)]
        yacc = S.sbuf("n_yacc", [128, 128], F32)
        yb_bf = S.sbuf("n_yb_bf", [128, 128], BF16)
        ybT = [S.sbuf(f"n_ybT{i}", [128, 128], BF16) for i in range(2)]

    x_t = [S.sbuf(f"a_x_t{i}", [128, 1024], F32) for i in range(2)]
    junk = S.sbuf("a_junk", [128, 1024], BF16)
    ssx = S.sbuf("a_ssx", [128, 1], F32)
    sstmp = S.sbuf("a_sstmp", [128, 4], F32)
    h_bf = S.sbuf("a_h_bf", [128, 1024], BF16)
    hT = S.sbuf("a_hT", [128, 1024], BF16)
    tma_l = [S.sbuf(f"a_tma{i}", [128, 512], F32) for i in range(2)]
    tmb = S.sbuf("a_tmb", [128, 512], F32)
    tmc_l = [S.sbuf(f"a_tmc{i}", [128, 136], F32) for i in range(2)]

    ps_hT = S.psum("a_ps_hT", [128, 1024], BF16)
    ps_bt = S.psum("a_ps_bt", [128, 1024], BF16)
    ps_x = S.psum("a_ps_x", [128, 512], F32)
    ps_y = S.psum("a_ps_y", [128, 512], F32)
    ps_f = S.psum("a_ps_f", [128, 512], F32)
    ps_s = [S.psum(f"a_ps_s{i}", [128, 512], F32) for i in range(2)]
    ps_o = S.psum("a_ps_o", [128, 512], F32)
    bt_q = sub(ps_bt, "bt_q")
    bt_kk = sub(ps_bt, "bt_kk")
    bt_kcv = sub(ps_bt, "bt_kcv")
    bt_m = sub(ps_bt, "bt_m")
    f_st = sub(ps_f, "f_st")
    f_num = sub(ps_f, "f_num")
    f_cn = sub(ps_f, "f_cn")
    f_row = sub(ps_f, "f_row")
    f_misc = sub(ps_f, "f_misc")

    if do_n:
        for kv in range(2):
            for m in range(16):
                S.op("pe", lambda e: e.matmul(ps_s[0][:, kv:kv + 1], lhsT=w1kv[kv][:, m, :], rhs=posT[:, 16 * kv + m:16 * kv + m + 1],
                                              start=(m == 0), stop=(m == 15)), r=[w1kv[kv], posT], w=[ps_s[0]])
        S.op("dve", lambda e: e.tensor_copy(out=cbias[:, :], in_=ps_s[0][:, 0:2]), r=[ps_s[0]], w=[cbias])

    def load(j):
        sl = j % 2
        S.op("sp", lambda e: e.dma_start(out=x_t[sl][:, :], in_=x[j * 128:(j + 1) * 128, :]), w=[x_t[sl]], dma=x_t[sl])

    def sigm_inplace(t, shape_ap):
        S.op("dve", lambda e: e.tensor_scalar(out=shape_ap(t), in0=shape_ap(t), scalar1=1.0, scalar2=None, op0=ALU.add), r=[t], w=[t])
        S.op("dve", lambda e: e.reciprocal(out=shape_ap(t), in_=shape_ap(t)), r=[t], w=[t])

    load(0)

    def common(j):
        sl = j % 2
        if j + 1 < nblk:
            load(j + 1)
        xt = x_t[sl]
        tma, tmc = tma_l[sl], tmc_l[sl]
        if do_m:
            qbuf, kbuf = qbuf_l[sl], kbuf_l[sl]
        S.op("act", lambda e: e.activation(out=junk[:, :], in_=xt[:, :], func=AF.Square, accum_out=ssx[:, 0:1]), r=[xt], w=[junk, ssx])
        rms_rstd(S, ssx, 1024, 128, 1, sstmp)
        S.op("act", lambda e: e.activation(out=h_bf[:, :], in_=xt[:, :], func=AF.Copy, scale=ssx[:, 0:1]), r=[xt, ssx], w=[h_bf])
        for hf in range(2):
            for c in range(4):
                cc = hf * 4 + c
                S.op("pe", lambda e: e.transpose(out=ps_hT[:, c * 128:(c + 1) * 128], in_=h_bf[:, cc * 128:(cc + 1) * 128], identity=ident[:, :]),
                     r=[h_bf, ident], w=[ps_hT])
            S.op("dve", lambda e: e.tensor_copy(out=hT[:, hf * 512:(hf + 1) * 512], in_=ps_hT[:, 0:512]), r=[ps_hT], w=[hT])

    def common_proj(j):
        sl = j % 2
        tma, tmc = tma_l[sl], tmc_l[sl]
        if do_m:
            qbuf, kbuf = qbuf_l[sl], kbuf_l[sl]
        for o in range(2):
            for c in range(8):
                S.op("pe", lambda e: e.matmul(ps_x[:, o * 128:(o + 1) * 128], lhsT=win_bf[:, c, o * 128:(o + 1) * 128], rhs=hT[:, c * 128:(c + 1) * 128],
                                              start=(c == 0), stop=(c == 7)), r=[win_bf, hT], w=[ps_x])
        for c in range(8):
            S.op("pe", lambda e: e.matmul(ps_x[:, 256:392], lhsT=hT[:, c * 128:(c + 1) * 128], rhs=win_bf[:, c, 1280:1416],
                                          start=(c == 0), stop=(c == 7)), r=[win_bf, hT], w=[ps_x])
        for c in range(8):
            S.op("pe", lambda e: e.matmul(ps_y[:, :], lhsT=hT[:, c * 128:(c + 1) * 128], rhs=win_bf[:, c, 256:768],
                                          start=(c == 0), stop=(c == 7)), r=[win_bf, hT], w=[ps_y])
        if do_m:
            S.op("act", lambda e: e.activation(out=qbuf[:, 3:131], in_=ps_x[:, 0:128], func=AF.Identity, bias=bfm[:, 0:1]), r=[ps_x, bfm], w=[qbuf])
            S.op("act", lambda e: e.activation(out=kbuf[:, 3:131], in_=ps_x[:, 128:256], func=AF.Identity, bias=bfm[:, 1:2]), r=[ps_x, bfm], w=[kbuf])
        S.op("dve", lambda e: e.tensor_tensor(out=tmc[:, :], in0=ps_x[:, 256:392], in1=btm[:, 1024:1160], op=ALU.add), r=[ps_x, btm], w=[tmc])
        S.op("dve", lambda e: e.tensor_tensor(out=tma[:, :], in0=ps_y[:, :], in1=btm[:, 0:512], op=ALU.add), r=[ps_y, btm], w=[tma])
        if do_n:
            for c in range(8):
                S.op("pe", lambda e: e.matmul(ps_x[:, :], lhsT=hT[:, c * 128:(c + 1) * 128], rhs=win_bf[:, c, 768:1280],
                                              start=(c == 0), stop=(c == 7)), r=[win_bf, hT], w=[ps_x])
            S.op("dve", lambda e: e.tensor_tensor(out=tmb[:, :], in0=ps_x[:, :], in1=btm[:, 512:1024], op=ALU.add), r=[ps_x, btm], w=[tmb])

    def mlstm_block(j):
        if True:
            tma, tmc = tma_l[j % 2], tmc_l[j % 2]
            qbuf, kbuf = qbuf_l[j % 2], kbuf_l[j % 2]
            qbuf_n, kbuf_n = qbuf_l[(j + 1) % 2], kbuf_l[(j + 1) % 2]
            R = rows[j % 2]
            Rp = rows[(j + 1) % 2]
            for (eng, buf, bufn, acc, sg, wofs, bcol_, dst, post) in (("dve", qbuf, qbuf_n, qacc, qsg, 0, 0, qT_bf, 1.0), ("dve", kbuf, kbuf_n, kacc, ksg, 4, 1, kT_bf, 128 ** -0.5)):
                S.op(eng, lambda e: e.tensor_scalar(out=acc[:, :], in0=buf[:, 0:128], scalar1=convw[:, wofs:wofs + 1], scalar2=convb[:, bcol_:bcol_ + 1],
                                                    op0=ALU.mult, op1=ALU.add), r=[buf, convw, convb], w=[acc])
                for i in range(1, 4):
                    S.op(eng, lambda e: e.scalar_tensor_tensor(out=acc[:, :], in0=buf[:, i:i + 128], scalar=convw[:, wofs + i:wofs + i + 1], in1=acc[:, :],
                                                               op0=ALU.mult, op1=ALU.add), r=[buf, convw, acc], w=[acc])
                S.op(eng, lambda e: e.tensor_copy(out=bufn[:, 0:3], in_=buf[:, 128:131]), r=[buf], w=[bufn])
                S.op("act", lambda e: e.activation(out=sg[:, :], in_=acc[:, :], func=AF.Exp, scale=-1.0), r=[acc], w=[sg])
                S.op("act", lambda e: e.activation(out=sg[:, :], in_=sg[:, :], func=AF.Ln, bias=1.0), r=[sg], w=[sg])
                S.op("act", lambda e: e.activation(out=sg[:, :], in_=sg[:, :], func=AF.Exp, scale=-1.0), r=[sg], w=[sg])
                S.op(eng, lambda e: e.scalar_tensor_tensor(out=dst[:, :], in0=acc[:, :], scalar=post, in1=sg[:, :], op0=ALU.mult, op1=ALU.mult),
                     r=[acc, sg], w=[dst])
            S.op("act", lambda e: e.copy(out=v1[:, 0:128], in_=tma[:, 0:128]), r=[tma], w=[v1])
            S.op("pe", lambda e: e.transpose(out=ps_f[0:1, 0:128], in_=tmc[:, 134:135], identity=identf[:, :]), r=[tmc, identf], w=[ps_f])
            S.op("pe", lambda e: e.transpose(out=ps_f[0:1, 128:256], in_=tmc[:, 135:136], identity=identf[:, :]), r=[tmc, identf], w=[ps_f])
            S.op("act", lambda e: e.copy(out=R[:, 1024:1280], in_=ps_f[0:1, 0:256]), r=[ps_f], w=[R])
            S.op("act", lambda e: e.activation(out=R[:, 896:1024], in_=R[:, 1152:1280], func=AF.Exp, scale=-1.0, bias=nfb[:, 0:1]), r=[R, nfb], w=[R])
            S.op("act", lambda e: e.activation(out=R[:, 0:128], in_=R[:, 896:1024], func=AF.Ln, bias=1.0), r=[R], w=[R])
            S.op("dve", lambda e: e.tensor_tensor_scan(out=R[:, 128:256], data0=ones[0:1, 0:128], data1=R[:, 0:128], initial=Rp[:, 255:256],
                                                       op0=ALU.mult, op1=ALU.add), r=[ones, R, Rp], w=[R])
            S.op("dve", lambda e: e.tensor_tensor(out=R[:, 256:384], in0=R[:, 1024:1152], in1=R[:, 128:256], op=ALU.add), r=[R], w=[R])
            S.op("dve", lambda e: e.tensor_tensor_scan(out=R[:, 384:512], data0=ones[0:1, 0:128], data1=R[:, 256:384], initial=Rp[:, 511:512],
                                                       op0=ALU.mult, op1=ALU.max), r=[ones, R, Rp], w=[R])
            S.op("dve", lambda e: e.tensor_scalar(out=R[:, 896:897], in0=Rp[:, 511:512], scalar1=-1.0, scalar2=None, op0=ALU.mult), r=[Rp], w=[R])
            S.op("act", lambda e: e.activation(out=R[:, 512:640], in_=R[:, 256:384], func=AF.Exp, bias=R[:, 896:897]), r=[R], w=[R])
            S.op("act", lambda e: e.activation(out=R[:, 640:768], in_=R[:, 384:512], func=AF.Exp, scale=-1.0, bias=Rp[:, 511:512]), r=[R, Rp], w=[R])
            S.op("dve", lambda e: e.tensor_tensor(out=R[:, 768:896], in0=R[:, 128:256], in1=R[:, 384:512], op=ALU.subtract), r=[R], w=[R])
            S.op("act", lambda e: e.activation(out=R[:, 768:896], in_=R[:, 768:896], func=AF.Exp), r=[R], w=[R])
            for ci, c0 in enumerate((512, 640, 768)):
                S.op("pe", lambda e: e.matmul(ps_f[:, 386 + ci:387 + ci], lhsT=R[:, c0:c0 + 128], rhs=ones[0:1, 0:1], start=True, stop=True),
                     r=[R, ones], w=[f_misc])
            S.op("pe", lambda e: e.matmul(ps_f[:, 389:390], lhsT=ones[0:1, 0:128], rhs=R[:, 767:768], start=True, stop=True), r=[R, ones], w=[f_misc])
            S.op("dve", lambda e: e.tensor_copy(out=mcols[:, 0:4], in_=ps_f[:, 386:390]), r=[f_misc], w=[mcols])
            S.op("dve", lambda e: e.tensor_tensor(out=mcols[:, 4:5], in0=mcols[:, 0:1], in1=mcols[:, 3:4], op=ALU.mult), r=[mcols], w=[mcols])
            S.op("pe", lambda e: e.matmul(ps_f[:, 0:128], lhsT=kT_bf[:, :], rhs=qT_bf[:, :], start=True, stop=True), r=[kT_bf, qT_bf], w=[f_st])
            S.op("dve", lambda e: e.scalar_tensor_tensor(out=ST_bf[:, :], in0=ps_f[:, 0:128], scalar=mcols[:, 0:1], in1=mask128[:, :], op0=ALU.mult, op1=ALU.mult),
                 r=[f_st, mcols, mask128], w=[ST_bf])
            S.op("pe", lambda e: e.matmul(ps_f[:, 128:257], lhsT=ST_bf[:, :], rhs=v1[:, :], start=True, stop=False), r=[ST_bf, v1], w=[f_num])
            S.op("pe", lambda e: e.matmul(ps_f[:, 128:257], lhsT=qT_bf[:, :], rhs=Cn_bf[:, :], start=False, stop=True), r=[qT_bf, Cn_bf], w=[f_num])
            S.op("pe", lambda e: e.transpose(out=ps_hT[:, 768:896], in_=kT_bf[:, :], identity=ident[:, :]), r=[kT_bf, ident], w=[ps_hT])
            S.op("act", lambda e: e.activation(out=kt_tok[:, :], in_=ps_hT[:, 768:896], func=AF.Copy, scale=mcols[:, 4:5]), r=[ps_hT, mcols], w=[kt_tok])
            S.op("pe", lambda e: e.matmul(ps_f[:, 257:386], lhsT=kt_tok[:, :], rhs=v1[:, :], start=True, stop=True), r=[kt_tok, v1], w=[f_cn])
            S.op("dve", lambda e: e.scalar_tensor_tensor(out=Cn32[:, :], in0=Cn32[:, :], scalar=mcols[:, 3:4], in1=ps_f[:, 257:386], op0=ALU.mult, op1=ALU.add),
                 r=[Cn32, mcols, f_cn], w=[Cn32])
            S.op("act", lambda e: e.copy(out=Cn_bf[:, :], in_=Cn32[:, :]), r=[Cn32], w=[Cn_bf])
            S.op("dve", lambda e: e.tensor_tensor(out=mcols[:, 5:6], in0=ps_f[:, 256:257], in1=mcols[:, 1:2], op=ALU.mult), r=[f_num, mcols], w=[mcols])
            S.op("dve", lambda e: e.tensor_scalar(out=mcols[:, 7:8], in0=mcols[:, 5:6], scalar1=-1.0, scalar2=None, op0=ALU.mult), r=[mcols], w=[mcols])
            S.op("dve", lambda e: e.tensor_tensor(out=mcols[:, 5:6], in0=mcols[:, 5:6], in1=mcols[:, 7:8], op=ALU.max), r=[mcols], w=[mcols])
            S.op("dve", lambda e: e.tensor_tensor(out=mcols[:, 5:6], in0=mcols[:, 5:6], in1=mcols[:, 2:3], op=ALU.max), r=[mcols], w=[mcols])
            S.op("dve", lambda e: e.reciprocal(out=mcols[:, 5:6], in_=mcols[:, 5:6]), r=[mcols], w=[mcols])
            S.op("dve", lambda e: e.tensor_tensor(out=mcols[:, 6:7], in0=mcols[:, 5:6], in1=mcols[:, 1:2], op=ALU.mult), r=[mcols], w=[mcols])
            S.op("act", lambda e: e.activation(out=hs[:, :], in_=ps_f[:, 128:256], func=AF.Copy, scale=mcols[:, 6:7]), r=[f_num, mcols], w=[hs])
            S.op("act", lambda e: e.activation(out=junk[:, 0:128], in_=hs[:, :], func=AF.Square, accum_out=mss[:, 0:1]), r=[hs], w=[junk, mss])
            rms_rstd(S, mss, 128, 128, 1, mtmp)
            S.op("act", lambda e: e.activation(out=eo[:, :], in_=tma[:, 128:256], func=AF.Exp, scale=-1.0), r=[tma], w=[eo])
            S.op("act", lambda e: e.activation(out=ez[:, :], in_=tma[:, 256:384], func=AF.Exp, scale=-1.0), r=[tma], w=[ez])
            S.op("act", lambda e: e.activation(out=eo[:, :], in_=eo[:, :], func=AF.Ln, bias=1.0), r=[eo], w=[eo])
            S.op("act", lambda e: e.activation(out=ez[:, :], in_=ez[:, :], func=AF.Ln, bias=1.0), r=[ez], w=[ez])
            S.op("pool", lambda e: e.tensor_tensor(out=ez[:, :], in0=ez[:, :], in1=eo[:, :], op=ALU.add), r=[ez, eo], w=[ez])
            S.op("act", lambda e: e.activation(out=ez[:, :], in_=ez[:, :], func=AF.Exp, scale=-1.0), r=[ez], w=[ez])
            S.op("pool", lambda e: e.tensor_tensor(out=gate[:, :], in0=tma[:, 256:384], in1=hnm[:, :], op=ALU.mult), r=[tma, hnm], w=[gate])
            S.op("pool", lambda e: e.tensor_tensor(out=gate[:, :], in0=gate[:, :], in1=ez[:, :], op=ALU.mult), r=[gate, ez], w=[gate])
            S.op("dve", lambda e: e.scalar_tensor_tensor(out=ya_bf[:, :], in0=hs[:, :], scalar=mss[:, 0:1], in1=gate[:, :], op0=ALU.mult, op1=ALU.mult),
                 r=[hs, mss, gate], w=[ya_bf])
            S.op("pe", lambda e: e.transpose(out=ps_hT[:, 768:896], in_=ya_bf[:, :], identity=ident[:, :]), r=[ya_bf, ident], w=[ps_hT])
            yT = yaT[j % 2]
            S.op("act", lambda e: e.copy(out=yT[:, :], in_=ps_hT[:, 768:896]), r=[ps_hT], w=[yT])
            out_toks.append(S.op("pool", lambda e: e.dma_start(out=io["y0dst"](j, 0), in_=yT[:, :]), r=[yT], dma=yT))

    L = dict(locals())

    def early(i):
        common(i)
        S.flag_wait(("psfree", i))
        common_proj(i)
        S.flag_set(("c", i))
        if do_n:
            nsa_qk(S, i, L)

    def cmp_chain(i):
        S.flag_wait(("c", i))
        nsa_cmp(S, i, L)

    def topk_chain(i):
        nsa_topk(S, i - 1, L, lambda: S.flag_set(("psfree", i)))
        S.flag_set(("psfree", i))

    lag = 2 if do_n else 1
    for i in range(nblk + lag):
        if "bc_preload_step" in io:
            io["bc_preload_step"](i, stage)
        fns = []
        cnts = []
        if do_n and i >= 2:
            fns.append(lambda i=i: nsa_attn(S, i - 2, L))
            cnts.append(60 + 2.5 * i)
        if 1 <= i <= nblk:
            if do_m:
                fns.append(lambda i=i: mlstm_block(i - 1))
                cnts.append(90)
            if do_n:
                fns.append(lambda i=i: topk_chain(i))
                cnts.append(60)
        if not (do_n and 1 <= i <= nblk):
            S.flag_set(("psfree", i))
        if i < nblk:
            fns.append(lambda i=i: early(i))
            cnts.append(100 if do_n else 60)
            if do_n:
                fns.append(lambda i=i: cmp_chain(i))
                cnts.append(65)
        mn = min(cnts)
        wts = [max(1, int(round(AW * c / mn))) for c in cnts] if AW else None
        S.parallel(fns, wts)
        if "after_block" in io and i >= lag:
            io["after_block"](i - lag, out_toks)
    return out_toks


NSA_STOP = 0
NSA_DBG = 0


class _NS:
    def __init__(self, d):
        self.__dict__.update(d)


def nsa_qk(S, j, L):
    V = _NS(L)
    io, nblk = V.io, V.nblk
    tma, tmb, tmc, ident, identf, ones = V.tma_l[j % 2], V.tmb, V.tmc_l[j % 2], V.ident, V.identf, V.ones
    ps_bt, ps_s, ps_x, ps_y, ps_f, ps_o = V.ps_bt, V.ps_s, V.ps_x, V.ps_y, V.ps_f, V.ps_o
    qk6, nsq, nss, nstmp, rt, qk6_bf, w6 = V.qk6, V.nsq, V.nss, V.nstmp, V.rt, V.qk6_bf, V.w6
    qT, qA, OB, gts, ez2 = V.qT_l[j % 2], V.qA_l[j % 2], V.OB_l[j % 2], V.gts_l[j % 2], V.ez2_l[j % 2]
    T0 = j * 128

    def v3(ap, a):
        return ap.rearrange("p (a b) -> p a b", a=a)

    S.op("act", lambda e: e.activation(out=nsq[:, :], in_=tmb[:, 0:384], func=AF.Square), r=[tmb], w=[nsq])
    S.op("dve", lambda e: e.tensor_reduce(out=nss[:, 0:6], in_=v3(nsq[:, :], 6), axis=AX.X, op=ALU.add), r=[nsq], w=[nss])
    S.op("act", lambda e: e.activation(out=nstmp[:, 0:6], in_=nss[:, 0:6], func=AF.Ln, scale=1.0 / 64, bias=EPS), r=[nss], w=[nstmp])
    S.op("act", lambda e: e.activation(out=nss[:, 0:4], in_=nstmp[:, 0:4], func=AF.Exp, scale=-0.5, bias=float(np.log(0.125))), r=[nstmp], w=[nss])
    S.op("act", lambda e: e.activation(out=nss[:, 4:6], in_=nstmp[:, 4:6], func=AF.Exp, scale=-0.5), r=[nstmp], w=[nss])
    S.op("dve", lambda e: e.tensor_tensor(out=v3(qk6[:, :], 6), in0=v3(tmb[:, 0:384], 6), in1=nss[:, 0:6].unsqueeze(2).to_broadcast([128, 6, 64]), op=ALU.mult),
         r=[tmb, nss], w=[qk6])
    S.op("dve", lambda e: e.tensor_tensor(out=qk6[:, :], in0=qk6[:, :], in1=w6[:, :], op=ALU.mult), r=[qk6, w6], w=[qk6])
    S.op("act", lambda e: e.copy(out=qk6_bf[:, :], in_=qk6[:, :]), r=[qk6], w=[qk6_bf])
    x1 = v3(qk6[:, :], 6)[:, :, 0:8]
    x2 = v3(qk6[:, :], 6)[:, :, 8:16]
    cosb = V.rcos[:, j, :].unsqueeze(1).to_broadcast([128, 6, 8])
    sinb = V.rsin[:, j, :].unsqueeze(1).to_broadcast([128, 6, 8])
    r3 = lambda k: rt[:, k, :].rearrange("p (a b) -> p a b", a=6)
    x12 = v3(qk6[:, :], 6)[:, :, 0:16].rearrange("p a (t b) -> p a t b", t=2)
    cos2 = V.rcos[:, j, :].unsqueeze(1).unsqueeze(1).to_broadcast([128, 6, 2, 8])
    ra = rt[:, 0, :].rearrange("p (a t b) -> p a t b", a=6, t=2)
    rb = rt[:, 1, :].rearrange("p (a t b) -> p a t b", a=6, t=2)
    S.op("dve", lambda e: e.tensor_tensor(out=ra, in0=x12, in1=cos2, op=ALU.mult), r=[qk6, V.rcos], w=[rt])
    S.op("pool", lambda e: e.tensor_tensor(out=rb[:, :, 0, :], in0=x2, in1=V.rnsin[:, j, :].unsqueeze(1).to_broadcast([128, 6, 8]), op=ALU.mult), r=[qk6, V.rnsin], w=[rt])
    S.op("pool", lambda e: e.tensor_tensor(out=rb[:, :, 1, :], in0=x1, in1=sinb, op=ALU.mult), r=[qk6, V.rsin], w=[rt])
    S.op("dve", lambda e: e.tensor_tensor(out=v3(qk6_bf[:, :], 6)[:, :, 0:16], in0=rt[:, 0, :].rearrange("p (a c) -> p a c", a=6)[:, :, 0:16],
                                          in1=rt[:, 1, :].rearrange("p (a c) -> p a c", a=6)[:, :, 0:16], op=ALU.add), r=[rt], w=[qk6_bf])
    for h in range(4):
        S.op("pe", lambda e: e.transpose(out=ps_bt[0:64, h * 128:(h + 1) * 128], in_=qk6_bf[:, h * 64:(h + 1) * 64], identity=ident[:, :]),
             r=[qk6_bf, ident], w=[V.bt_q])
    S.op("act", lambda e: e.copy(out=qT[:, :], in_=ps_bt[0:64, 0:512]), r=[V.bt_q], w=[qT])
    S.op("pe", lambda e: e.transpose(out=ps_bt[0:64, 512:640], in_=qk6_bf[:, 256:320], identity=ident[:, :]), r=[qk6_bf, ident], w=[V.bt_kk])
    S.op("pe", lambda e: e.transpose(out=ps_bt[0:64, 640:768], in_=qk6_bf[:, 320:384], identity=ident[:, :]), r=[qk6_bf, ident], w=[V.bt_kk])
    S.op("dve", lambda e: e.tensor_copy(out=V.KST[0:64, T0:T0 + 128], in_=ps_bt[0:64, 512:640]), r=[V.bt_kk], w=[V.KSTb[j]])
    S.op("dve", lambda e: e.tensor_copy(out=V.KWT[:, (j % WR) * 128:(j % WR + 1) * 128], in_=ps_bt[0:64, 640:768]), r=[V.bt_kk], w=[V.KWTb[j]])
    S.op("pool", lambda e: e.tensor_copy(out=V.VS1[:, j, 0:64], in_=tmc[:, 0:64]), r=[tmc], w=[V.VS1b[j]])
    S.op("pool", lambda e: e.tensor_copy(out=V.VW1[:, j % WR, 0:64], in_=tmc[:, 64:128]), r=[tmc], w=[V.VW1b[j]])


def nsa_cmp(S, j, L):
    V = _NS(L)
    io, nblk = V.io, V.nblk
    tma, tmb, tmc, ident, identf, ones = V.tma_l[j % 2], V.tmb, V.tmc_l[j % 2], V.ident, V.identf, V.ones
    ps_bt, ps_s, ps_x, ps_y, ps_f, ps_o = V.ps_bt, V.ps_s, V.ps_x, V.ps_y, V.ps_f, V.ps_o
    qk6, nsq, nss, nstmp, rt, qk6_bf, w6 = V.qk6, V.nsq, V.nss, V.nstmp, V.rt, V.qk6_bf, V.w6
    qT, qA, OB, gts, ez2 = V.qT_l[j % 2], V.qA_l[j % 2], V.OB_l[j % 2], V.gts_l[j % 2], V.ez2_l[j % 2]
    T0 = j * 128

    def v3(ap, a):
        return ap.rearrange("p (a b) -> p a b", a=a)

    kcv, kcv_bf = V.kcv, V.kcv_bf
    for kv in range(2):
        for dup in range(2):
            S.op("act" if dup == 0 else "pool", lambda e: (e.copy if dup == 0 else e.tensor_copy)(out=kcv_bf[:, 128 * kv + 64 * dup:128 * kv + 64 * dup + 64],
                                                                                            in_=tmb[:, 384 + 64 * kv:448 + 64 * kv]), r=[tmb], w=[kcv_bf])
    for kv in range(2):
        S.op("pe", lambda e: e.transpose(out=ps_bt[:, 768 + 128 * kv:896 + 128 * kv], in_=kcv_bf[:, 128 * kv:128 * kv + 128], identity=ident[:, :]),
             r=[kcv_bf, ident], w=[V.bt_kcv])
    for kv in range(2):
        b0 = 144 * kv
        S.op("pool", lambda e: e.tensor_copy(out=kcv[0:64, b0:b0 + 16], in_=kcv[0:64, b0 + 128:b0 + 144]), r=[kcv], w=[kcv])
    for kv in range(2):
        b0 = 144 * kv
        S.op("dve", lambda e: e.tensor_copy(out=kcv[0:64, b0 + 16:b0 + 144], in_=ps_bt[0:64, 768 + 128 * kv:896 + 128 * kv]), r=[V.bt_kcv], w=[kcv])
        S.op("act", lambda e: e.copy(out=kcv[64:128, b0:b0 + 128], in_=ps_bt[64:128, 768 + 128 * kv:896 + 128 * kv]), r=[V.bt_kcv], w=[kcv])
    for kv in range(2):
        kview = kcv[:, 144 * kv:144 * kv + 144].rearrange("p (a b) -> p a b", b=16)
        for m in range(16):
            l = m
            S.op("pe", lambda e: e.matmul(ps_y[:, 8 * kv:8 * kv + 8], lhsT=V.w1kv[kv][:, m, :], rhs=kview[:, (l // 16):(l // 16) + 8, l % 16],
                                          start=(m == 0), stop=(m == 15)), r=[V.w1kv[kv], kcv], w=[ps_y])
    cu, cw, cg, c2, c2b, c2n, css = V.cu, V.cw, V.cg, V.c2, V.c2b, V.c2n, V.css
    for kv in range(2):
        S.op("act", lambda e: e.activation(out=cu[:, 8 * kv:8 * kv + 8], in_=ps_y[:, 8 * kv:8 * kv + 8], func=AF.Identity, bias=V.cbias[:, kv:kv + 1]),
             r=[ps_y, V.cbias], w=[cu])
    S.op("dve", lambda e: e.tensor_tensor(out=cw[:, :], in0=cu[:, :], in1=cu[:, :], op=ALU.mult), r=[cu], w=[cw])
    S.op("dve", lambda e: e.tensor_scalar(out=cw[:, :], in0=cw[:, :], scalar1=0.044715, scalar2=1.0, op0=ALU.mult, op1=ALU.add), r=[cw], w=[cw])
    S.op("dve", lambda e: e.tensor_tensor(out=cw[:, :], in0=cw[:, :], in1=cu[:, :], op=ALU.mult), r=[cw, cu], w=[cw])
    S.op("act", lambda e: e.activation(out=cw[:, :], in_=cw[:, :], func=AF.Exp, scale=-2.0 * 0.7978845608028654), r=[cw], w=[cw])
    S.op("dve", lambda e: e.tensor_scalar(out=cw[:, :], in0=cw[:, :], scalar1=1.0, scalar2=None, op0=ALU.add), r=[cw], w=[cw])
    S.op("dve", lambda e: e.reciprocal(out=cw[:, :], in_=cw[:, :]), r=[cw], w=[cw])
    S.op("dve", lambda e: e.tensor_tensor(out=cg[:, :], in0=cu[:, :], in1=cw[:, :], op=ALU.mult), r=[cu, cw], w=[cg])
    for kv in range(2):
        S.op("pe", lambda e: e.matmul(ps_y[0:8, 16 + 64 * kv:16 + 64 * kv + 64], lhsT=cg[:, 8 * kv:8 * kv + 8], rhs=V.w2kv[:, 64 * kv:64 * kv + 64], start=True, stop=True),
             r=[cg, V.w2kv], w=[ps_y])
    S.op("act", lambda e: e.copy(out=c2[:, :], in_=ps_y[0:8, 16:144]), r=[ps_y], w=[c2])
    S.op("act", lambda e: e.activation(out=c2n[:, :], in_=c2[:, 0:64], func=AF.Square, accum_out=css[:, 0:1]), r=[c2], w=[c2n, css])
    S.op("act", lambda e: e.activation(out=css[:, 1:2], in_=css[:, 0:1], func=AF.Ln, scale=1.0 / 64, bias=EPS), r=[css], w=[css])
    S.op("act", lambda e: e.activation(out=css[:, 0:1], in_=css[:, 1:2], func=AF.Exp, scale=-0.5), r=[css], w=[css])
    S.op("dve", lambda e: e.scalar_tensor_tensor(out=c2n[:, :], in0=c2[:, 0:64], scalar=css[:, 0:1], in1=V.kcw[:, :], op0=ALU.mult, op1=ALU.mult),
         r=[c2, css, V.kcw], w=[c2n])
    S.op("act", lambda e: e.copy(out=c2b[:, 0:64], in_=c2n[:, :]), r=[c2n], w=[c2b])
    S.op("act", lambda e: e.copy(out=c2b[:, 64:128], in_=c2[:, 64:128]), r=[c2], w=[c2b])
    crt = V.crt
    cc, cs_ = V.ccos[:, j, :], V.csin[:, j, :]
    S.op("dve", lambda e: e.tensor_tensor(out=crt[:, :, 0], in0=c2n[:, 0:8], in1=cc, op=ALU.mult), r=[c2n, V.ccos], w=[crt])
    S.op("dve", lambda e: e.tensor_tensor(out=crt[:, :, 1], in0=c2n[:, 8:16], in1=cs_, op=ALU.mult), r=[c2n, V.csin], w=[crt])
    S.op("dve", lambda e: e.tensor_tensor(out=crt[:, :, 2], in0=c2n[:, 0:8], in1=cs_, op=ALU.mult), r=[c2n, V.csin], w=[crt])
    S.op("dve", lambda e: e.tensor_tensor(out=crt[:, :, 3], in0=c2n[:, 8:16], in1=cc, op=ALU.mult), r=[c2n, V.ccos], w=[crt])
    S.op("dve", lambda e: e.tensor_tensor(out=c2b[:, 0:8], in0=crt[:, :, 0], in1=crt[:, :, 1], op=ALU.subtract), r=[crt], w=[c2b])
    S.op("dve", lambda e: e.tensor_tensor(out=c2b[:, 8:16], in0=crt[:, :, 2], in1=crt[:, :, 3], op=ALU.add), r=[crt], w=[c2b])
    for kv in range(2):
        S.op("pe", lambda e: e.transpose(out=ps_bt[0:64, 768 + 8 * kv:776 + 8 * kv], in_=c2b[:, 64 * kv:64 * kv + 64], identity=ident[0:8, 0:8]),
             r=[c2b, ident], w=[V.bt_kcv])
    S.op("dve", lambda e: e.tensor_copy(out=V.KCT[:, 8 * j:8 * j + 8], in_=ps_bt[0:64, 768:776]), r=[V.bt_kcv], w=[V.KCT])
    S.op("dve", lambda e: e.tensor_copy(out=V.VCT[:, 8 * j:8 * j + 8], in_=ps_bt[0:64, 776:784]), r=[V.bt_kcv], w=[V.VCT])
    if j == 0:
        S.op("pool", lambda e: e.memset(V.KCT[:, 0:1], 0.0), w=[V.KCT])
        S.op("pool", lambda e: e.memset(V.VCT[:, 0:1], 0.0), w=[V.VCT])
    ktl = j // 16
    S.op("pe", lambda e: e.transpose(out=ps_bt[:, 784:848], in_=V.VCT[:, ktl * 128:(ktl + 1) * 128], identity=ident[0:64, 0:64]), r=[V.VCT, ident], w=[V.bt_kcv])
    S.op("act", lambda e: e.copy(out=V.VC1[:, ktl, 0:64], in_=ps_bt[:, 784:848]), r=[V.bt_kcv], w=[V.VC1])


def nsa_topk(S, j, L, on_ps_free=None):
    V = _NS(L)
    io, nblk = V.io, V.nblk
    tma, tmb, tmc, ident, identf, ones = V.tma_l[j % 2], V.tmb, V.tmc_l[j % 2], V.ident, V.identf, V.ones
    ps_bt, ps_s, ps_x, ps_y, ps_f, ps_o = V.ps_bt, V.ps_s, V.ps_x, V.ps_y, V.ps_f, V.ps_o
    qk6, nsq, nss, nstmp, rt, qk6_bf, w6 = V.qk6, V.nsq, V.nss, V.nstmp, V.rt, V.qk6_bf, V.w6
    qT, qA, OB, gts, ez2 = V.qT_l[j % 2], V.qA_l[j % 2], V.OB_l[j % 2], V.gts_l[j % 2], V.ez2_l[j % 2]
    T0 = j * 128

    def v3(ap, a):
        return ap.rearrange("p (a b) -> p a b", a=a)

    ktl = j // 16
    nkt = ktl + 1
    Pc = V.Pc
    for kt in range(nkt):
        pt = ps_x
        S.op("pe", lambda e: e.matmul(pt[:, 0:512], lhsT=V.KCT[:, kt * 128:(kt + 1) * 128], rhs=qT[:, 0:512], start=True, stop=True), r=[V.KCT, qT], w=[pt])
        S.op("act", lambda e: e.activation(out=Pc[kt][:, :], in_=pt[:, 0:512], func=AF.Exp), r=[pt], w=[Pc[kt]])
        if kt == nkt - 1:
            S.op("dve", lambda e: e.tensor_tensor(out=v3(Pc[kt][:, :], 4), in0=v3(Pc[kt][:, :], 4),
                                                  in1=V.cmask[:, j % 16, :].unsqueeze(1).to_broadcast([128, 4, 128]), op=ALU.mult), r=[Pc[kt], V.cmask], w=[Pc[kt]])
    ocs, imp, impw, m8 = V.ocs, V.imp, V.impw, V.m8
    for h in range(4):
        bank = ps_x if h < 2 else ps_y
        c0 = (h % 2) * 193
        for kt in range(nkt):
            S.op("pe", lambda e: e.matmul(bank[:, c0:c0 + 193], lhsT=Pc[kt][:, h * 128:(h + 1) * 128], rhs=V.VC1[:, kt, :], start=(kt == 0), stop=(kt == nkt - 1)),
                 r=[Pc[kt], V.VC1], w=[bank])
    for h in range(4):
        bank = ps_x if h < 2 else ps_y
        c0 = (h % 2) * 193
        S.op("dve", lambda e: e.tensor_scalar(out=ocs[:, h:h + 1], in0=bank[:, c0 + 64:c0 + 65], scalar1=1e-30, scalar2=None, op0=ALU.max), r=[bank], w=[ocs])
    S.op("dve", lambda e: e.reciprocal(out=ocs[:, :], in_=ocs[:, :]), r=[ocs], w=[ocs])
    for h in range(4):
        bank = ps_x if h < 2 else ps_y
        c0 = (h % 2) * 193
        if h == 0:
            S.op("dve", lambda e: e.tensor_scalar(out=imp[:, :], in0=bank[:, c0 + 65:c0 + 193], scalar1=ocs[:, 0:1], scalar2=None, op0=ALU.mult), r=[bank, ocs], w=[imp])
        else:
            S.op("dve", lambda e: e.scalar_tensor_tensor(out=imp[:, :], in0=bank[:, c0 + 65:c0 + 193], scalar=ocs[:, h:h + 1], in1=imp[:, :], op0=ALU.mult, op1=ALU.add),
                 r=[bank, ocs, imp], w=[imp])
        if h < 2:
            S.op("act", lambda e: e.copy(out=OB[:, 3 * h, :], in_=bank[:, c0:c0 + 65]), r=[bank], w=[OB])
    if on_ps_free is not None:
        on_ps_free()
    S.op("dve", lambda e: e.tensor_tensor(out=impw[:, :], in0=imp[:, :], in1=V.bwide[:, 126 - 2 * j:254 - 2 * j], op=ALU.add), r=[imp, V.bwide], w=[impw])
    S.op("dve", lambda e: e.memset(impw[:, 0:1], BIGV), w=[impw])
    S.op("dve", lambda e: e.max(out=m8[:, 0:8], in_=impw[:, :]), r=[impw], w=[m8])
    S.op("dve", lambda e: e.match_replace(out=imp[:, :], in_to_replace=m8[:, 0:8], in_values=impw[:, :], imm_value=-3.0e38), r=[m8, impw], w=[imp])
    S.op("dve", lambda e: e.max(out=m8[:, 8:16], in_=imp[:, :]), r=[imp], w=[m8])
    S.op("dve", lambda e: e.tensor_scalar(out=imp[:, :], in0=impw[:, :], scalar1=m8[:, 15:16], scalar2=None, op0=ALU.is_ge), r=[impw, m8], w=[imp])
    S.op("dve", lambda e: e.tensor_scalar(out=V.negm[:, :].rearrange("p (a b) -> p a b", a=2)[:, :, 64:128], in0=imp[:, :].rearrange("p (a b) -> p a b", a=2),
                                          scalar1=NEGM, scalar2=-NEGM, op0=ALU.mult, op1=ALU.add), r=[imp], w=[V.negm])
    nhalf = 2 if j >= 32 else 1
    for hf in range(nhalf):
        S.op("pe", lambda e: e.transpose(out=V.ps_hT[:, 512 + hf * 128:512 + (hf + 1) * 128], in_=V.negm[:, hf * 128:(hf + 1) * 128], identity=ident[:, :]), r=[V.negm, ident], w=[V.ps_hT])
    for hf in range(nhalf):
        S.op("act", lambda e: e.copy(out=qA[hf][64:128, 0:128], in_=V.ps_hT[64:128, 512 + hf * 128:512 + (hf + 1) * 128]), r=[V.ps_hT], w=[qA[hf]])
        S.op("dve", lambda e: e.tensor_copy(out=qA[hf][64:128, 128:256], in_=V.ps_hT[64:128, 512 + hf * 128:512 + (hf + 1) * 128]), r=[V.ps_hT], w=[qA[hf]])
        S.op("pool", lambda e: e.tensor_copy(out=qA[hf][0:64, :], in_=qT[:, 0:256]), r=[qT], w=[qA[hf]])
    S.op("act", lambda e: e.activation(out=gts[:, :], in_=tmc[:, 128:134], func=AF.Exp, scale=-1.0), r=[tmc], w=[gts])
    S.op("dve", lambda e: e.tensor_scalar(out=gts[:, :], in0=gts[:, :], scalar1=1.0, scalar2=None, op0=ALU.add), r=[gts], w=[gts])
    S.op("act", lambda e: e.activation(out=ez2[:, :], in_=tma[:, 384:512], func=AF.Exp, scale=-1.0), r=[tma], w=[ez2])
    S.op("act", lambda e: e.activation(out=ez2[:, :], in_=ez2[:, :], func=AF.Ln, bias=1.0), r=[ez2], w=[ez2])
    S.op("act", lambda e: e.activation(out=ez2[:, :], in_=ez2[:, :], func=AF.Exp, scale=-1.0), r=[ez2], w=[ez2])
    S.op("pool", lambda e: e.tensor_tensor(out=ez2[:, :], in0=ez2[:, :], in1=tma[:, 384:512], op=ALU.mult), r=[ez2, tma], w=[ez2])


def nsa_attn(S, j, L):
    V = _NS(L)
    io, nblk = V.io, V.nblk
    ident, identf = V.ident, V.identf
    ps_s, ps_o = V.ps_s, V.ps_o
    qT, qA, OB, gts, ez2 = V.qT_l[j % 2], V.qA_l[j % 2], V.OB_l[j % 2], V.gts_l[j % 2], V.ez2_l[j % 2]
    T0 = j * 128
    Pa, oT = V.Pa, V.oT
    kts = list(range(max(0, j - 4), j + 1))

    def w_score(idx):
        kt = kts[idx]
        pt = ps_s[idx % 2]
        caus = (kt == j)
        anti = (kt == j - 4)
        S.op("pe", lambda e: e.matmul(pt[:, 0:256], lhsT=V.KWT[:, (kt % WR) * 128:(kt % WR + 1) * 128], rhs=qA[0][0:64, :], start=True, stop=not (caus or anti)),
             r=[V.KWTb[kt], qA[0]], w=[pt])
        if caus:
            S.op("pe", lambda e: e.matmul(pt[:, 0:256], lhsT=ident[:, :], rhs=V.causneg2[:, :], start=False, stop=True), r=[ident, V.causneg2], w=[pt])
        if anti:
            S.op("pe", lambda e: e.matmul(pt[:, 0:256], lhsT=ident[:, :], rhs=V.anti2[:, :], start=False, stop=True), r=[ident, V.anti2], w=[pt])

    w_score(0)
    for idx, kt in enumerate(kts):
        pt = ps_s[idx % 2]
        if idx + 1 < len(kts):
            w_score(idx + 1)
        P = Pa[idx % 3]
        S.op("act", lambda e: e.activation(out=P[:, 0:256], in_=pt[:, 0:256], func=AF.Exp), r=[pt], w=[P])
        S.op("pe", lambda e: e.matmul(ps_o[0:65, 256:512], lhsT=V.VW1[:, kt % WR, :], rhs=P[:, 0:256], start=(idx == 0), stop=(idx == len(kts) - 1)),
             r=[V.VW1b[kt], P], w=[ps_o])
    S.op("act", lambda e: e.copy(out=oT[:, 256:512], in_=ps_o[0:65, 256:512]), r=[ps_o], w=[oT])
    groups = [list(range(g0, min(g0 + 2, j + 1))) for g0 in range(0, j + 1, 2)]

    def s_score(gi):
        pt = ps_s[gi % 2]
        for ti, kt in enumerate(groups[gi]):
            S.op("pe", lambda e: e.matmul(pt[:, ti * 256:(ti + 1) * 256], lhsT=V.KST[:, kt * 128:(kt + 1) * 128], rhs=qA[kt // 32][:, :], start=True, stop=(kt != j)),
                 r=[V.KSTb[kt], V.KST, qA[kt // 32]], w=[pt])
            if kt == j:
                S.op("pe", lambda e: e.matmul(pt[:, ti * 256:(ti + 1) * 256], lhsT=ident[:, :], rhs=V.causneg2[:, :], start=False, stop=True), r=[ident, V.causneg2], w=[pt])

    s_score(0)
    for gi, grp in enumerate(groups):
        pt = ps_s[gi % 2]
        if gi + 1 < len(groups):
            s_score(gi + 1)
        P = Pa[gi % 3]
        w_ = 256 * len(grp)
        S.op("act", lambda e: e.activation(out=P[:, 0:w_], in_=pt[:, 0:w_], func=AF.Exp), r=[pt], w=[P])
        for ti, kt in enumerate(grp):
            S.op("pe", lambda e: e.matmul(ps_o[0:65, 0:256], lhsT=V.VS1[:, kt, :], rhs=P[:, ti * 256:(ti + 1) * 256], start=(kt == 0), stop=(kt == j)), r=[V.VS1b[kt], P], w=[ps_o])
    S.op("act", lambda e: e.copy(out=oT[:, 0:256], in_=ps_o[0:65, 0:256]), r=[ps_o], w=[oT])
    for br in (1, 2):
        for h in range(2):
            src0 = (br - 1) * 256 + h * 128
            pos = (3 * h + br) * 65
            S.op("pe", lambda e: e.transpose(out=ps_s[0][:, pos:pos + 65], in_=oT[0:65, src0:src0 + 128], identity=identf[0:65, 0:65]), r=[oT, identf], w=[ps_s[0]])
    for h in range(2):
        S.op("dve", lambda e: e.tensor_copy(out=OB[:, 3 * h + 1:3 * h + 3, :], in_=ps_s[0][:, (3 * h + 1) * 65:(3 * h + 3) * 65].rearrange("p (a b) -> p a b", a=2)),
             r=[ps_s[0]], w=[OB])
    coef = V.coef
    S.op("dve", lambda e: e.tensor_scalar(out=coef[:, :], in0=OB[:, :, 64], scalar1=1e-30, scalar2=None, op0=ALU.max), r=[OB], w=[coef])
    S.op("dve", lambda e: e.tensor_tensor(out=coef[:, :], in0=coef[:, :], in1=gts[:, :], op=ALU.mult), r=[coef, gts], w=[coef])
    S.op("dve", lambda e: e.reciprocal(out=coef[:, :], in_=coef[:, :]), r=[coef], w=[coef])
    yacc = V.yacc
    for h in range(2):
        ys = yacc[:, h * 64:(h + 1) * 64]
        S.op("dve", lambda e: e.tensor_scalar(out=ys, in0=OB[:, 3 * h, 0:64], scalar1=coef[:, 3 * h:3 * h + 1], scalar2=None, op0=ALU.mult), r=[OB, coef], w=[yacc])
        for br in (1, 2):
            S.op("dve", lambda e: e.scalar_tensor_tensor(out=ys, in0=OB[:, 3 * h + br, 0:64], scalar=coef[:, 3 * h + br:3 * h + br + 1], in1=ys, op0=ALU.mult, op1=ALU.add),
                 r=[OB, coef, yacc], w=[yacc])
    S.op("dve", lambda e: e.tensor_tensor(out=V.yb_bf[:, :], in0=yacc[:, :], in1=ez2[:, :], op=ALU.mult), r=[yacc, ez2], w=[V.yb_bf])
    S.op("pe", lambda e: e.transpose(out=V.ps_hT[:, 896:1024], in_=V.yb_bf[:, :], identity=ident[:, :]), r=[V.yb_bf, ident], w=[V.ps_hT])
    yT = V.ybT[j % 2]
    S.op("act", lambda e: e.copy(out=yT[:, :], in_=V.ps_hT[:, 896:1024]), r=[V.ps_hT], w=[yT])
    V.out_toks.append(S.op("pool", lambda e: e.dma_start(out=io["y0dst"](j, 1), in_=yT[:, :]), r=[yT], dma=yT))


MLSTM_COLS = 2568


def consts_A(nblk=NB):
    c = {}
    c["ident"] = np.eye(128, dtype=np.float32).astype(NPBF)
    c["identf"] = np.eye(128, dtype=np.float32)
    s = np.arange(128)[:, None]
    t = np.arange(128)[None, :]
    c["mask128"] = (s <= t).astype(np.float32)
    cn = np.where(s > t, -NEGM, 0.0).astype(np.float32)
    an = np.where(s <= t, -NEGM, 0.0).astype(np.float32)
    c["causneg2"] = np.ascontiguousarray(np.tile(cn, (1, 2))).astype(NPBF)
    c["anti2"] = np.ascontiguousarray(np.tile(an, (1, 2))).astype(NPBF)
    half = 8
    inv_freq = (np.float32(500000.0) ** (-np.arange(half, dtype=np.float32) / np.float32(half))).astype(np.float32)
    pos = (np.arange(nblk)[None, :] * 128 + np.arange(128)[:, None]).astype(np.float32)
    ang = pos[:, :, None] * inv_freq[None, None, :]
    c["rcos"] = np.cos(ang).astype(np.float32)
    c["rsin"] = np.sin(ang).astype(np.float32)
    n = (8 * np.arange(nblk)[None, :] - 1 + np.arange(8)[:, None])
    cpos = (16 * n + 31).astype(np.float32)
    cang = cpos[:, :, None] * inv_freq[None, None, :]
    c["ccos"] = np.cos(cang).astype(np.float32)
    c["csin"] = np.sin(cang).astype(np.float32)
    u = np.arange(254)[None, :]
    hi = (np.arange(128)[:, None] >= 64).astype(np.int64)
    d = u - 126 - hi
    c["bwide"] = (BIGV * ((d == 0) | (d == -1)) - BIGV * (d > 0)).astype(np.float32)
    cl = np.arange(128)[:, None, None]
    jj = np.arange(16)[None, :, None]
    tl = np.arange(128)[None, None, :]
    c["cmask"] = ((cl <= 8 * jj + 7) & (16 * (cl - 8 * jj) + 15 <= tl)).astype(np.float32).astype(NPBF)
    key = np.arange(nblk * 128)[None, :]
    rr = np.arange(64)[:, None]
    c["kind"] = (rr == ((key // 64) % 64)).astype(np.float32).astype(NPBF)
    col = np.arange(512)
    nn = col - 1
    m = np.arange(128)
    ov = np.clip(np.minimum(16 * nn[:, None] + 32, 64 * m[None, :] + 64) - np.maximum(16 * nn[:, None], 64 * m[None, :]), 0, None).astype(np.float32)
    ov[0, :] = 0.0
    c["ov"] = np.ascontiguousarray(ov.reshape(4, 128, 128).transpose(1, 0, 2)).astype(NPBF)
    return c


def cols_A(s):
    hm = s
    g = s // 2
    ha = 4 * g + 2 * (s % 2)
    hb = ha + 1
    oth = [h for h in range(4 * g, 4 * g + 4) if h not in (ha, hb)]
    N0 = MLSTM_COLS
    ar = np.arange
    fm = [ar(hm * 128, hm * 128 + 128), ar(512 + hm * 128, 512 + hm * 128 + 128)]
    tma = [ar(1024 + hm * 128, 1024 + hm * 128 + 128), ar(1536 + hm * 128, 1536 + hm * 128 + 128), ar(2048 + hm * 128, 2048 + hm * 128 + 128),
           ar(N0 + 1304 + ha * 64, N0 + 1304 + ha * 64 + 128)]
    tmb = [ar(N0 + h * 64, N0 + h * 64 + 64) for h in (ha, hb, oth[0], oth[1])]
    tmb += [ar(N0 + base + g * 64, N0 + base + g * 64 + 64) for base in (768, 1024, 512, 640)]
    tmc = [ar(N0 + 896 + g * 64, N0 + 896 + g * 64 + 64), ar(N0 + 1152 + g * 64, N0 + 1152 + g * 64 + 64),
           ar(N0 + 1280 + ha * 3, N0 + 1280 + ha * 3 + 6), ar(2560 + hm, 2561 + hm), ar(2564 + hm, 2565 + hm)]
    return np.concatenate(fm), np.concatenate(tma + tmb + tmc)


def inputs_A(inp, nblk=NB):
    T = nblk * 128
    cs = consts_A(nblk)
    maps = []
    for core in range(8):
        b, s = core // 4, core % 4
        g = s // 2
        fm, tm = cols_A(s)
        cols = np.concatenate([fm, tm[0:512], tm[512:1024], tm[1024:1160]])
        W = inp["ev_w_in"][0][:, cols]
        bias = inp["ev_b_in"][0]
        m = dict(cs)
        m["x"] = np.ascontiguousarray(inp["x"][b, :T])
        m["win0"] = np.ascontiguousarray(W)
        m["normw0"] = colT(inp["norm_w"][0], 8)
        m["bfm"] = colT(bias[fm], 2)
        m["btm"] = np.ascontiguousarray(np.broadcast_to(bias[tm][None, :], (128, 1160))).astype(np.float32)
        cw = inp["mlstm_conv_w"][0]
        m["convw"] = np.ascontiguousarray(np.concatenate([cw[:, s * 128:(s + 1) * 128].T, cw[:, 512 + s * 128:512 + (s + 1) * 128].T], axis=1)).astype(np.float32)
        cb = inp["mlstm_conv_b"][0]
        m["convb"] = np.ascontiguousarray(np.stack([cb[s * 128:(s + 1) * 128], cb[512 + s * 128:512 + (s + 1) * 128]], axis=1)).astype(np.float32)
        m["fbias"] = np.array([[inp["mlstm_f_bias"][0][s], 0.0]], dtype=np.float32)
        m["hnm"] = np.ascontiguousarray(np.broadcast_to(inp["mlstm_head_norm"][0][s * 128:(s + 1) * 128][None, :], (128, 128))).astype(np.float32)
        qn = inp["nsa_q_norm"][0]
        kn = inp["nsa_k_norm"][0]
        w6 = np.concatenate([qn, qn, qn, qn, kn[1], kn[2]])
        m["w6"] = np.ascontiguousarray(np.broadcast_to(w6[None, :], (128, 384))).astype(np.float32)
        m["kcw"] = np.ascontiguousarray(np.broadcast_to(kn[0][None, :], (8, 64))).astype(np.float32)
        m["w1k"] = np.ascontiguousarray(inp["cmp_k_w1"][0])
        m["w1v"] = np.ascontiguousarray(inp["cmp_v_w1"][0])
        m["w2kv"] = np.ascontiguousarray(np.concatenate([inp["cmp_k_w2"][0], inp["cmp_v_w2"][0]], axis=1))
        pk, pv = inp["cmp_k_pos"][0], inp["cmp_v_pos"][0]
        m["posT"] = np.ascontiguousarray(np.concatenate([np.concatenate([p_[0:16].T, p_[16:32].T], axis=0) for p_ in (pk, pv)], axis=1)).astype(np.float32)
        maps.append(m)
    return maps


A_IN_SPECS = [
    ("ident", [128, 128], BF16), ("identf", [128, 128], F32), ("mask128", [128, 128], F32), ("causneg2", [128, 256], BF16), ("anti2", [128, 256], BF16),
    ("rcos", [128, None, 8], F32), ("rsin", [128, None, 8], F32), ("ccos", [8, None, 8], F32), ("csin", [8, None, 8], F32), ("bwide", [128, 254], F32),
    ("cmask", [128, 16, 128], BF16), ("kind", [64, "T", ], BF16), ("ov", [128, 4, 128], BF16),
    ("win0", [1024, NCOL_A], F32), ("normw0", [128, 8], F32), ("bfm", [128, 2], F32), ("btm", [128, 1160], F32), ("convw", [128, 8], F32),
    ("convb", [128, 2], F32), ("fbias", [1, 2], F32), ("hnm", [128, 128], F32), ("w6", [128, 384], F32), ("kcw", [8, 64], F32),
    ("w1k", [2048, 128], F32), ("w1v", [2048, 128], F32), ("w2kv", [128, 128], F32), ("posT", [128, 32], F32),
]


BC_IN_SPECS = [
    ("xr", [SEQ, 1024], F32), ("wout0", [1024, 1024], F32), ("normw1", [128, 8], F32), ("win1", [1024, 1024], F32),
    ("bcol1", [128, 8], F32), ("lbl", [128, 4], F32), ("hn1", [128, 2], F32), ("mask32", [CS, 128], F32),
]
NQ = 4
QB = NB // NQ
QT = QB * 128
GROUPS = [[0, 1, 2, 3], [4, 5, 6, 7]]


def make_nc_fused():
    nblk = NB
    nc = bass.Bass("TRN2", target_bir_lowering=False)
    S = Sched(nc)
    io = {"x": dram_in(nc, "x", [SEQ, 1024], F32)}
    for (name, shape, dt_) in A_IN_SPECS:
        shape = [nblk if d is None else (nblk * 128 if d == "T" else d) for d in shape]
        io[name] = dram_in(nc, name, shape, dt_)
    for (name, shape, dt_) in BC_IN_SPECS:
        io[name] = dram_in(nc, name, shape, dt_)
    io["wout1"] = dram_in(nc, "wout1", [1024, 256], F32)
    io["outs"] = dram_out(nc, "outs", [SEQ, 256], F32)
    y0loc = [nc.dram_tensor(f"y0loc{q}", [256, QT], BF16) for q in range(NQ)]
    y0all = [nc.dram_tensor(f"y0all{q}", [1024, QT], BF16) for q in range(NQ)]
    y1loc = [nc.dram_tensor(f"y1loc{q}", [256, QT], BF16) for q in range(NQ)]
    y1all = [nc.dram_tensor(f"y1all{q}", [1024, QT], BF16) for q in range(NQ)]
    io["x1s"] = nc.dram_tensor("x1s_scr", [SEQ, 256], F32).ap()
    y0buf = [Buf(f"y0all{q}") for q in range(NQ)]
    y1buf = [Buf(f"y1all{q}") for q in range(NQ)]
    ccsem = S.new_sem("cc")

    def gather(loc, dst, buf, out_toks):
        pq = S.q["pool"]
        mx = {}
        for (sem, val) in out_toks:
            mx[sem] = max(mx.get(sem, 0), val)
        for sem, val in mx.items():
            S._wait(pq, sem, val)
        ins = nc.gpsimd.collective_compute("AllGather", ALU.bypass, replica_groups=GROUPS, ins=[loc.ap().opt()], outs=[dst.ap().opt()])
        ccsem.v += 1
        ins.then_inc(ccsem.h, 1)
        buf.w = (ccsem, ccsem.v)

    def after_A(j, out_toks):
        if j % QB == QB - 1:
            q = j // QB
            gather(y0loc[q], y0all[q], y0buf[q], out_toks)

    def after_B(j, out_toks):
        if j % QB == QB - 1:
            q = j // QB
            gather(y1loc[q], y1all[q], y1buf[q], out_toks)

    io["y0dst"] = lambda j, which: y0loc[j // QB][which * 128:(which + 1) * 128, (j % QB) * 128:(j % QB + 1) * 128]
    io["y0src"] = lambda j: y0all[j // QB].ap().rearrange("(c p) t -> p c t", p=128)[:, :, (j % QB) * 128:(j % QB + 1) * 128]
    io["y0buf"] = lambda j: y0buf[j // QB]
    io["y1dst"] = lambda j, hd: y1loc[j // QB][hd * 128:(hd + 1) * 128, (j % QB) * 128:(j % QB + 1) * 128]
    io["y1src"] = lambda j: y1all[j // QB].ap().rearrange("(c p) t -> p c t", p=128)[:, :, (j % QB) * 128:(j % QB + 1) * 128]
    io["y1buf"] = lambda j: y1buf[j // QB]
    io["y1src4"] = lambda g: y1all[(4 * g) // QB].ap().rearrange("(c p) t -> p c t", p=128)[:, :, ((4 * g) % QB) * 128:((4 * g) % QB + 4) * 128]

    wout0_bf = S.sbuf("wout0_bf", [128, 8, 1024], BF16)
    win1_bf = S.sbuf("win1_bf", [128, 8, 1024], BF16)
    normw1_t = S.sbuf("g_normw1", [128, 8], F32)
    S.op("sp", lambda e: e.dma_start(out=normw1_t[:, :], in_=io["normw1"][:, :]), w=[normw1_t], dma=normw1_t)
    io["bc_weights"] = (wout0_bf, win1_bf)

    def bc_preload_step(i, stage):
        k = i - 2
        if not (0 <= k < 16):
            return
        wsrc, wdst, sc = (io["wout0"], wout0_bf, None) if k < 8 else (io["win1"], win1_bf, normw1_t)
        kc = k % 8
        st = stage[k % 2]
        wv = wsrc.rearrange("(c p) n -> p c n", p=128)
        S.op("sp", lambda e: e.dma_start(out=st[:, 0:1024], in_=wv[:, kc, :]), w=[st], dma=st)
        if sc is None:
            S.op("pool", lambda e: e.tensor_copy(out=wdst[:, kc, :], in_=st[:, 0:1024]), r=[st], w=[wdst])
        else:
            S.op("pool", lambda e: e.tensor_scalar(out=wdst[:, kc, :], in0=st[:, 0:1024], scalar1=sc[:, kc:kc + 1], scalar2=None, op0=ALU.mult), r=[st, sc], w=[wdst])

    io["bc_preload_step"] = bc_preload_step
    io["after_block"] = after_A
    S.push_scope()
    build_A(S, io, nblk, True, True)
    S.barrier()
    S.pop_scope()
    io["after_block"] = after_B
    S.push_scope()
    build_BC(S, io, nblk)
    S.barrier()
    S.pop_scope()
    del io["after_block"]
    S.push_scope()
    toks = build_D(S, io, nblk)
    S.finish(toks)
    S.pop_scope()
    S.close()
    return nc, S


def consts_BC():
    s = np.arange(CS)[:, None]
    t = np.arange(CS)[None, :]
    m = (s <= t).astype(np.float32)
    return {"mask32": np.ascontiguousarray(np.tile(m, (1, NCH)))}


def colT(v, n):
    return np.ascontiguousarray(np.asarray(v, dtype=np.float32).reshape(n, 128).T)


def inputs_fused(inp):
    mapsA = inputs_A(inp, NB)
    cb = consts_BC()
    perm = np.concatenate([np.concatenate([np.arange(r * 128, r * 128 + 128), np.arange(512 + r * 128, 512 + r * 128 + 128)]) for r in range(4)])
    wout0_g = inp["ev_w_out"][0][perm, :]
    maps = []
    for core in range(8):
        b, s = core // 4, core % 4
        r = 256 * s
        h0, h1 = 2 * s, 2 * s + 1
        cols = []
        for base in (0, 2048, 1024, 3072):
            for h in (h0, h1):
                cols.append(np.arange(base + h * 128, base + (h + 1) * 128))
        cols = np.concatenate(cols)
        win = np.roll(inp["od_w_in"][0], -r, axis=0)[:, cols]
        lbl = np.stack([inp["hgrn_lb_logits"][sl, h * 128:(h + 1) * 128] for h in (h0, h1) for sl in (0, 1)], axis=1)
        hn = np.stack([inp["hgrn_head_norm"][0][h * 128:(h + 1) * 128] for h in (h0, h1)], axis=1)
        m = dict(mapsA[core])
        m.update(cb)
        m["xr"] = np.ascontiguousarray(np.roll(inp["x"][b], -r, axis=1))
        m["wout0"] = np.ascontiguousarray(np.roll(wout0_g, -r, axis=1))
        m["normw1"] = colT(np.roll(inp["norm_w"][1], -r), 8)
        m["win1"] = np.ascontiguousarray(win)
        m["bcol1"] = colT(inp["od_b_in"][0][cols], 8)
        m["lbl"] = np.ascontiguousarray(lbl.astype(np.float32))
        m["hn1"] = np.ascontiguousarray(hn.astype(np.float32))
        m["wout1"] = np.ascontiguousarray(inp["od_w_out"][0][:, s * 256:(s + 1) * 256])
        maps.append(m)
    return maps


def kernel(**inputs):
    inp = {k: np.asarray(v, dtype=np.float32) for k, v in inputs.items()}
    B = inp["x"].shape[0]
    nc, _ = make_nc_fused()
    res = run_bass_kernel_spmd(nc, inputs_fused(inp), core_ids=list(range(8)))
    out = np.zeros((B, SEQ, D), dtype=np.float32)
    for core in range(8):
        b, s = core // 4, core % 4
        out[b][:, s * 256:(s + 1) * 256] = res.results[core]["outs"]
    return out
```

```python
import numpy as np
import ml_dtypes
from contextlib import ExitStack
import threading
import concourse.bass as bass
import concourse.mybir as mybir
from concourse.bass_utils import run_bass_kernel_spmd

F32 = mybir.dt.float32
BF16 = mybir.dt.bfloat16
AF = mybir.ActivationFunctionType
ALU = mybir.AluOpType
AX = mybir.AxisListType
NPBF = ml_dtypes.bfloat16

SEM_LIMIT = 30000
FUSE_WAIT = True
NO_SES = set()
AW = 0.75


class Sem:
    __slots__ = ("h", "v")

    def __init__(self, h):
        self.h = h
        self.v = 0


class Buf:
    __slots__ = ("name", "w", "r", "dsem", "excl")

    def __init__(self, name=""):
        self.name = name
        self.excl = False
        self.w = None
        self.r = {}
        self.dsem = None


class Tile:
    def __init__(self, t, b):
        self.t = t
        self.b = b

    def __getitem__(self, k):
        return self.t[k]


class Queue:
    def __init__(self, name, eng):
        self.name = name
        self.eng = eng
        self.sem = None
        self.known = {}
        self.own = set()


class Sched:
    def __init__(self, nc, same_engine_sync=True):
        self.nc = nc
        self.es = ExitStack()
        self.nsem = 0
        self.same_engine_sync = same_engine_sync
        self.q = {}
        for name, eng in [("pe", nc.tensor), ("dve", nc.vector), ("act", nc.scalar),
                          ("pool", nc.gpsimd), ("sp", nc.sync)]:
            q = Queue(name, eng)
            self.q[name] = q
        for name in ("pe", "dve", "act", "pool"):
            q = self.q[name]
            q.sem = self.new_sem(name)
            q.own.add(q.sem)
        self.ninst = 0
        self.nwait = 0
        self.out_toks = []
        self.scopes = []
        self.dsems = []
        self._coop = None
        self._flags = set()
        self._tls = threading.local()

    def new_sem(self, name):
        self.nsem += 1
        h = self.es.enter_context(self.nc.semaphore(f"s{self.nsem}_{name}"))
        return Sem(h)

    def push_scope(self):
        self.scopes.append(ExitStack())

    def pop_scope(self):
        self.scopes.pop().close()

    def _stack(self):
        return self.scopes[-1] if self.scopes else self.es

    def barrier(self):
        toks = []
        for qn in ("pe", "dve", "act", "pool"):
            for sem in self.q[qn].own:
                if sem.v > 0:
                    toks.append((sem, sem.v))
        for sem in self.dsems:
            if sem.v > 0:
                toks.append((sem, sem.v))
        for qn in ("pe", "dve", "act", "pool", "sp"):
            q = self.q[qn]
            for (sem, val) in toks:
                self._wait(q, sem, val)

    def sbuf(self, name, shape, dtype):
        t = self._stack().enter_context(self.nc.sbuf_tensor(name, list(shape), dtype))
        return Tile(t, Buf(name))

    def psum(self, name, shape, dtype):
        t = self._stack().enter_context(self.nc.psum_tensor(name, list(shape), dtype))
        b = Buf(name)
        b.excl = True
        return Tile(t, b)

    @staticmethod
    def _bufs(lst):
        out = []
        for x in lst:
            if x is None:
                continue
            out.append(x.b if isinstance(x, Tile) else x)
        return out

    def _wait(self, q, sem, val):
        if q.known.get(sem, 0) >= val:
            return
        q.eng.wait_ge(sem.h, val)
        q.known[sem] = val
        self.nwait += 1

    def op(self, qname, fn, r=(), w=(), dma=None):
        q = self.q[qname]
        rb = self._bufs(r)
        wb = self._bufs(w)
        ex = [b for b in rb if b.excl and b not in wb]
        if ex:
            wb = wb + ex
            rb = [b for b in rb if not b.excl]
        deps = []
        for b in rb:
            if b.w is not None:
                deps.append(b.w)
        for b in wb:
            if b.w is not None:
                deps.append(b.w)
            deps.extend(b.r.items())
        need = {}
        for (sem, val) in deps:
            if sem in q.own and dma is None:
                if qname == "pe" or not self.same_engine_sync or qname in NO_SES:
                    continue
            if q.known.get(sem, 0) >= val:
                continue
            if need.get(sem, 0) < val:
                need[sem] = val
        need = list(need.items())
        fused = None
        if FUSE_WAIT and need and dma is None:
            fused = need.pop()
        for (sem, val) in need:
            self._wait(q, sem, val)
        ins = fn(q.eng)
        if fused is not None:
            ins._wait_ge(fused[0].h, fused[1])
            q.known[fused[0]] = fused[1]
        self.ninst += 1
        if dma is not None:
            ob = dma.b if isinstance(dma, Tile) else dma
            if ob.dsem is None or ob.dsem.v + 16 > SEM_LIMIT:
                ob.dsem = self.new_sem("d_" + ob.name)
                self.dsems.append(ob.dsem)
            ob.dsem.v += 16
            ins.then_inc(ob.dsem.h, 16)
            tok = (ob.dsem, ob.dsem.v)
        else:
            if q.sem.v + 1 > SEM_LIMIT:
                q.sem = self.new_sem(qname)
                q.own.add(q.sem)
            q.sem.v += 1
            ins.then_inc(q.sem.h, 1)
            tok = (q.sem, q.sem.v)
            q.known[q.sem] = max(q.known.get(q.sem, 0), 0)
        for b in wb:
            b.w = tok
            b.r = {}
        for b in rb:
            if b in wb:
                continue
            if b.r.get(tok[0], 0) < tok[1]:
                b.r[tok[0]] = tok[1]
        if self._coop is not None:
            st = self._coop
            i = self._tls.idx
            st["cnt"][i] += 1
            if st["cnt"][i] >= st["w"][i]:
                st["cnt"][i] = 0
                self._yield()
        return tok

    def parallel(self, fns, weights=None):
        assert self._coop is None
        n = len(fns)
        if n == 1:
            fns[0]()
            return
        st = {"turn": 0, "alive": [True] * n, "cv": threading.Condition(), "err": None,
              "w": list(weights) if weights else [1] * n, "cnt": [0] * n}
        self._coop = st

        def nxt_alive(i):
            for k in range(1, n + 1):
                c = (i + k) % n
                if st["alive"][c]:
                    return c
            return None

        def runner(i, fn):
            self._tls.idx = i
            with st["cv"]:
                while st["turn"] != i:
                    st["cv"].wait()
            try:
                if st["err"] is None:
                    fn()
            except BaseException as e:
                st["err"] = e
            finally:
                with st["cv"]:
                    st["alive"][i] = False
                    c = nxt_alive(i)
                    st["turn"] = c if c is not None else -1
                    st["cv"].notify_all()

        st["nxt"] = nxt_alive
        ths = [threading.Thread(target=runner, args=(i, f)) for i, f in enumerate(fns)]
        for t in ths:
            t.start()
        for t in ths:
            t.join()
        self._coop = None
        if st["err"] is not None:
            raise st["err"]

    def flag_set(self, name):
        self._flags.add(name)

    def flag_wait(self, name):
        assert self._coop is not None, "flag_wait outside parallel section would deadlock"
        while name not in self._flags:
            if self._coop["err"] is not None:
                raise RuntimeError("sibling emitter failed")
            self._yield()

    def _yield(self):
        st = self._coop
        i = self._tls.idx
        if st["err"] is not None:
            raise RuntimeError("sibling emitter failed")
        with st["cv"]:
            c = st["nxt"](i)
            if c is not None and c != i:
                st["turn"] = c
                st["cv"].notify_all()
                while st["turn"] != i:
                    st["cv"].wait()

    def finish(self, toks):
        q = self.q["sp"]
        for (sem, val) in toks:
            self._wait(q, sem, val)

    def close(self):
        self.es.close()


D = 1024
SEQ = 8192
CS = 64
NCH = 128 // CS
NB = SEQ // 128
EPS = 1e-6


def dram_in(nc, name, shape, dtype):
    return nc.dram_tensor(name, list(shape), dtype, kind="ExternalInput").ap()


def dram_out(nc, name, shape, dtype):
    return nc.dram_tensor(name, list(shape), dtype, kind="ExternalOutput").ap()


def load_cast_weight(S, w_dram, w_bf, ncols, stage, scale_cols=None, nk=8, qname="dve"):
    wv = w_dram.rearrange("(c p) n -> p c n", p=128)
    for kc in range(nk):
        st = stage[kc % len(stage)]
        S.op("sp", lambda e: e.dma_start(out=st[:, 0:ncols], in_=wv[:, kc, :]), w=[st], dma=st)
        if scale_cols is not None:
            S.op(qname, lambda e: e.tensor_scalar(out=w_bf[:, kc, :], in0=st[:, 0:ncols], scalar1=scale_cols[:, kc:kc + 1],
                                                  scalar2=None, op0=ALU.mult), r=[st, scale_cols], w=[w_bf])
        else:
            S.op(qname, lambda e: e.tensor_copy(out=w_bf[:, kc, :], in_=st[:, 0:ncols]), r=[st], w=[w_bf])


def rms_rstd(S, ss, n, npart, ncol, tmp):
    S.op("act", lambda e: e.activation(out=tmp[0:npart, 0:ncol], in_=ss[0:npart, 0:ncol], func=AF.Ln, scale=1.0 / n, bias=EPS),
         r=[ss], w=[tmp])
    S.op("act", lambda e: e.activation(out=ss[0:npart, 0:ncol], in_=tmp[0:npart, 0:ncol], func=AF.Exp, scale=-0.5),
         r=[tmp], w=[ss])


def build_BC(S, io, nblk=NB):
    x, wout0, normw1, win1, bcol1, lbl, hn1 = (io[k] for k in ("xr", "wout0", "normw1", "win1", "bcol1", "lbl", "hn1"))
    identd, mask32d = io["ident"], io["mask32"]
    x1s = io["x1s"]
    out_toks = []

    ident = S.sbuf("c_ident", [128, 128], BF16)
    mask32 = S.sbuf("c_mask32", [CS, 128], F32)
    normw = S.sbuf("c_normw", [128, 8], F32)
    bcol = S.sbuf("c_bcol", [128, 8], F32)
    nbcol = S.sbuf("c_nbcol", [128, 8], F32)
    lblt = S.sbuf("c_lbl", [128, 4], F32)
    lb = S.sbuf("c_lb", [128, 2], F32)
    oml = S.sbuf("c_oml", [128, 2], F32)
    hn = S.sbuf("c_hn", [128, 2], F32)
    ones = S.sbuf("c_ones", [128, 128], F32)
    for (t, d) in ((ident, identd), (mask32, mask32d), (normw, normw1), (bcol, bcol1), (lblt, lbl), (hn, hn1)):
        S.op("sp", lambda e: e.dma_start(out=t[:, :], in_=d[:, :]), w=[t], dma=t)
    S.op("pool", lambda e: e.memset(ones[:, :], 1.0), w=[ones])
    S.op("dve", lambda e: e.tensor_scalar(out=nbcol[:, :], in0=bcol[:, :], scalar1=-1.0, scalar2=None, op0=ALU.mult), r=[bcol], w=[nbcol])
    for hd in range(2):
        S.op("dve", lambda e: e.tensor_tensor(out=lb[:, hd:hd + 1], in0=lblt[:, 2 * hd + 1:2 * hd + 2], in1=lblt[:, 2 * hd:2 * hd + 1], op=ALU.subtract),
             r=[lblt], w=[lb])
    S.op("act", lambda e: e.activation(out=lb[:, :], in_=lb[:, :], func=AF.Exp), r=[lb], w=[lb])
    S.op("dve", lambda e: e.tensor_scalar(out=lb[:, :], in0=lb[:, :], scalar1=1.0, scalar2=None, op0=ALU.add), r=[lb], w=[lb])
    S.op("dve", lambda e: e.reciprocal(out=lb[:, :], in_=lb[:, :]), r=[lb], w=[lb])
    S.op("dve", lambda e: e.tensor_scalar(out=oml[:, :], in0=lb[:, :], scalar1=-1.0, scalar2=1.0, op0=ALU.mult, op1=ALU.add), r=[lb], w=[oml])

    if "bc_weights" in io:
        wout0_bf, win1_bf = io["bc_weights"]
    else:
        stage = [S.sbuf(f"wstage{i}", [128, 1024], F32) for i in range(2)]
        wout0_bf = S.sbuf("wout0_bf", [128, 8, 1024], BF16)
        win1_bf = S.sbuf("win1_bf", [128, 8, 1024], BF16)
        load_cast_weight(S, wout0, wout0_bf, 1024, stage)
        load_cast_weight(S, win1, win1_bf, 1024, stage, scale_cols=normw)

    x_t = [S.sbuf(f"x_t{i}", [128, 1024], F32) for i in range(2)]
    y0_t = [S.sbuf(f"y0_t{i}", [128, 8, 128], BF16) for i in range(2)]
    x1_t = [S.sbuf(f"x1_t{i}", [128, 1024], F32) for i in range(2)]
    junk = S.sbuf("junk", [128, 1024], BF16)
    ssx = S.sbuf("ssx", [128, 1], F32)
    sstmp = S.sbuf("sstmp", [128, 4], F32)
    h_bf = S.sbuf("h_bf", [128, 1024], BF16)
    hT = S.sbuf("hT", [128, 1024], BF16)
    t1 = [S.sbuf(f"t1_{i}", [128, 128], F32) for i in range(2)]
    t2 = [S.sbuf(f"t2_{i}", [128, 128], F32) for i in range(2)]
    t3 = [S.sbuf(f"t3_{i}", [128, 128], F32) for i in range(2)]
    fT = [S.sbuf(f"fT{i}", [128, 128], F32) for i in range(2)]
    lfT = [S.sbuf(f"lfT{i}", [128, 128], F32) for i in range(2)]
    kT = [S.sbuf(f"kT{i}", [128, 128], F32) for i in range(2)]
    qT = [S.sbuf(f"qT{i}", [128, 128], F32) for i in range(2)]
    szT_l = [[S.sbuf(f"szT{p}_{i}", [128, 128], F32) for i in range(2)] for p in range(2)]
    szT = szT_l[0]
    vT_l = [[S.sbuf(f"vT{p}_{i}", [128, 128], BF16) for i in range(2)] for p in range(2)]
    pre_l = [[S.sbuf(f"pre{p}_{g}", [128, 128], F32) for g in range(8)] for p in range(2)]
    G = [S.sbuf(f"G{i}", [128, 128], F32) for i in range(2)]
    eq = [S.sbuf(f"eq{i}", [128, 128], F32) for i in range(2)]
    ek = [S.sbuf(f"ek{i}", [128, 128], F32) for i in range(2)]
    dec_l = [[S.sbuf(f"dec{p}_{i}", [128, NCH], F32) for i in range(2)] for p in range(2)]
    dec = dec_l[0]
    qt_bf_l = [[S.sbuf(f"qt_bf{p}_{i}", [128, 128], BF16) for i in range(2)] for p in range(2)]
    kt_bf = [S.sbuf(f"kt_bf{i}", [128, 128], BF16) for i in range(2)]
    kh_bf = [S.sbuf(f"kh_bf{i}", [128, 128], BF16) for i in range(2)]
    AT_bf_l = [[S.sbuf(f"AT_bf{p}_{i}", [CS, 128], BF16) for i in range(2)] for p in range(2)]
    kh_tok_l = [[S.sbuf(f"kh_tok{p}_{i}", [CS, NCH * 128], BF16) for i in range(2)] for p in range(2)]
    v_tok_l = [[S.sbuf(f"v_tok{p}_{i}", [CS, NCH * 128], BF16) for i in range(2)] for p in range(2)]
    S32 = [S.sbuf(f"S32_{i}", [128, 128], F32) for i in range(2)]
    S_bf = [S.sbuf(f"S_bf{i}", [128, 128], BF16) for i in range(2)]
    osq_l = [S.sbuf(f"osq{i}", [CS, NCH * 128], F32) for i in range(2)]
    oss_l = [S.sbuf(f"oss{i}", [CS, NCH], F32) for i in range(2)]
    on_bf_l = [S.sbuf(f"on_bf{i}", [CS, NCH * 128], BF16) for i in range(2)]
    sstmp_l = [S.sbuf(f"sstmp_h{i}", [128, 4], F32) for i in range(2)]
    y_bf = [S.sbuf(f"y_bf{i}", [128, 128], BF16) for i in range(4)]

    ps_a = [S.psum(f"ps_a{i}", [128, 512], F32) for i in range(2)]
    ps_hT = S.psum("ps_hT", [128, 1024], BF16)
    ps_m1 = S.psum("ps_m1", [128, 512], F32)
    ps_m2_l = [S.psum(f"ps_m2_{i}", [128, 1024], BF16) for i in range(2)]
    ps_o_l = [S.psum(f"ps_o_{i}", [128, 512], F32) for i in range(2)]

    for hd in range(2):
        S.op("pool", lambda e: e.memset(S32[hd][:, :], 0.0), w=[S32[hd]])
        S.op("pool", lambda e: e.memset(S_bf[hd][:, :], 0.0), w=[S_bf[hd]])

    def load(j):
        sl = j % 2
        S.op("sp", lambda e: e.dma_start(out=x_t[sl][:, :], in_=x[j * 128:(j + 1) * 128, :]), w=[x_t[sl]], dma=x_t[sl])
        S.op("sp", lambda e: e.dma_start(out=y0_t[sl][:, :, :], in_=io["y0src"](j)), r=[io["y0buf"](j)], w=[y0_t[sl]], dma=y0_t[sl])

    load(0)

    def common(j):
        sl = j % 2
        if j + 1 < nblk:
            load(j + 1)
        xt, yt, x1 = x_t[sl], y0_t[sl], x1_t[sl]
        for n in range(2):
            for c in range(8):
                S.op("pe", lambda e: e.matmul(ps_a[n][:, :], lhsT=yt[:, c, :], rhs=wout0_bf[:, c, n * 512:(n + 1) * 512],
                                              start=(c == 0), stop=(c == 7)), r=[yt, wout0_bf], w=[ps_a[n]])
        for n in range(2):
            S.op("dve", lambda e: e.tensor_tensor(out=x1[:, n * 512:(n + 1) * 512], in0=ps_a[n][:, :], in1=xt[:, n * 512:(n + 1) * 512], op=ALU.add),
                 r=[ps_a[n], xt], w=[x1])
        if "x1s_sb" in io:
            S.op("pool", lambda e: e.tensor_copy(out=io["x1s_sb"][:, j, :], in_=x1[:, 0:256]), r=[x1], w=[io["x1s_sb"]])
        else:
            out_toks.append(S.op("pool", lambda e: e.dma_start(out=x1s[j * 128:(j + 1) * 128, :], in_=x1[:, 0:256]), r=[x1], dma=x1))
        S.op("act", lambda e: e.activation(out=junk[:, :], in_=x1[:, :], func=AF.Square, accum_out=ssx[:, 0:1]), r=[x1], w=[junk, ssx])
        rms_rstd(S, ssx, 1024, 128, 1, sstmp)
        S.op("act", lambda e: e.activation(out=h_bf[:, :], in_=x1[:, :], func=AF.Copy, scale=ssx[:, 0:1]), r=[x1, ssx], w=[h_bf])
        for hf in range(2):
            for c in range(4):
                cc = hf * 4 + c
                S.op("pe", lambda e: e.transpose(out=ps_hT[:, c * 128:(c + 1) * 128], in_=h_bf[:, cc * 128:(cc + 1) * 128], identity=ident[:, :]),
                     r=[h_bf, ident], w=[ps_hT])
            S.op("dve", lambda e: e.tensor_copy(out=hT[:, hf * 512:(hf + 1) * 512], in_=ps_hT[:, 0:512]), r=[ps_hT], w=[hT])
        for o in range(8):
            pt = ps_a[o // 4]
            for c in range(8):
                S.op("pe", lambda e: e.matmul(pt[:, (o % 4) * 128:(o % 4 + 1) * 128], lhsT=win1_bf[:, c, o * 128:(o + 1) * 128],
                                              rhs=hT[:, c * 128:(c + 1) * 128], start=(c == 0), stop=(c == 7)), r=[win1_bf, hT], w=[pt])
        for g in range(8):
            dst = vT_l[sl][g - 4] if g in (4, 5) else pre_l[sl][g]
            srcp = ps_a[g // 4][:, (g % 4) * 128:(g % 4 + 1) * 128]
            if g % 2 == 0:
                S.op("act", lambda e: e.activation(out=dst[:, :], in_=srcp, func=AF.Identity, bias=bcol[:, g:g + 1]), r=[ps_a[g // 4], bcol], w=[dst])
            else:
                S.op("dve", lambda e: e.tensor_scalar(out=dst[:, :], in0=srcp, scalar1=bcol[:, g:g + 1], scalar2=None, op0=ALU.add), r=[ps_a[g // 4], bcol], w=[dst])

    if True:
        def head(hd, j, part):
            sl = j % 2
            vT = vT_l[sl]
            pre = pre_l[sl]
            ps_m2, ps_o, osq, oss, on_bf, sstmp_h = ps_m2_l[hd], ps_o_l[hd], osq_l[hd], oss_l[hd], on_bf_l[hd], sstmp_l[hd]
            ma = hd * 256
            ms = hd * 256 + 128
            ps_f = ps_a[0][:, hd * 128:(hd + 1) * 128]
            ps_q = ps_a[0][:, (2 + hd) * 128:(3 + hd) * 128]
            ps_v = ps_a[1][:, hd * 128:(hd + 1) * 128]
            ps_z = ps_a[1][:, (2 + hd) * 128:(3 + hd) * 128]
            T1, FT, LF, KT, QT, SZ, VT, GG, EQ, EK, DEC = t1[hd], fT[hd], lfT[hd], kT[hd], qT[hd], szT_l[sl][hd], vT[hd], G[hd], eq[hd], ek[hd], dec_l[sl][hd]
            qtb, ATb, khk, vtk = qt_bf_l[sl][hd], AT_bf_l[sl][hd], kh_tok_l[sl][hd], v_tok_l[sl][hd]
            T2, T3 = t2[hd], t3[hd]
            if part == 0:
                S.op("act", lambda e: e.activation(out=T1[:, :], in_=pre[hd][:, :], func=AF.Exp, scale=-1.0), r=[pre[hd]], w=[T1])
                S.op("act", lambda e: e.activation(out=T1[:, :], in_=T1[:, :], func=AF.Ln, bias=1.0), r=[T1], w=[T1])
                S.op("act", lambda e: e.activation(out=T1[:, :], in_=T1[:, :], func=AF.Exp, scale=-1.0), r=[T1], w=[T1])
                S.op("dve", lambda e: e.tensor_scalar(out=FT[:, :], in0=T1[:, :], scalar1=oml[:, hd:hd + 1], scalar2=lb[:, hd:hd + 1], op0=ALU.mult, op1=ALU.add),
                     r=[T1, oml, lb], w=[FT])
                S.op("act", lambda e: e.activation(out=LF[:, :], in_=FT[:, :], func=AF.Ln), r=[FT], w=[LF])
                S.op("dve", lambda e: e.tensor_scalar(out=KT[:, :], in0=FT[:, :], scalar1=-1.0, scalar2=1.0, op0=ALU.mult, op1=ALU.add), r=[FT], w=[KT])
                S.op("act", lambda e: e.activation(out=T2[:, :], in_=pre[2 + hd][:, :], func=AF.Exp, scale=-1.0), r=[pre[2 + hd]], w=[T2])
                S.op("act", lambda e: e.activation(out=T2[:, :], in_=T2[:, :], func=AF.Ln, bias=1.0), r=[T2], w=[T2])
                S.op("act", lambda e: e.activation(out=T2[:, :], in_=T2[:, :], func=AF.Exp, scale=-1.0), r=[T2], w=[T2])
                S.op("pool", lambda e: e.tensor_tensor(out=QT[:, :], in0=pre[2 + hd][:, :], in1=T2[:, :], op=ALU.mult), r=[pre[2 + hd], T2], w=[QT])
                S.op("act", lambda e: e.activation(out=T3[:, :], in_=pre[6 + hd][:, :], func=AF.Exp, scale=-1.0), r=[pre[6 + hd]], w=[T3])
                S.op("act", lambda e: e.activation(out=T3[:, :], in_=T3[:, :], func=AF.Ln, bias=1.0), r=[T3], w=[T3])
                S.op("act", lambda e: e.activation(out=T3[:, :], in_=T3[:, :], func=AF.Exp, scale=-1.0), r=[T3], w=[T3])
                S.op("pool", lambda e: e.tensor_tensor(out=SZ[:, :], in0=pre[6 + hd][:, :], in1=T3[:, :], op=ALU.mult), r=[pre[6 + hd], T3], w=[SZ])
                for ch in range(NCH):
                    S.op("pe", lambda e: e.transpose(out=ps_m2[0:CS, 512 + ch * 128:512 + (ch + 1) * 128], in_=VT[:, ch * CS:(ch + 1) * CS], identity=ident[:, :]),
                         r=[VT, ident], w=[ps_m2])
                S.op("act", lambda e: e.copy(out=vtk[:, :], in_=ps_m2[0:CS, 512:512 + NCH * 128]), r=[ps_m2], w=[vtk])
                for ch in range(NCH):
                    S.op("dve", lambda e: e.tensor_tensor_scan(out=GG[:, ch * CS:(ch + 1) * CS], data0=ones[:, 0:CS], data1=LF[:, ch * CS:(ch + 1) * CS],
                                                               initial=0.0, op0=ALU.mult, op1=ALU.add), r=[ones, LF], w=[GG])
                S.op("act", lambda e: e.activation(out=EQ[:, :], in_=GG[:, :], func=AF.Exp), r=[GG], w=[EQ])
                S.op("act", lambda e: e.activation(out=EK[:, :], in_=GG[:, :], func=AF.Exp, scale=-1.0), r=[GG], w=[EK])
                S.op("act", lambda e: e.activation(out=DEC[:, :], in_=GG[:, :].rearrange("p (a b) -> p a b", a=NCH)[:, :, CS - 1], func=AF.Exp), r=[GG], w=[DEC])
                S.op("dve", lambda e: e.tensor_tensor(out=qtb[:, :], in0=QT[:, :], in1=EQ[:, :], op=ALU.mult), r=[QT, EQ], w=[qtb])
                S.op("dve", lambda e: e.tensor_tensor(out=kt_bf[hd][:, :], in0=KT[:, :], in1=EK[:, :], op=ALU.mult), r=[KT, EK], w=[kt_bf[hd]])
                S.op("dve", lambda e: e.tensor_tensor(out=kh_bf[hd][:, :].rearrange("p (a b) -> p a b", a=NCH), in0=kt_bf[hd][:, :].rearrange("p (a b) -> p a b", a=NCH),
                                                      in1=DEC[:, :].unsqueeze(2).to_broadcast([128, NCH, CS]), op=ALU.mult), r=[kt_bf[hd], DEC], w=[kh_bf[hd]])
                for ch in range(NCH):
                    S.op("pe", lambda e: e.matmul(ps_m1[0:CS, ma + ch * CS:ma + (ch + 1) * CS], lhsT=kt_bf[hd][:, ch * CS:(ch + 1) * CS], rhs=qtb[:, ch * CS:(ch + 1) * CS],
                                                  start=True, stop=True), r=[kt_bf[hd], qtb], w=[ps_m1])
                S.op("dve", lambda e: e.tensor_tensor(out=ATb[:, :], in0=ps_m1[0:CS, ma:ma + 128], in1=mask32[:, :], op=ALU.mult), r=[ps_m1, mask32], w=[ATb])
                for ch in range(NCH):
                    S.op("pe", lambda e: e.transpose(out=ps_m2[0:CS, ch * 128:(ch + 1) * 128], in_=kh_bf[hd][:, ch * CS:(ch + 1) * CS], identity=ident[:, :]),
                         r=[kh_bf[hd], ident], w=[ps_m2])
                S.op("act", lambda e: e.copy(out=khk[:, :], in_=ps_m2[0:CS, 0:NCH * 128]), r=[ps_m2], w=[khk])
                return
            for ch in range(NCH):
                S.op("pe", lambda e: e.matmul(ps_o[0:CS, ch * 128:(ch + 1) * 128], lhsT=ATb[:, ch * CS:(ch + 1) * CS], rhs=vtk[:, ch * 128:(ch + 1) * 128],
                                              start=True, stop=False), r=[ATb, vtk], w=[ps_o])
                S.op("pe", lambda e: e.matmul(ps_o[0:CS, ch * 128:(ch + 1) * 128], lhsT=qtb[:, ch * CS:(ch + 1) * CS], rhs=S_bf[hd][:, :],
                                              start=False, stop=True), r=[qtb, S_bf[hd]], w=[ps_o])
                S.op("pe", lambda e: e.matmul(ps_m1[:, ms:ms + 128], lhsT=khk[:, ch * 128:(ch + 1) * 128], rhs=vtk[:, ch * 128:(ch + 1) * 128],
                                              start=True, stop=True), r=[khk, vtk], w=[ps_m1])
                S.op("dve", lambda e: e.scalar_tensor_tensor(out=S32[hd][:, :], in0=S32[hd][:, :], scalar=DEC[:, ch:ch + 1], in1=ps_m1[:, ms:ms + 128],
                                                             op0=ALU.mult, op1=ALU.add), r=[S32[hd], DEC, ps_m1], w=[S32[hd]])
                S.op("act", lambda e: e.copy(out=S_bf[hd][:, :], in_=S32[hd][:, :]), r=[S32[hd]], w=[S_bf[hd]])
            S.op("act", lambda e: e.activation(out=osq[:, :], in_=ps_o[0:CS, 0:NCH * 128], func=AF.Square), r=[ps_o], w=[osq])
            S.op("dve", lambda e: e.tensor_reduce(out=oss[:, :], in_=osq[:, :].rearrange("p (a b) -> p a b", a=NCH), axis=AX.X, op=ALU.add), r=[osq], w=[oss])
            rms_rstd(S, oss, 128, CS, NCH, sstmp_h)
            S.op("dve", lambda e: e.tensor_tensor(out=on_bf[:, :].rearrange("p (a b) -> p a b", a=NCH), in0=ps_o[0:CS, 0:NCH * 128].rearrange("p (a b) -> p a b", a=NCH),
                                                  in1=oss[:, :].unsqueeze(2).to_broadcast([CS, NCH, 128]), op=ALU.mult), r=[ps_o, oss], w=[on_bf])
            for ch in range(NCH):
                S.op("pe", lambda e: e.transpose(out=ps_hT[:, 512 + hd * 128 + ch * CS:512 + hd * 128 + (ch + 1) * CS], in_=on_bf[:, ch * 128:(ch + 1) * 128], identity=ident[0:CS, 0:CS]),
                     r=[on_bf, ident], w=[ps_hT])
            yb = y_bf[(2 * j + hd) % 4]
            S.op("dve", lambda e: e.scalar_tensor_tensor(out=yb[:, :], in0=ps_hT[:, 512 + hd * 128:512 + (hd + 1) * 128], scalar=hn[:, hd:hd + 1], in1=SZ[:, :], op0=ALU.mult, op1=ALU.mult),
                 r=[ps_hT, hn, SZ], w=[yb])
            out_toks.append(S.op("pool", lambda e: e.dma_start(out=io["y1dst"](j, hd), in_=yb[:, :]), r=[yb], dma=yb))

    for i in range(nblk + 2):
        fns = []
        wts = []
        if 2 <= i:
            fns.append(lambda i=i: head(0, i - 2, 1))
            fns.append(lambda i=i: head(1, i - 2, 1))
            wts += [1, 1]
        if 1 <= i <= nblk:
            fns.append(lambda i=i: head(0, i - 1, 0))
            fns.append(lambda i=i: head(1, i - 1, 0))
            wts += [1, 1]
        if i < nblk:
            fns.append(lambda i=i: common(i))
            wts += [3]
        S.parallel(fns, wts)
        if i >= 2 and "after_block" in io:
            io["after_block"](i - 2, out_toks)
    return out_toks


def build_D(S, io, nblk=NB):
    x1s, wout1, outs = io["x1s"], io["wout1"], io["outs"]
    out_toks = []
    stage = [S.sbuf(f"dwstage{i}", [128, 256], F32) for i in range(2)]
    w_bf = S.sbuf("wout1_bf", [128, 8, 256], BF16)
    load_cast_weight(S, wout1, w_bf, 256, stage)
    G = 4
    ng = nblk // G
    y_t = [S.sbuf(f"dy_t{i}", [128, 8, G * 128], BF16) for i in range(2)]
    x_t = [S.sbuf(f"dx_t{i}", [128, G, 256], F32) for i in range(2)]
    o_t = [S.sbuf(f"do_t{i}", [128, G, 256], F32) for i in range(2)]
    ps = [S.psum(f"dps{i}", [128, 512], F32) for i in range(4)]

    def load(g):
        sl = g % 2
        S.op("sp", lambda e: e.dma_start(out=y_t[sl][:, :, :], in_=io["y1src4"](g)), r=[io["y1buf"](g * G)], w=[y_t[sl]], dma=y_t[sl])
        if "x1s_sb" not in io:
            S.op("sp", lambda e: e.dma_start(out=x_t[sl][:, :, :], in_=x1s[g * G * 128:(g + 1) * G * 128, :].rearrange("(b p) c -> p b c", p=128)), w=[x_t[sl]], dma=x_t[sl])

    load(0)
    for g in range(ng):
        sl = g % 2
        if g + 1 < ng:
            load(g + 1)
        for b in range(G):
            pt = ps[b]
            for c in range(8):
                S.op("pe", lambda e: e.matmul(pt[:, 0:256], lhsT=y_t[sl][:, c, b * 128:(b + 1) * 128], rhs=w_bf[:, c, :], start=(c == 0), stop=(c == 7)),
                     r=[y_t[sl], w_bf], w=[pt])
        for b in range(G):
            if "x1s_sb" in io:
                xs = io["x1s_sb"]
                S.op("dve", lambda e: e.tensor_tensor(out=o_t[sl][:, b, :], in0=ps[b][:, 0:256], in1=xs[:, g * G + b, :], op=ALU.add), r=[ps[b], xs], w=[o_t[sl]])
            else:
                S.op("dve", lambda e: e.tensor_tensor(out=o_t[sl][:, b, :], in0=ps[b][:, 0:256], in1=x_t[sl][:, b, :], op=ALU.add), r=[ps[b], x_t[sl]], w=[o_t[sl]])
        out_toks.append(S.op("pool", lambda e: e.dma_start(out=outs[g * G * 128:(g + 1) * G * 128, :].rearrange("(b p) c -> p b c", p=128), in_=o_t[sl][:, :, :]), r=[o_t[sl]], dma=o_t[sl]))
    return out_toks


NCOL_A = 1416
WR = 8
BIGV = 1.0e30
NEGM = 30000.0


def sub(tile, name):
    return tile


def build_A(S, io, nblk=NB, do_m=True, do_n=True):
    nc = S.nc
    x = io["x"]
    out_toks = []

    def cload(name, shape, dtype, src):
        t = S.sbuf(name, shape, dtype)
        idx = tuple(slice(None) for _ in shape)
        S.op("sp", lambda e: e.dma_start(out=t[idx], in_=src[idx]), w=[t], dma=t)
        return t

    ident = cload("a_ident", [128, 128], BF16, io["ident"])
    identf = cload("a_identf", [128, 128], F32, io["identf"])
    normw = cload("a_normw", [128, 8], F32, io["normw0"])
    bfm = cload("a_bfm", [128, 2], F32, io["bfm"])
    btm = cload("a_btm", [128, 1160], F32, io["btm"])
    ones = S.sbuf("a_ones", [128, 128], F32)
    S.op("pool", lambda e: e.memset(ones[:, :], 1.0), w=[ones])
    stage = [S.sbuf(f"a_wstage{i}", [128, NCOL_A], F32) for i in range(2)]
    win_bf = S.sbuf("a_win_bf", [128, 8, NCOL_A], BF16)
    load_cast_weight(S, io["win0"], win_bf, NCOL_A, stage, scale_cols=normw)

    if do_m:
        convw = cload("m_convw", [128, 8], F32, io["convw"])
        convb = cload("m_convb", [128, 2], F32, io["convb"])
        fb = cload("m_fb", [1, 2], F32, io["fbias"])
        hnm = cload("m_hnm", [128, 128], F32, io["hnm"])
        mask128 = cload("m_mask128", [128, 128], F32, io["mask128"])
        nfb = S.sbuf("m_nfb", [1, 2], F32)
        S.op("dve", lambda e: e.tensor_scalar(out=nfb[:, :], in0=fb[:, :], scalar1=-1.0, scalar2=None, op0=ALU.mult), r=[fb], w=[nfb])
        qbuf_l = [S.sbuf(f"m_qbuf{i}", [128, 131], F32) for i in range(2)]
        kbuf_l = [S.sbuf(f"m_kbuf{i}", [128, 131], F32) for i in range(2)]
        for i in range(2):
            S.op("pool", lambda e: e.memset(qbuf_l[i][:, :], 0.0), w=[qbuf_l[i]])
            S.op("pool", lambda e: e.memset(kbuf_l[i][:, :], 0.0), w=[kbuf_l[i]])
        qacc = S.sbuf("m_qacc", [128, 128], F32)
        kacc = S.sbuf("m_kacc", [128, 128], F32)
        qsg = S.sbuf("m_qsg", [128, 128], F32)
        ksg = S.sbuf("m_ksg", [128, 128], F32)
        qT_bf = S.sbuf("m_qT_bf", [128, 128], BF16)
        kT_bf = S.sbuf("m_kT_bf", [128, 128], BF16)
        kt_tok = S.sbuf("m_kt_tok", [128, 128], BF16)
        v1 = S.sbuf("m_v1", [128, 129], BF16)
        S.op("pool", lambda e: e.memset(v1[:, :], 1.0), w=[v1])
        rows = [S.sbuf(f"m_rows{i}", [1, 1280], F32) for i in range(2)]
        for i in range(2):
            S.op("pool", lambda e: e.memset(rows[i][:, :], 0.0), w=[rows[i]])
        mcols = S.sbuf("m_cols", [128, 8], F32)
        ST_bf = S.sbuf("m_ST_bf", [128, 128], BF16)
        Cn32 = S.sbuf("m_Cn32", [128, 129], F32)
        Cn_bf = S.sbuf("m_Cn_bf", [128, 129], BF16)
        S.op("pool", lambda e: e.memset(Cn32[:, :], 0.0), w=[Cn32])
        S.op("pool", lambda e: e.memset(Cn_bf[:, :], 0.0), w=[Cn_bf])
        hs = S.sbuf("m_hs", [128, 128], F32)
        mss = S.sbuf("m_ss", [128, 2], F32)
        mtmp = S.sbuf("m_tmp", [128, 4], F32)
        eo = S.sbuf("m_eo", [128, 128], F32)
        ez = S.sbuf("m_ez", [128, 128], F32)
        gate = S.sbuf("m_gate", [128, 128], F32)
        ya_bf = S.sbuf("m_ya_bf", [128, 128], BF16)
        yaT = [S.sbuf(f"m_yaT{i}", [128, 128], BF16) for i in range(2)]

    if do_n:
        w6 = cload("n_w6", [128, 384], F32, io["w6"])
        kcw = cload("n_kcw", [8, 64], F32, io["kcw"])
        rcos = cload("n_rcos", [128, nblk, 8], F32, io["rcos"])
        rsin = cload("n_rsin", [128, nblk, 8], F32, io["rsin"])
        rnsin = S.sbuf("n_rnsin", [128, nblk, 8], F32)
        S.op("dve", lambda e: e.tensor_scalar(out=rnsin[:, :, :], in0=rsin[:, :, :], scalar1=-1.0, scalar2=None, op0=ALU.mult), r=[rsin], w=[rnsin])
        ccos = cload("n_ccos", [8, nblk, 8], F32, io["ccos"])
        csin = cload("n_csin", [8, nblk, 8], F32, io["csin"])
        bwide = cload("n_bwide", [128, 254], F32, io["bwide"])
        cmask = cload("n_cmask", [128, 16, 128], BF16, io["cmask"])
        causneg2 = cload("n_causneg2", [128, 256], BF16, io["causneg2"])
        anti2 = cload("n_anti2", [128, 256], BF16, io["anti2"])
        w2kv = S.sbuf("n_w2kv", [128, 128], BF16)
        posT = S.sbuf("n_posT", [128, 32], BF16)
        w1kv = [S.sbuf(f"n_w1kv{i}", [128, 16, 128], BF16) for i in range(2)]
        cnt = 0
        for i, nm in enumerate(("w1k", "w1v")):
            wsrc = io[nm].rearrange("(a l d) h -> a d l h", a=2, d=64)
            for m0 in range(0, 16, 8):
                st = stage[cnt % 2]
                cnt += 1
                stv = st[:, 0:1024].rearrange("p (a b) -> p a b", a=8)
                for a in range(2):
                    S.op("sp", lambda e: e.dma_start(out=stv[64 * a:64 * a + 64], in_=wsrc[a, :, m0:m0 + 8, :]), w=[st], dma=st)
                S.op("dve", lambda e: e.tensor_copy(out=w1kv[i][:, m0:m0 + 8, :], in_=stv), r=[st], w=[w1kv[i]])
        w2st = cload("n_w2st", [128, 128], F32, io["w2kv"])
        S.op("dve", lambda e: e.tensor_copy(out=w2kv[:, :], in_=w2st[:, :]), r=[w2st], w=[w2kv])
        posst = cload("n_posst", [128, 32], F32, io["posT"])
        S.op("dve", lambda e: e.tensor_copy(out=posT[:, :], in_=posst[:, :]), r=[posst], w=[posT])
        cbias = S.sbuf("n_cbias", [128, 2], F32)
        KST = S.sbuf("n_KST", [128, nblk * 128], BF16)
        S.op("sp", lambda e: e.dma_start(out=KST[64:128, :], in_=io["kind"][:, :]), w=[KST], dma=KST)
        KWT = S.sbuf("n_KWT", [64, WR * 128], BF16)
        KSTb = [Buf(f"kst{j}") for j in range(nblk)]
        KWTr = [Buf(f"kwt{j}") for j in range(WR)]
        KWTb = [KWTr[j % WR] for j in range(nblk)]
        VS1 = S.sbuf("n_VS1", [128, nblk, 65], BF16)
        VW1 = S.sbuf("n_VW1", [128, WR, 65], BF16)
        VS1b = [Buf(f"vs{j}") for j in range(nblk)]
        VW1r = [Buf(f"vw{j}") for j in range(WR)]
        VW1b = [VW1r[j % WR] for j in range(nblk)]
        S.op("pool", lambda e: e.memset(VS1[:, :, :], 1.0), w=[VS1] + VS1b)
        S.op("pool", lambda e: e.memset(VW1[:, :, :], 1.0), w=[VW1] + VW1r)
        KCT = S.sbuf("n_KCT", [64, 512], BF16)
        VCT = S.sbuf("n_VCT", [64, 512], BF16)
        S.op("pool", lambda e: e.memset(KCT[:, :], 0.0), w=[KCT])
        S.op("pool", lambda e: e.memset(VCT[:, :], 0.0), w=[VCT])
        VC1 = S.sbuf("n_VC1", [128, 4, 193], BF16)
        S.op("pool", lambda e: e.memset(VC1[:, :, :], 1.0), w=[VC1])
        S.op("pool", lambda e: e.memset(VC1[0:1, 0, 64:65], 0.0), w=[VC1])
        ovst = cload("n_ovst", [128, 4, 128], BF16, io["ov"])
        S.op("pool", lambda e: e.tensor_copy(out=VC1[:, :, 65:193], in_=ovst[:, :, :]), r=[ovst], w=[VC1])
        kcv = S.sbuf("n_kcv", [128, 288], BF16)
        S.op("pool", lambda e: e.memset(kcv[:, :], 0.0), w=[kcv])
        qk6 = S.sbuf("n_qk6", [128, 384], F32)
        nsq = S.sbuf("n_sq", [128, 384], F32)
        nss = S.sbuf("n_ss", [128, 8], F32)
        nstmp = S.sbuf("n_stmp", [128, 8], F32)
        rt = S.sbuf("n_rt", [128, 2, 96], F32)
        qk6_bf = S.sbuf("n_qk6_bf", [128, 384], BF16)
        qT_l = [S.sbuf(f"n_qT{i}", [64, 512], BF16) for i in range(2)]
        kcv_bf = S.sbuf("n_kcv_bf", [128, 256], BF16)
        cu = S.sbuf("n_cu", [128, 16], F32)
        cw = S.sbuf("n_cw", [128, 16], F32)
        cg = S.sbuf("n_cg", [128, 16], BF16)
        c2 = S.sbuf("n_c2", [8, 128], F32)
        c2b = S.sbuf("n_c2b", [8, 128], BF16)
        crt = S.sbuf("n_crt", [8, 8, 4], F32)
        Pc = [S.sbuf(f"n_Pc{i}", [128, 512], BF16) for i in range(4)]
        ocs = S.sbuf("n_ocs", [128, 4], F32)
        imp = S.sbuf("n_imp", [128, 128], F32)
        impw = S.sbuf("n_impw", [128, 128], F32)
        m8 = S.sbuf("n_m8", [128, 16], F32)
        negm = S.sbuf("n_negm", [128, 256], BF16)
        S.op("pool", lambda e: e.memset(negm[:, :], 0.0), w=[negm])
        qA_l = [[S.sbuf(f"n_qA{p}_{i}", [128, 256], BF16) for i in range(2)] for p in range(2)]
        Pa = [S.sbuf(f"n_Pa{i}", [128, 512], BF16) for i in range(3)]
        oT = S.sbuf("n_oT", [65, 512], F32)
        OB_l = [S.sbuf(f"n_OB{i}", [128, 6, 65], F32) for i in range(2)]
        c2n = S.sbuf("n_c2n", [8, 64], F32)
        css = S.sbuf("n_css", [8, 2], F32)
        gts_l = [S.sbuf(f"n_gts{i}", [128, 6], F32) for i in range(2)]
        coef = S.sbuf("n_coef", [128, 6], F32)
        sums = S.sbuf("n_sums", [128, 6], F32)
        ez2_l = [S.sbuf(f"n_ez2{i}", [128, 128], F32) for i in range(2)]
        yacc = S.sbuf("n_yacc", [128, 128], F32)
        yb_bf = S.sbuf("n_yb_bf", [128, 128], BF16)
        ybT = [S.sbuf(f"n_ybT{i}", [128, 128], BF16) for i in range(2)]

    x_t = [S.sbuf(f"a_x_t{i}", [128, 1024], F32) for i in range(2)]
    junk = S.sbuf("a_junk", [128, 1024], BF16)
    ssx = S.sbuf("a_ssx", [128, 1], F32)
    sstmp = S.sbuf("a_sstmp", [128, 4], F32)
    h_bf = S.sbuf("a_h_bf", [128, 1024], BF16)
    hT = S.sbuf("a_hT", [128, 1024], BF16)
    tma_l = [S.sbuf(f"a_tma{i}", [128, 512], F32) for i in range(2)]
    tmb = S.sbuf("a_tmb", [128, 512], F32)
    tmc_l = [S.sbuf(f"a_tmc{i}", [128, 136], F32) for i in range(2)]

    ps_hT = S.psum("a_ps_hT", [128, 1024], BF16)
    ps_bt = S.psum("a_ps_bt", [128, 1024], BF16)
    ps_x = S.psum("a_ps_x", [128, 512], F32)
    ps_y = S.psum("a_ps_y", [128, 512], F32)
    ps_f = S.psum("a_ps_f", [128, 512], F32)
    ps_s = [S.psum(f"a_ps_s{i}", [128, 512], F32) for i in range(2)]
    ps_o = S.psum("a_ps_o", [128, 512], F32)
    bt_q = sub(ps_bt, "bt_q")
    bt_kk = sub(ps_bt, "bt_kk")
    bt_kcv = sub(ps_bt, "bt_kcv")
    bt_m = sub(ps_bt, "bt_m")
    f_st = sub(ps_f, "f_st")
    f_num = sub(ps_f, "f_num")
    f_cn = sub(ps_f, "f_cn")
    f_row = sub(ps_f, "f_row")
    f_misc = sub(ps_f, "f_misc")

    if do_n:
        for kv in range(2):
            for m in range(16):
                S.op("pe", lambda e: e.matmul(ps_s[0][:, kv:kv + 1], lhsT=w1kv[kv][:, m, :], rhs=posT[:, 16 * kv + m:16 * kv + m + 1],
                                              start=(m == 0), stop=(m == 15)), r=[w1kv[kv], posT], w=[ps_s[0]])
        S.op("dve", lambda e: e.tensor_copy(out=cbias[:, :], in_=ps_s[0][:, 0:2]), r=[ps_s[0]], w=[cbias])

    def load(j):
        sl = j % 2
        S.op("sp", lambda e: e.dma_start(out=x_t[sl][:, :], in_=x[j * 128:(j + 1) * 128, :]), w=[x_t[sl]], dma=x_t[sl])

    def sigm_inplace(t, shape_ap):
        S.op("dve", lambda e: e.tensor_scalar(out=shape_ap(t), in0=shape_ap(t), scalar1=1.0, scalar2=None, op0=ALU.add), r=[t], w=[t])
        S.op("dve", lambda e: e.reciprocal(out=shape_ap(t), in_=shape_ap(t)), r=[t], w=[t])

    load(0)

    def common(j):
        sl = j % 2
        if j + 1 < nblk:
            load(j + 1)
        xt = x_t[sl]
        tma, tmc = tma_l[sl], tmc_l[sl]
        if do_m:
            qbuf, kbuf = qbuf_l[sl], kbuf_l[sl]
        S.op("act", lambda e: e.activation(out=junk[:, :], in_=xt[:, :], func=AF.Square, accum_out=ssx[:, 0:1]), r=[xt], w=[junk, ssx])
        rms_rstd(S, ssx, 1024, 128, 1, sstmp)
        S.op("act", lambda e: e.activation(out=h_bf[:, :], in_=xt[:, :], func=AF.Copy, scale=ssx[:, 0:1]), r=[xt, ssx], w=[h_bf])
        for hf in range(2):
            for c in range(4):
                cc = hf * 4 + c
                S.op("pe", lambda e: e.transpose(out=ps_hT[:, c * 128:(c + 1) * 128], in_=h_bf[:, cc * 128:(cc + 1) * 128], identity=ident[:, :]),
                     r=[h_bf, ident], w=[ps_hT])
            S.op("dve", lambda e: e.tensor_copy(out=hT[:, hf * 512:(hf + 1) * 512], in_=ps_hT[:, 0:512]), r=[ps_hT], w=[hT])

    def common_proj(j):
        sl = j % 2
        tma, tmc = tma_l[sl], tmc_l[sl]
        if do_m:
            qbuf, kbuf = qbuf_l[sl], kbuf_l[sl]
        for o in range(2):
            for c in range(8):
                S.op("pe", lambda e: e.matmul(ps_x[:, o * 128:(o + 1) * 128], lhsT=win_bf[:, c, o * 128:(o + 1) * 128], rhs=hT[:, c * 128:(c + 1) * 128],
                                              start=(c == 0), stop=(c == 7)), r=[win_bf, hT], w=[ps_x])
        for c in range(8):
            S.op("pe", lambda e: e.matmul(ps_x[:, 256:392], lhsT=hT[:, c * 128:(c + 1) * 128], rhs=win_bf[:, c, 1280:1416],
                                          start=(c == 0), stop=(c == 7)), r=[win_bf, hT], w=[ps_x])
        for c in range(8):
            S.op("pe", lambda e: e.matmul(ps_y[:, :], lhsT=hT[:, c * 128:(c + 1) * 128], rhs=win_bf[:, c, 256:768],
                                          start=(c == 0), stop=(c == 7)), r=[win_bf, hT], w=[ps_y])
        if do_m:
            S.op("act", lambda e: e.activation(out=qbuf[:, 3:131], in_=ps_x[:, 0:128], func=AF.Identity, bias=bfm[:, 0:1]), r=[ps_x, bfm], w=[qbuf])
            S.op("act", lambda e: e.activation(out=kbuf[:, 3:131], in_=ps_x[:, 128:256], func=AF.Identity, bias=bfm[:, 1:2]), r=[ps_x, bfm], w=[kbuf])
        S.op("dve", lambda e: e.tensor_tensor(out=tmc[:, :], in0=ps_x[:, 256:392], in1=btm[:, 1024:1160], op=ALU.add), r=[ps_x, btm], w=[tmc])
        S.op("dve", lambda e: e.tensor_tensor(out=tma[:, :], in0=ps_y[:, :], in1=btm[:, 0:512], op=ALU.add), r=[ps_y, btm], w=[tma])
        if do_n:
            for c in range(8):
                S.op("pe", lambda e: e.matmul(ps_x[:, :], lhsT=hT[:, c * 128:(c + 1) * 128], rhs=win_bf[:, c, 768:1280],
                                              start=(c == 0), stop=(c == 7)), r=[win_bf, hT], w=[ps_x])
            S.op("dve", lambda e: e.tensor_tensor(out=tmb[:, :], in0=ps_x[:, :], in1=btm[:, 512:1024], op=ALU.add), r=[ps_x, btm], w=[tmb])

    def mlstm_block(j):
        if True:
            tma, tmc = tma_l[j % 2], tmc_l[j % 2]
            qbuf, kbuf = qbuf_l[j % 2], kbuf_l[j % 2]
            qbuf_n, kbuf_n = qbuf_l[(j + 1) % 2], kbuf_l[(j + 1) % 2]
            R = rows[j % 2]
            Rp = rows[(j + 1) % 2]
            for (eng, buf, bufn, acc, sg, wofs, bcol_, dst, post) in (("dve", qbuf, qbuf_n, qacc, qsg, 0, 0, qT_bf, 1.0), ("dve", kbuf, kbuf_n, kacc, ksg, 4, 1, kT_bf, 128 ** -0.5)):
                S.op(eng, lambda e: e.tensor_scalar(out=acc[:, :], in0=buf[:, 0:128], scalar1=convw[:, wofs:wofs + 1], scalar2=convb[:, bcol_:bcol_ + 1],
                                                    op0=ALU.mult, op1=ALU.add), r=[buf, convw, convb], w=[acc])
                for i in range(1, 4):
                    S.op(eng, lambda e: e.scalar_tensor_tensor(out=acc[:, :], in0=buf[:, i:i + 128], scalar=convw[:, wofs + i:wofs + i + 1], in1=acc[:, :],
                                                               op0=ALU.mult, op1=ALU.add), r=[buf, convw, acc], w=[acc])
                S.op(eng, lambda e: e.tensor_copy(out=bufn[:, 0:3], in_=buf[:, 128:131]), r=[buf], w=[bufn])
                S.op("act", lambda e: e.activation(out=sg[:, :], in_=acc[:, :], func=AF.Exp, scale=-1.0), r=[acc], w=[sg])
                S.op("act", lambda e: e.activation(out=sg[:, :], in_=sg[:, :], func=AF.Ln, bias=1.0), r=[sg], w=[sg])
                S.op("act", lambda e: e.activation(out=sg[:, :], in_=sg[:, :], func=AF.Exp, scale=-1.0), r=[sg], w=[sg])
                S.op(eng, lambda e: e.scalar_tensor_tensor(out=dst[:, :], in0=acc[:, :], scalar=post, in1=sg[:, :], op0=ALU.mult, op1=ALU.mult),
                     r=[acc, sg], w=[dst])
            S.op("act", lambda e: e.copy(out=v1[:, 0:128], in_=tma[:, 0:128]), r=[tma], w=[v1])
            S.op("pe", lambda e: e.transpose(out=ps_f[0:1, 0:128], in_=tmc[:, 134:135], identity=identf[:, :]), r=[tmc, identf], w=[ps_f])
            S.op("pe", lambda e: e.transpose(out=ps_f[0:1, 128:256], in_=tmc[:, 135:136], identity=identf[:, :]), r=[tmc, identf], w=[ps_f])
            S.op("act", lambda e: e.copy(out=R[:, 1024:1280], in_=ps_f[0:1, 0:256]), r=[ps_f], w=[R])
            S.op("act", lambda e: e.activation(out=R[:, 896:1024], in_=R[:, 1152:1280], func=AF.Exp, scale=-1.0, bias=nfb[:, 0:1]), r=[R, nfb], w=[R])
            S.op("act", lambda e: e.activation(out=R[:, 0:128], in_=R[:, 896:1024], func=AF.Ln, bias=1.0), r=[R], w=[R])
            S.op("dve", lambda e: e.tensor_tensor_scan(out=R[:, 128:256], data0=ones[0:1, 0:128], data1=R[:, 0:128], initial=Rp[:, 255:256],
                                                       op0=ALU.mult, op1=ALU.add), r=[ones, R, Rp], w=[R])
            S.op("dve", lambda e: e.tensor_tensor(out=R[:, 256:384], in0=R[:, 1024:1152], in1=R[:, 128:256], op=ALU.add), r=[R], w=[R])
            S.op("dve", lambda e: e.tensor_tensor_scan(out=R[:, 384:512], data0=ones[0:1, 0:128], data1=R[:, 256:384], initial=Rp[:, 511:512],
                                                       op0=ALU.mult, op1=ALU.max), r=[ones, R, Rp], w=[R])
            S.op("dve", lambda e: e.tensor_scalar(out=R[:, 896:897], in0=Rp[:, 511:512], scalar1=-1.0, scalar2=None, op0=ALU.mult), r=[Rp], w=[R])
            S.op("act", lambda e: e.activation(out=R[:, 512:640], in_=R[:, 256:384], func=AF.Exp, bias=R[:, 896:897]), r=[R], w=[R])
            S.op("act", lambda e: e.activation(out=R[:, 640:768], in_=R[:, 384:512], func=AF.Exp, scale=-1.0, bias=Rp[:, 511:512]), r=[R, Rp], w=[R])
            S.op("dve", lambda e: e.tensor_tensor(out=R[:, 768:896], in0=R[:, 128:256], in1=R[:, 384:512], op=ALU.subtract), r=[R], w=[R])
            S.op("act", lambda e: e.activation(out=R[:, 768:896], in_=R[:, 768:896], func=AF.Exp), r=[R], w=[R])
            for ci, c0 in enumerate((512, 640, 768)):
                S.op("pe", lambda e: e.matmul(ps_f[:, 386 + ci:387 + ci], lhsT=R[:, c0:c0 + 128], rhs=ones[0:1, 0:1], start=True, stop=True),
                     r=[R, ones], w=[f_misc])
            S.op("pe", lambda e: e.matmul(ps_f[:, 389:390], lhsT=ones[0:1, 0:128], rhs=R[:, 767:768], start=True, stop=True), r=[R, ones], w=[f_misc])
            S.op("dve", lambda e: e.tensor_copy(out=mcols[:, 0:4], in_=ps_f[:, 386:390]), r=[f_misc], w=[mcols])
            S.op("dve", lambda e: e.tensor_tensor(out=mcols[:, 4:5], in0=mcols[:, 0:1], in1=mcols[:, 3:4], op=ALU.mult), r=[mcols], w=[mcols])
            S.op("pe", lambda e: e.matmul(ps_f[:, 0:128], lhsT=kT_bf[:, :], rhs=qT_bf[:, :], start=True, stop=True), r=[kT_bf, qT_bf], w=[f_st])
            S.op("dve", lambda e: e.scalar_tensor_tensor(out=ST_bf[:, :], in0=ps_f[:, 0:128], scalar=mcols[:, 0:1], in1=mask128[:, :], op0=ALU.mult, op1=ALU.mult),
                 r=[f_st, mcols, mask128], w=[ST_bf])
            S.op("pe", lambda e: e.matmul(ps_f[:, 128:257], lhsT=ST_bf[:, :], rhs=v1[:, :], start=True, stop=False), r=[ST_bf, v1], w=[f_num])
            S.op("pe", lambda e: e.matmul(ps_f[:, 128:257], lhsT=qT_bf[:, :], rhs=Cn_bf[:, :], start=False, stop=True), r=[qT_bf, Cn_bf], w=[f_num])
            S.op("pe", lambda e: e.transpose(out=ps_hT[:, 768:896], in_=kT_bf[:, :], identity=ident[:, :]), r=[kT_bf, ident], w=[ps_hT])
            S.op("act", lambda e: e.activation(out=kt_tok[:, :], in_=ps_hT[:, 768:896], func=AF.Copy, scale=mcols[:, 4:5]), r=[ps_hT, mcols], w=[kt_tok])
            S.op("pe", lambda e: e.matmul(ps_f[:, 257:386], lhsT=kt_tok[:, :], rhs=v1[:, :], start=True, stop=True), r=[kt_tok, v1], w=[f_cn])
            S.op("dve", lambda e: e.scalar_tensor_tensor(out=Cn32[:, :], in0=Cn32[:, :], scalar=mcols[:, 3:4], in1=ps_f[:, 257:386], op0=ALU.mult, op1=ALU.add),
                 r=[Cn32, mcols, f_cn], w=[Cn32])
            S.op("act", lambda e: e.copy(out=Cn_bf[:, :], in_=Cn32[:, :]), r=[Cn32], w=[Cn_bf])
            S.op("dve", lambda e: e.tensor_tensor(out=mcols[:, 5:6], in0=ps_f[:, 256:257], in1=mcols[:, 1:2], op=ALU.mult), r=[f_num, mcols], w=[mcols])
            S.op("dve", lambda e: e.tensor_scalar(out=mcols[:, 7:8], in0=mcols[:, 5:6], scalar1=-1.0, scalar2=None, op0=ALU.mult), r=[mcols], w=[mcols])
            S.op("dve", lambda e: e.tensor_tensor(out=mcols[:, 5:6], in0=mcols[:, 5:6], in1=mcols[:, 7:8], op=ALU.max), r=[mcols], w=[mcols])
            S.op("dve", lambda e: e.tensor_tensor(out=mcols[:, 5:6], in0=mcols[:, 5:6], in1=mcols[:, 2:3], op=ALU.max), r=[mcols], w=[mcols])
            S.op("dve", lambda e: e.reciprocal(out=mcols[:, 5:6], in_=mcols[:, 5:6]), r=[mcols], w=[mcols])
            S.op("dve", lambda e: e.tensor_tensor(out=mcols[:, 6:7], in0=mcols[:, 5:6], in1=mcols[:, 1:2], op=ALU.mult), r=[mcols], w=[mcols])
            S.op("act", lambda e: e.activation(out=hs[:, :], in_=ps_f[:, 128:256], func=AF.Copy, scale=mcols[:, 6:7]), r=[f_num, mcols], w=[hs])
            S.op("act", lambda e: e.activation(out=junk[:, 0:128], in_=hs[:, :], func=AF.Square, accum_out=mss[:, 0:1]), r=[hs], w=[junk, mss])
            rms_rstd(S, mss, 128, 128, 1, mtmp)
            S.op("act", lambda e: e.activation(out=eo[:, :], in_=tma[:, 128:256], func=AF.Exp, scale=-1.0), r=[tma], w=[eo])
            S.op("act", lambda e: e.activation(out=ez[:, :], in_=tma[:, 256:384], func=AF.Exp, scale=-1.0), r=[tma], w=[ez])
            S.op("act", lambda e: e.activation(out=eo[:, :], in_=eo[:, :], func=AF.Ln, bias=1.0), r=[eo], w=[eo])
            S.op("act", lambda e: e.activation(out=ez[:, :], in_=ez[:, :], func=AF.Ln, bias=1.0), r=[ez], w=[ez])
            S.op("pool", lambda e: e.tensor_tensor(out=ez[:, :], in0=ez[:, :], in1=eo[:, :], op=ALU.add), r=[ez, eo], w=[ez])
            S.op("act", lambda e: e.activation(out=ez[:, :], in_=ez[:, :], func=AF.Exp, scale=-1.0), r=[ez], w=[ez])
            S.op("pool", lambda e: e.tensor_tensor(out=gate[:, :], in0=tma[:, 256:384], in1=hnm[:, :], op=ALU.mult), r=[tma, hnm], w=[gate])
            S.op("pool", lambda e: e.tensor_tensor(out=gate[:, :], in0=gate[:, :], in1=ez[:, :], op=ALU.mult), r=[gate, ez], w=[gate])
            S.op("dve", lambda e: e.scalar_tensor_tensor(out=ya_bf[:, :], in0=hs[:, :], scalar=mss[:, 0:1], in1=gate[:, :], op0=ALU.mult, op1=ALU.mult),
                 r=[hs, mss, gate], w=[ya_bf])
            S.op("pe", lambda e: e.transpose(out=ps_hT[:, 768:896], in_=ya_bf[:, :], identity=ident[:, :]), r=[ya_bf, ident], w=[ps_hT])
            yT = yaT[j % 2]
            S.op("act", lambda e: e.copy(out=yT[:, :], in_=ps_hT[:, 768:896]), r=[ps_hT], w=[yT])
            out_toks.append(S.op("pool", lambda e: e.dma_start(out=io["y0dst"](j, 0), in_=yT[:, :]), r=[yT], dma=yT))

    L = dict(locals())

    def early(i):
        common(i)
        S.flag_wait(("psfree", i))
        common_proj(i)
        S.flag_set(("c", i))
        if do_n:
            nsa_qk(S, i, L)

    def cmp_chain(i):
        S.flag_wait(("c", i))
        nsa_cmp(S, i, L)

    def topk_chain(i):
        nsa_topk(S, i - 1, L, lambda: S.flag_set(("psfree", i)))
        S.flag_set(("psfree", i))

    lag = 2 if do_n else 1
    for i in range(nblk + lag):
        if "bc_preload_step" in io:
            io["bc_preload_step"](i, stage)
        fns = []
        cnts = []
        if do_n and i >= 2:
            fns.append(lambda i=i: nsa_attn(S, i - 2, L))
            cnts.append(60 + 2.5 * i)
        if 1 <= i <= nblk:
            if do_m:
                fns.append(lambda i=i: mlstm_block(i - 1))
                cnts.append(90)
            if do_n:
                fns.append(lambda i=i: topk_chain(i))
                cnts.append(60)
        if not (do_n and 1 <= i <= nblk):
            S.flag_set(("psfree", i))
        if i < nblk:
            fns.append(lambda i=i: early(i))
            cnts.append(100 if do_n else 60)
            if do_n:
                fns.append(lambda i=i: cmp_chain(i))
                cnts.append(65)
        mn = min(cnts)
        wts = [max(1, int(round(AW * c / mn))) for c in cnts] if AW else None
        S.parallel(fns, wts)
        if "after_block" in io and i >= lag:
            io["after_block"](i - lag, out_toks)
    return out_toks


NSA_STOP = 0
NSA_DBG = 0


class _NS:
    def __init__(self, d):
        self.__dict__.update(d)


def nsa_qk(S, j, L):
    V = _NS(L)
    io, nblk = V.io, V.nblk
    tma, tmb, tmc, ident, identf, ones = V.tma_l[j % 2], V.tmb, V.tmc_l[j % 2], V.ident, V.identf, V.ones
    ps_bt, ps_s, ps_x, ps_y, ps_f, ps_o = V.ps_bt, V.ps_s, V.ps_x, V.ps_y, V.ps_f, V.ps_o
    qk6, nsq, nss, nstmp, rt, qk6_bf, w6 = V.qk6, V.nsq, V.nss, V.nstmp, V.rt, V.qk6_bf, V.w6
    qT, qA, OB, gts, ez2 = V.qT_l[j % 2], V.qA_l[j % 2], V.OB_l[j % 2], V.gts_l[j % 2], V.ez2_l[j % 2]
    T0 = j * 128

    def v3(ap, a):
        return ap.rearrange("p (a b) -> p a b", a=a)

    S.op("act", lambda e: e.activation(out=nsq[:, :], in_=tmb[:, 0:384], func=AF.Square), r=[tmb], w=[nsq])
    S.op("dve", lambda e: e.tensor_reduce(out=nss[:, 0:6], in_=v3(nsq[:, :], 6), axis=AX.X, op=ALU.add), r=[nsq], w=[nss])
    S.op("act", lambda e: e.activation(out=nstmp[:, 0:6], in_=nss[:, 0:6], func=AF.Ln, scale=1.0 / 64, bias=EPS), r=[nss], w=[nstmp])
    S.op("act", lambda e: e.activation(out=nss[:, 0:4], in_=nstmp[:, 0:4], func=AF.Exp, scale=-0.5, bias=float(np.log(0.125))), r=[nstmp], w=[nss])
    S.op("act", lambda e: e.activation(out=nss[:, 4:6], in_=nstmp[:, 4:6], func=AF.Exp, scale=-0.5), r=[nstmp], w=[nss])
    S.op("dve", lambda e: e.tensor_tensor(out=v3(qk6[:, :], 6), in0=v3(tmb[:, 0:384], 6), in1=nss[:, 0:6].unsqueeze(2).to_broadcast([128, 6, 64]), op=ALU.mult),
         r=[tmb, nss], w=[qk6])
    S.op("dve", lambda e: e.tensor_tensor(out=qk6[:, :], in0=qk6[:, :], in1=w6[:, :], op=ALU.mult), r=[qk6, w6], w=[qk6])
    S.op("act", lambda e: e.copy(out=qk6_bf[:, :], in_=qk6[:, :]), r=[qk6], w=[qk6_bf])
    x1 = v3(qk6[:, :], 6)[:, :, 0:8]
    x2 = v3(qk6[:, :], 6)[:, :, 8:16]
    cosb = V.rcos[:, j, :].unsqueeze(1).to_broadcast([128, 6, 8])
    sinb = V.rsin[:, j, :].unsqueeze(1).to_broadcast([128, 6, 8])
    r3 = lambda k: rt[:, k, :].rearrange("p (a b) -> p a b", a=6)
    x12 = v3(qk6[:, :], 6)[:, :, 0:16].rearrange("p a (t b) -> p a t b", t=2)
    cos2 = V.rcos[:, j, :].unsqueeze(1).unsqueeze(1).to_broadcast([128, 6, 2, 8])
    ra = rt[:, 0, :].rearrange("p (a t b) -> p a t b", a=6, t=2)
    rb = rt[:, 1, :].rearrange("p (a t b) -> p a t b", a=6, t=2)
    S.op("dve", lambda e: e.tensor_tensor(out=ra, in0=x12, in1=cos2, op=ALU.mult), r=[qk6, V.rcos], w=[rt])
    S.op("pool", lambda e: e.tensor_tensor(out=rb[:, :, 0, :], in0=x2, in1=V.rnsin[:, j, :].unsqueeze(1).to_broadcast([128, 6, 8]), op=ALU.mult), r=[qk6, V.rnsin], w=[rt])
    S.op("pool", lambda e: e.tensor_tensor(out=rb[:, :, 1, :], in0=x1, in1=sinb, op=ALU.mult), r=[qk6, V.rsin], w=[rt])
    S.op("dve", lambda e: e.tensor_tensor(out=v3(qk6_bf[:, :], 6)[:, :, 0:16], in0=rt[:, 0, :].rearrange("p (a c) -> p a c", a=6)[:, :, 0:16],
                                          in1=rt[:, 1, :].rearrange("p (a c) -> p a c", a=6)[:, :, 0:16], op=ALU.add), r=[rt], w=[qk6_bf])
    for h in range(4):
        S.op("pe", lambda e: e.transpose(out=ps_bt[0:64, h * 128:(h + 1) * 128], in_=qk6_bf[:, h * 64:(h + 1) * 64], identity=ident[:, :]),
             r=[qk6_bf, ident], w=[V.bt_q])
    S.op("act", lambda e: e.copy(out=qT[:, :], in_=ps_bt[0:64, 0:512]), r=[V.bt_q], w=[qT])
    S.op("pe", lambda e: e.transpose(out=ps_bt[0:64, 512:640], in_=qk6_bf[:, 256:320], identity=ident[:, :]), r=[qk6_bf, ident], w=[V.bt_kk])
    S.op("pe", lambda e: e.transpose(out=ps_bt[0:64, 640:768], in_=qk6_bf[:, 320:384], identity=ident[:, :]), r=[qk6_bf, ident], w=[V.bt_kk])
    S.op("dve", lambda e: e.tensor_copy(out=V.KST[0:64, T0:T0 + 128], in_=ps_bt[0:64, 512:640]), r=[V.bt_kk], w=[V.KSTb[j]])
    S.op("dve", lambda e: e.tensor_copy(out=V.KWT[:, (j % WR) * 128:(j % WR + 1) * 128], in_=ps_bt[0:64, 640:768]), r=[V.bt_kk], w=[V.KWTb[j]])
    S.op("pool", lambda e: e.tensor_copy(out=V.VS1[:, j, 0:64], in_=tmc[:, 0:64]), r=[tmc], w=[V.VS1b[j]])
    S.op("pool", lambda e: e.tensor_copy(out=V.VW1[:, j % WR, 0:64], in_=tmc[:, 64:128]), r=[tmc], w=[V.VW1b[j]])


def nsa_cmp(S, j, L):
    V = _NS(L)
    io, nblk = V.io, V.nblk
    tma, tmb, tmc, ident, identf, ones = V.tma_l[j % 2], V.tmb, V.tmc_l[j % 2], V.ident, V.identf, V.ones
    ps_bt, ps_s, ps_x, ps_y, ps_f, ps_o = V.ps_bt, V.ps_s, V.ps_x, V.ps_y, V.ps_f, V.ps_o
    qk6, nsq, nss, nstmp, rt, qk6_bf, w6 = V.qk6, V.nsq, V.nss, V.nstmp, V.rt, V.qk6_bf, V.w6
    qT, qA, OB, gts, ez2 = V.qT_l[j % 2], V.qA_l[j % 2], V.OB_l[j % 2], V.gts_l[j % 2], V.ez2_l[j % 2]
    T0 = j * 128

    def v3(ap, a):
        return ap.rearrange("p (a b) -> p a b", a=a)

    kcv, kcv_bf = V.kcv, V.kcv_bf
    for kv in range(2):
        for dup in range(2):
            S.op("act" if dup == 0 else "pool", lambda e: (e.copy if dup == 0 else e.tensor_copy)(out=kcv_bf[:, 128 * kv + 64 * dup:128 * kv + 64 * dup + 64],
                                                                                            in_=tmb[:, 384 + 64 * kv:448 + 64 * kv]), r=[tmb], w=[kcv_bf])
    for kv in range(2):
        S.op("pe", lambda e: e.transpose(out=ps_bt[:, 768 + 128 * kv:896 + 128 * kv], in_=kcv_bf[:, 128 * kv:128 * kv + 128], identity=ident[:, :]),
             r=[kcv_bf, ident], w=[V.bt_kcv])
    for kv in range(2):
        b0 = 144 * kv
        S.op("pool", lambda e: e.tensor_copy(out=kcv[0:64, b0:b0 + 16], in_=kcv[0:64, b0 + 128:b0 + 144]), r=[kcv], w=[kcv])
    for kv in range(2):
        b0 = 144 * kv
        S.op("dve", lambda e: e.tensor_copy(out=kcv[0:64, b0 + 16:b0 + 144], in_=ps_bt[0:64, 768 + 128 * kv:896 + 128 * kv]), r=[V.bt_kcv], w=[kcv])
        S.op("act", lambda e: e.copy(out=kcv[64:128, b0:b0 + 128], in_=ps_bt[64:128, 768 + 128 * kv:896 + 128 * kv]), r=[V.bt_kcv], w=[kcv])
    for kv in range(2):
        kview = kcv[:, 144 * kv:144 * kv + 144].rearrange("p (a b) -> p a b", b=16)
        for m in range(16):
            l = m
            S.op("pe", lambda e: e.matmul(ps_y[:, 8 * kv:8 * kv + 8], lhsT=V.w1kv[kv][:, m, :], rhs=kview[:, (l // 16):(l // 16) + 8, l % 16],
                                          start=(m == 0), stop=(m == 15)), r=[V.w1kv[kv], kcv], w=[ps_y])
    cu, cw, cg, c2, c2b, c2n, css = V.cu, V.cw, V.cg, V.c2, V.c2b, V.c2n, V.css
    for kv in range(2):
        S.op("act", lambda e: e.activation(out=cu[:, 8 * kv:8 * kv + 8], in_=ps_y[:, 8 * kv:8 * kv + 8], func=AF.Identity, bias=V.cbias[:, kv:kv + 1]),
             r=[ps_y, V.cbias], w=[cu])
    S.op("dve", lambda e: e.tensor_tensor(out=cw[:, :], in0=cu[:, :], in1=cu[:, :], op=ALU.mult), r=[cu], w=[cw])
    S.op("dve", lambda e: e.tensor_scalar(out=cw[:, :], in0=cw[:, :], scalar1=0.044715, scalar2=1.0, op0=ALU.mult, op1=ALU.add), r=[cw], w=[cw])
    S.op("dve", lambda e: e.tensor_tensor(out=cw[:, :], in0=cw[:, :], in1=cu[:, :], op=ALU.mult), r=[cw, cu], w=[cw])
    S.op("act", lambda e: e.activation(out=cw[:, :], in_=cw[:, :], func=AF.Exp, scale=-2.0 * 0.7978845608028654), r=[cw], w=[cw])
    S.op("dve", lambda e: e.tensor_scalar(out=cw[:, :], in0=cw[:, :], scalar1=1.0, scalar2=None, op0=ALU.add), r=[cw], w=[cw])
    S.op("dve", lambda e: e.reciprocal(out=cw[:, :], in_=cw[:, :]), r=[cw], w=[cw])
    S.op("dve", lambda e: e.tensor_tensor(out=cg[:, :], in0=cu[:, :], in1=cw[:, :], op=ALU.mult), r=[cu, cw], w=[cg])
    for kv in range(2):
        S.op("pe", lambda e: e.matmul(ps_y[0:8, 16 + 64 * kv:16 + 64 * kv + 64], lhsT=cg[:, 8 * kv:8 * kv + 8], rhs=V.w2kv[:, 64 * kv:64 * kv + 64], start=True, stop=True),
             r=[cg, V.w2kv], w=[ps_y])
    S.op("act", lambda e: e.copy(out=c2[:, :], in_=ps_y[0:8, 16:144]), r=[ps_y], w=[c2])
    S.op("act", lambda e: e.activation(out=c2n[:, :], in_=c2[:, 0:64], func=AF.Square, accum_out=css[:, 0:1]), r=[c2], w=[c2n, css])
    S.op("act", lambda e: e.activation(out=css[:, 1:2], in_=css[:, 0:1], func=AF.Ln, scale=1.0 / 64, bias=EPS), r=[css], w=[css])
    S.op("act", lambda e: e.activation(out=css[:, 0:1], in_=css[:, 1:2], func=AF.Exp, scale=-0.5), r=[css], w=[css])
    S.op("dve", lambda e: e.scalar_tensor_tensor(out=c2n[:, :], in0=c2[:, 0:64], scalar=css[:, 0:1], in1=V.kcw[:, :], op0=ALU.mult, op1=ALU.mult),
         r=[c2, css, V.kcw], w=[c2n])
    S.op("act", lambda e: e.copy(out=c2b[:, 0:64], in_=c2n[:, :]), r=[c2n], w=[c2b])
    S.op("act", lambda e: e.copy(out=c2b[:, 64:128], in_=c2[:, 64:128]), r=[c2], w=[c2b])
    crt = V.crt
    cc, cs_ = V.ccos[:, j, :], V.csin[:, j, :]
    S.op("dve", lambda e: e.tensor_tensor(out=crt[:, :, 0], in0=c2n[:, 0:8], in1=cc, op=ALU.mult), r=[c2n, V.ccos], w=[crt])
    S.op("dve", lambda e: e.tensor_tensor(out=crt[:, :, 1], in0=c2n[:, 8:16], in1=cs_, op=ALU.mult), r=[c2n, V.csin], w=[crt])
    S.op("dve", lambda e: e.tensor_tensor(out=crt[:, :, 2], in0=c2n[:, 0:8], in1=cs_, op=ALU.mult), r=[c2n, V.csin], w=[crt])
    S.op("dve", lambda e: e.tensor_tensor(out=crt[:, :, 3], in0=c2n[:, 8:16], in1=cc, op=ALU.mult), r=[c2n, V.ccos], w=[crt])
    S.op("dve", lambda e: e.tensor_tensor(out=c2b[:, 0:8], in0=crt[:, :, 0], in1=crt[:, :, 1], op=ALU.subtract), r=[crt], w=[c2b])
    S.op("dve", lambda e: e.tensor_tensor(out=c2b[:, 8:16], in0=crt[:, :, 2], in1=crt[:, :, 3], op=ALU.add), r=[crt], w=[c2b])
    for kv in range(2):
        S.op("pe", lambda e: e.transpose(out=ps_bt[0:64, 768 + 8 * kv:776 + 8 * kv], in_=c2b[:, 64 * kv:64 * kv + 64], identity=ident[0:8, 0:8]),
             r=[c2b, ident], w=[V.bt_kcv])
    S.op("dve", lambda e: e.tensor_copy(out=V.KCT[:, 8 * j:8 * j + 8], in_=ps_bt[0:64, 768:776]), r=[V.bt_kcv], w=[V.KCT])
    S.op("dve", lambda e: e.tensor_copy(out=V.VCT[:, 8 * j:8 * j + 8], in_=ps_bt[0:64, 776:784]), r=[V.bt_kcv], w=[V.VCT])
    if j == 0:
        S.op("pool", lambda e: e.memset(V.KCT[:, 0:1], 0.0), w=[V.KCT])
        S.op("pool", lambda e: e.memset(V.VCT[:, 0:1], 0.0), w=[V.VCT])
    ktl = j // 16
    S.op("pe", lambda e: e.transpose(out=ps_bt[:, 784:848], in_=V.VCT[:, ktl * 128:(ktl + 1) * 128], identity=ident[0:64, 0:64]), r=[V.VCT, ident], w=[V.bt_kcv])
    S.op("act", lambda e: e.copy(out=V.VC1[:, ktl, 0:64], in_=ps_bt[:, 784:848]), r=[V.bt_kcv], w=[V.VC1])


def nsa_topk(S, j, L, on_ps_free=None):
    V = _NS(L)
    io, nblk = V.io, V.nblk
    tma, tmb, tmc, ident, identf, ones = V.tma_l[j % 2], V.tmb, V.tmc_l[j % 2], V.ident, V.identf, V.ones
    ps_bt, ps_s, ps_x, ps_y, ps_f, ps_o = V.ps_bt, V.ps_s, V.ps_x, V.ps_y, V.ps_f, V.ps_o
    qk6, nsq, nss, nstmp, rt, qk6_bf, w6 = V.qk6, V.nsq, V.nss, V.nstmp, V.rt, V.qk6_bf, V.w6
    qT, qA, OB, gts, ez2 = V.qT_l[j % 2], V.qA_l[j % 2], V.OB_l[j % 2], V.gts_l[j % 2], V.ez2_l[j % 2]
    T0 = j * 128

    def v3(ap, a):
        return ap.rearrange("p (a b) -> p a b", a=a)

    ktl = j // 16
    nkt = ktl + 1
    Pc = V.Pc
    for kt in range(nkt):
        pt = ps_x
        S.op("pe", lambda e: e.matmul(pt[:, 0:512], lhsT=V.KCT[:, kt * 128:(kt + 1) * 128], rhs=qT[:, 0:512], start=True, stop=True), r=[V.KCT, qT], w=[pt])
        S.op("act", lambda e: e.activation(out=Pc[kt][:, :], in_=pt[:, 0:512], func=AF.Exp), r=[pt], w=[Pc[kt]])
        if kt == nkt - 1:
            S.op("dve", lambda e: e.tensor_tensor(out=v3(Pc[kt][:, :], 4), in0=v3(Pc[kt][:, :], 4),
                                                  in1=V.cmask[:, j % 16, :].unsqueeze(1).to_broadcast([128, 4, 128]), op=ALU.mult), r=[Pc[kt], V.cmask], w=[Pc[kt]])
    ocs, imp, impw, m8 = V.ocs, V.imp, V.impw, V.m8
    for h in range(4):
        bank = ps_x if h < 2 else ps_y
        c0 = (h % 2) * 193
        for kt in range(nkt):
            S.op("pe", lambda e: e.matmul(bank[:, c0:c0 + 193], lhsT=Pc[kt][:, h * 128:(h + 1) * 128], rhs=V.VC1[:, kt, :], start=(kt == 0), stop=(kt == nkt - 1)),
                 r=[Pc[kt], V.VC1], w=[bank])
    for h in range(4):
        bank = ps_x if h < 2 else ps_y
        c0 = (h % 2) * 193
        S.op("dve", lambda e: e.tensor_scalar(out=ocs[:, h:h + 1], in0=bank[:, c0 + 64:c0 + 65], scalar1=1e-30, scalar2=None, op0=ALU.max), r=[bank], w=[ocs])
    S.op("dve", lambda e: e.reciprocal(out=ocs[:, :], in_=ocs[:, :]), r=[ocs], w=[ocs])
    for h in range(4):
        bank = ps_x if h < 2 else ps_y
        c0 = (h % 2) * 193
        if h == 0:
            S.op("dve", lambda e: e.tensor_scalar(out=imp[:, :], in0=bank[:, c0 + 65:c0 + 193], scalar1=ocs[:, 0:1], scalar2=None, op0=ALU.mult), r=[bank, ocs], w=[imp])
        else:
            S.op("dve", lambda e: e.scalar_tensor_tensor(out=imp[:, :], in0=bank[:, c0 + 65:c0 + 193], scalar=ocs[:, h:h + 1], in1=imp[:, :], op0=ALU.mult, op1=ALU.add),
                 r=[bank, ocs, imp], w=[imp])
        if h < 2:
            S.op("act", lambda e: e.copy(out=OB[:, 3 * h, :], in_=bank[:, c0:c0 + 65]), r=[bank], w=[OB])
    if on_ps_free is not None:
        on_ps_free()
    S.op("dve", lambda e: e.tensor_tensor(out=impw[:, :], in0=imp[:, :], in1=V.bwide[:, 126 - 2 * j:254 - 2 * j], op=ALU.add), r=[imp, V.bwide], w=[impw])
    S.op("dve", lambda e: e.memset(impw[:, 0:1], BIGV), w=[impw])
    S.op("dve", lambda e: e.max(out=m8[:, 0:8], in_=impw[:, :]), r=[impw], w=[m8])
    S.op("dve", lambda e: e.match_replace(out=imp[:, :], in_to_replace=m8[:, 0:8], in_values=impw[:, :], imm_value=-3.0e38), r=[m8, impw], w=[imp])
    S.op("dve", lambda e: e.max(out=m8[:, 8:16], in_=imp[:, :]), r=[imp], w=[m8])
    S.op("dve", lambda e: e.tensor_scalar(out=imp[:, :], in0=impw[:, :], scalar1=m8[:, 15:16], scalar2=None, op0=ALU.is_ge), r=[impw, m8], w=[imp])
    S.op("dve", lambda e: e.tensor_scalar(out=V.negm[:, :].rearrange("p (a b) -> p a b", a=2)[:, :, 64:128], in0=imp[:, :].rearrange("p (a b) -> p a b", a=2),
                                          scalar1=NEGM, scalar2=-NEGM, op0=ALU.mult, op1=ALU.add), r=[imp], w=[V.negm])
    nhalf = 2 if j >= 32 else 1
    for hf in range(nhalf):
        S.op("pe", lambda e: e.transpose(out=V.ps_hT[:, 512 + hf * 128:512 + (hf + 1) * 128], in_=V.negm[:, hf * 128:(hf + 1) * 128], identity=ident[:, :]), r=[V.negm, ident], w=[V.ps_hT])
    for hf in range(nhalf):
        S.op("act", lambda e: e.copy(out=qA[hf][64:128, 0:128], in_=V.ps_hT[64:128, 512 + hf * 128:512 + (hf + 1) * 128]), r=[V.ps_hT], w=[qA[hf]])
        S.op("dve", lambda e: e.tensor_copy(out=qA[hf][64:128, 128:256], in_=V.ps_hT[64:128, 512 + hf * 128:512 + (hf + 1) * 128]), r=[V.ps_hT], w=[qA[hf]])
        S.op("pool", lambda e: e.tensor_copy(out=qA[hf][0:64, :], in_=qT[:, 0:256]), r=[qT], w=[qA[hf]])
    S.op("act", lambda e: e.activation(out=gts[:, :], in_=tmc[:, 128:134], func=AF.Exp, scale=-1.0), r=[tmc], w=[gts])
    S.op("dve", lambda e: e.tensor_scalar(out=gts[:, :], in0=gts[:, :], scalar1=1.0, scalar2=None, op0=ALU.add), r=[gts], w=[gts])
    S.op("act", lambda e: e.activation(out=ez2[:, :], in_=tma[:, 384:512], func=AF.Exp, scale=-1.0), r=[tma], w=[ez2])
    S.op("act", lambda e: e.activation(out=ez2[:, :], in_=ez2[:, :], func=AF.Ln, bias=1.0), r=[ez2], w=[ez2])
    S.op("act", lambda e: e.activation(out=ez2[:, :], in_=ez2[:, :], func=AF.Exp, scale=-1.0), r=[ez2], w=[ez2])
    S.op("pool", lambda e: e.tensor_tensor(out=ez2[:, :], in0=ez2[:, :], in1=tma[:, 384:512], op=ALU.mult), r=[ez2, tma], w=[ez2])


def nsa_attn(S, j, L):
    V = _NS(L)
    io, nblk = V.io, V.nblk
    ident, identf = V.ident, V.identf
    ps_s, ps_o = V.ps_s, V.ps_o
    qT, qA, OB, gts, ez2 = V.qT_l[j % 2], V.qA_l[j % 2], V.OB_l[j % 2], V.gts_l[j % 2], V.ez2_l[j % 2]
    T0 = j * 128
    Pa, oT = V.Pa, V.oT
    kts = list(range(max(0, j - 4), j + 1))

    def w_score(idx):
        kt = kts[idx]
        pt = ps_s[idx % 2]
        caus = (kt == j)
        anti = (kt == j - 4)
        S.op("pe", lambda e: e.matmul(pt[:, 0:256], lhsT=V.KWT[:, (kt % WR) * 128:(kt % WR + 1) * 128], rhs=qA[0][0:64, :], start=True, stop=not (caus or anti)),
             r=[V.KWTb[kt], qA[0]], w=[pt])
        if caus:
            S.op("pe", lambda e: e.matmul(pt[:, 0:256], lhsT=ident[:, :], rhs=V.causneg2[:, :], start=False, stop=True), r=[ident, V.causneg2], w=[pt])
        if anti:
            S.op("pe", lambda e: e.matmul(pt[:, 0:256], lhsT=ident[:, :], rhs=V.anti2[:, :], start=False, stop=True), r=[ident, V.anti2], w=[pt])

    w_score(0)
    for idx, kt in enumerate(kts):
        pt = ps_s[idx % 2]
        if idx + 1 < len(kts):
            w_score(idx + 1)
        P = Pa[idx % 3]
        S.op("act", lambda e: e.activation(out=P[:, 0:256], in_=pt[:, 0:256], func=AF.Exp), r=[pt], w=[P])
        S.op("pe", lambda e: e.matmul(ps_o[0:65, 256:512], lhsT=V.VW1[:, kt % WR, :], rhs=P[:, 0:256], start=(idx == 0), stop=(idx == len(kts) - 1)),
             r=[V.VW1b[kt], P], w=[ps_o])
    S.op("act", lambda e: e.copy(out=oT[:, 256:512], in_=ps_o[0:65, 256:512]), r=[ps_o], w=[oT])
    groups = [list(range(g0, min(g0 + 2, j + 1))) for g0 in range(0, j + 1, 2)]

    def s_score(gi):
        pt = ps_s[gi % 2]
        for ti, kt in enumerate(groups[gi]):
            S.op("pe", lambda e: e.matmul(pt[:, ti * 256:(ti + 1) * 256], lhsT=V.KST[:, kt * 128:(kt + 1) * 128], rhs=qA[kt // 32][:, :], start=True, stop=(kt != j)),
                 r=[V.KSTb[kt], V.KST, qA[kt // 32]], w=[pt])
            if kt == j:
                S.op("pe", lambda e: e.matmul(pt[:, ti * 256:(ti + 1) * 256], lhsT=ident[:, :], rhs=V.causneg2[:, :], start=False, stop=True), r=[ident, V.causneg2], w=[pt])

    s_score(0)
    for gi, grp in enumerate(groups):
        pt = ps_s[gi % 2]
        if gi + 1 < len(groups):
            s_score(gi + 1)
        P = Pa[gi % 3]
        w_ = 256 * len(grp)
        S.op("act", lambda e: e.activation(out=P[:, 0:w_], in_=pt[:, 0:w_], func=AF.Exp), r=[pt], w=[P])
        for ti, kt in enumerate(grp):
            S.op("pe", lambda e: e.matmul(ps_o[0:65, 0:256], lhsT=V.VS1[:, kt, :], rhs=P[:, ti * 256:(ti + 1) * 256], start=(kt == 0), stop=(kt == j)), r=[V.VS1b[kt], P], w=[ps_o])
    S.op("act", lambda e: e.copy(out=oT[:, 0:256], in_=ps_o[0:65, 0:256]), r=[ps_o], w=[oT])
    for br in (1, 2):
        for h in range(2):
            src0 = (br - 1) * 256 + h * 128
            pos = (3 * h + br) * 65
            S.op("pe", lambda e: e.transpose(out=ps_s[0][:, pos:pos + 65], in_=oT[0:65, src0:src0 + 128], identity=identf[0:65, 0:65]), r=[oT, identf], w=[ps_s[0]])
    for h in range(2):
        S.op("dve", lambda e: e.tensor_copy(out=OB[:, 3 * h + 1:3 * h + 3, :], in_=ps_s[0][:, (3 * h + 1) * 65:(3 * h + 3) * 65].rearrange("p (a b) -> p a b", a=2)),
             r=[ps_s[0]], w=[OB])
    coef = V.coef
    S.op("dve", lambda e: e.tensor_scalar(out=coef[:, :], in0=OB[:, :, 64], scalar1=1e-30, scalar2=None, op0=ALU.max), r=[OB], w=[coef])
    S.op("dve", lambda e: e.tensor_tensor(out=coef[:, :], in0=coef[:, :], in1=gts[:, :], op=ALU.mult), r=[coef, gts], w=[coef])
    S.op("dve", lambda e: e.reciprocal(out=coef[:, :], in_=coef[:, :]), r=[coef], w=[coef])
    yacc = V.yacc
    for h in range(2):
        ys = yacc[:, h * 64:(h + 1) * 64]
        S.op("dve", lambda e: e.tensor_scalar(out=ys, in0=OB[:, 3 * h, 0:64], scalar1=coef[:, 3 * h:3 * h + 1], scalar2=None, op0=ALU.mult), r=[OB, coef], w=[yacc])
        for br in (1, 2):
            S.op("dve", lambda e: e.scalar_tensor_tensor(out=ys, in0=OB[:, 3 * h + br, 0:64], scalar=coef[:, 3 * h + br:3 * h + br + 1], in1=ys, op0=ALU.mult, op1=ALU.add),
                 r=[OB, coef, yacc], w=[yacc])
    S.op("dve", lambda e: e.tensor_tensor(out=V.yb_bf[:, :], in0=yacc[:, :], in1=ez2[:, :], op=ALU.mult), r=[yacc, ez2], w=[V.yb_bf])
    S.op("pe", lambda e: e.transpose(out=V.ps_hT[:, 896:1024], in_=V.yb_bf[:, :], identity=ident[:, :]), r=[V.yb_bf, ident], w=[V.ps_hT])
    yT = V.ybT[j % 2]
    S.op("act", lambda e: e.copy(out=yT[:, :], in_=V.ps_hT[:, 896:1024]), r=[V.ps_hT], w=[yT])
    V.out_toks.append(S.op("pool", lambda e: e.dma_start(out=io["y0dst"](j, 1), in_=yT[:, :]), r=[yT], dma=yT))


MLSTM_COLS = 2568


def consts_A(nblk=NB):
    c = {}
    c["ident"] = np.eye(128, dtype=np.float32).astype(NPBF)
    c["identf"] = np.eye(128, dtype=np.float32)
    s = np.arange(128)[:, None]
    t = np.arange(128)[None, :]
    c["mask128"] = (s <= t).astype(np.float32)
    cn = np.where(s > t, -NEGM, 0.0).astype(np.float32)
    an = np.where(s <= t, -NEGM, 0.0).astype(np.float32)
    c["causneg2"] = np.ascontiguousarray(np.tile(cn, (1, 2))).astype(NPBF)
    c["anti2"] = np.ascontiguousarray(np.tile(an, (1, 2))).astype(NPBF)
    half = 8
    inv_freq = (np.float32(500000.0) ** (-np.arange(half, dtype=np.float32) / np.float32(half))).astype(np.float32)
    pos = (np.arange(nblk)[None, :] * 128 + np.arange(128)[:, None]).astype(np.float32)
    ang = pos[:, :, None] * inv_freq[None, None, :]
    c["rcos"] = np.cos(ang).astype(np.float32)
    c["rsin"] = np.sin(ang).astype(np.float32)
    n = (8 * np.arange(nblk)[None, :] - 1 + np.arange(8)[:, None])
    cpos = (16 * n + 31).astype(np.float32)
    cang = cpos[:, :, None] * inv_freq[None, None, :]
    c["ccos"] = np.cos(cang).astype(np.float32)
    c["csin"] = np.sin(cang).astype(np.float32)
    u = np.arange(254)[None, :]
    hi = (np.arange(128)[:, None] >= 64).astype(np.int64)
    d = u - 126 - hi
    c["bwide"] = (BIGV * ((d == 0) | (d == -1)) - BIGV * (d > 0)).astype(np.float32)
    cl = np.arange(128)[:, None, None]
    jj = np.arange(16)[None, :, None]
    tl = np.arange(128)[None, None, :]
    c["cmask"] = ((cl <= 8 * jj + 7) & (16 * (cl - 8 * jj) + 15 <= tl)).astype(np.float32).astype(NPBF)
    key = np.arange(nblk * 128)[None, :]
    rr = np.arange(64)[:, None]
    c["kind"] = (rr == ((key // 64) % 64)).astype(np.float32).astype(NPBF)
    col = np.arange(512)
    nn = col - 1
    m = np.arange(128)
    ov = np.clip(np.minimum(16 * nn[:, None] + 32, 64 * m[None, :] + 64) - np.maximum(16 * nn[:, None], 64 * m[None, :]), 0, None).astype(np.float32)
    ov[0, :] = 0.0
    c["ov"] = np.ascontiguousarray(ov.reshape(4, 128, 128).transpose(1, 0, 2)).astype(NPBF)
    return c


def cols_A(s):
    hm = s
    g = s // 2
    ha = 4 * g + 2 * (s % 2)
    hb = ha + 1
    oth = [h for h in range(4 * g, 4 * g + 4) if h not in (ha, hb)]
    N0 = MLSTM_COLS
    ar = np.arange
    fm = [ar(hm * 128, hm * 128 + 128), ar(512 + hm * 128, 512 + hm * 128 + 128)]
    tma = [ar(1024 + hm * 128, 1024 + hm * 128 + 128), ar(1536 + hm * 128, 1536 + hm * 128 + 128), ar(2048 + hm * 128, 2048 + hm * 128 + 128),
           ar(N0 + 1304 + ha * 64, N0 + 1304 + ha * 64 + 128)]
    tmb = [ar(N0 + h * 64, N0 + h * 64 + 64) for h in (ha, hb, oth[0], oth[1])]
    tmb += [ar(N0 + base + g * 64, N0 + base + g * 64 + 64) for base in (768, 1024, 512, 640)]
    tmc = [ar(N0 + 896 + g * 64, N0 + 896 + g * 64 + 64), ar(N0 + 1152 + g * 64, N0 + 1152 + g * 64 + 64),
           ar(N0 + 1280 + ha * 3, N0 + 1280 + ha * 3 + 6), ar(2560 + hm, 2561 + hm), ar(2564 + hm, 2565 + hm)]
    return np.concatenate(fm), np.concatenate(tma + tmb + tmc)


def inputs_A(inp, nblk=NB):
    T = nblk * 128
    cs = consts_A(nblk)
    maps = []
    for core in range(8):
        b, s = core // 4, core % 4
        g = s // 2
        fm, tm = cols_A(s)
        cols = np.concatenate([fm, tm[0:512], tm[512:1024], tm[1024:1160]])
        W = inp["ev_w_in"][0][:, cols]
        bias = inp["ev_b_in"][0]
        m = dict(cs)
        m["x"] = np.ascontiguousarray(inp["x"][b, :T])
        m["win0"] = np.ascontiguousarray(W)
        m["normw0"] = colT(inp["norm_w"][0], 8)
        m["bfm"] = colT(bias[fm], 2)
        m["btm"] = np.ascontiguousarray(np.broadcast_to(bias[tm][None, :], (128, 1160))).astype(np.float32)
        cw = inp["mlstm_conv_w"][0]
        m["convw"] = np.ascontiguousarray(np.concatenate([cw[:, s * 128:(s + 1) * 128].T, cw[:, 512 + s * 128:512 + (s + 1) * 128].T], axis=1)).astype(np.float32)
        cb = inp["mlstm_conv_b"][0]
        m["convb"] = np.ascontiguousarray(np.stack([cb[s * 128:(s + 1) * 128], cb[512 + s * 128:512 + (s + 1) * 128]], axis=1)).astype(np.float32)
        m["fbias"] = np.array([[inp["mlstm_f_bias"][0][s], 0.0]], dtype=np.float32)
        m["hnm"] = np.ascontiguousarray(np.broadcast_to(inp["mlstm_head_norm"][0][s * 128:(s + 1) * 128][None, :], (128, 128))).astype(np.float32)
        qn = inp["nsa_q_norm"][0]
        kn = inp["nsa_k_norm"][0]
        w6 = np.concatenate([qn, qn, qn, qn, kn[1], kn[2]])
        m["w6"] = np.ascontiguousarray(np.broadcast_to(w6[None, :], (128, 384))).astype(np.float32)
        m["kcw"] = np.ascontiguousarray(np.broadcast_to(kn[0][None, :], (8, 64))).astype(np.float32)
        m["w1k"] = np.ascontiguousarray(inp["cmp_k_w1"][0])
        m["w1v"] = np.ascontiguousarray(inp["cmp_v_w1"][0])
        m["w2kv"] = np.ascontiguousarray(np.concatenate([inp["cmp_k_w2"][0], inp["cmp_v_w2"][0]], axis=1))
        pk, pv = inp["cmp_k_pos"][0], inp["cmp_v_pos"][0]
        m["posT"] = np.ascontiguousarray(np.concatenate([np.concatenate([p_[0:16].T, p_[16:32].T], axis=0) for p_ in (pk, pv)], axis=1)).astype(np.float32)
        maps.append(m)
    return maps


A_IN_SPECS = [
    ("ident", [128, 128], BF16), ("identf", [128, 128], F32), ("mask128", [128, 128], F32), ("causneg2", [128, 256], BF16), ("anti2", [128, 256], BF16),
    ("rcos", [128, None, 8], F32), ("rsin", [128, None, 8], F32), ("ccos", [8, None, 8], F32), ("csin", [8, None, 8], F32), ("bwide", [128, 254], F32),
    ("cmask", [128, 16, 128], BF16), ("kind", [64, "T", ], BF16), ("ov", [128, 4, 128], BF16),
    ("win0", [1024, NCOL_A], F32), ("normw0", [128, 8], F32), ("bfm", [128, 2], F32), ("btm", [128, 1160], F32), ("convw", [128, 8], F32),
    ("convb", [128, 2], F32), ("fbias", [1, 2], F32), ("hnm", [128, 128], F32), ("w6", [128, 384], F32), ("kcw", [8, 64], F32),
    ("w1k", [2048, 128], F32), ("w1v", [2048, 128], F32), ("w2kv", [128, 128], F32), ("posT", [128, 32], F32),
]


BC_IN_SPECS = [
    ("xr", [SEQ, 1024], F32), ("wout0", [1024, 1024], F32), ("normw1", [128, 8], F32), ("win1", [1024, 1024], F32),
    ("bcol1", [128, 8], F32), ("lbl", [128, 4], F32), ("hn1", [128, 2], F32), ("mask32", [CS, 128], F32),
]
NQ = 4
QB = NB // NQ
QT = QB * 128
GROUPS = [[0, 1, 2, 3], [4, 5, 6, 7]]


def make_nc_fused():
    nblk = NB
    nc = bass.Bass("TRN2", target_bir_lowering=False)
    S = Sched(nc)
    io = {"x": dram_in(nc, "x", [SEQ, 1024], F32)}
    for (name, shape, dt_) in A_IN_SPECS:
        shape = [nblk if d is None else (nblk * 128 if d == "T" else d) for d in shape]
        io[name] = dram_in(nc, name, shape, dt_)
    for (name, shape, dt_) in BC_IN_SPECS:
        io[name] = dram_in(nc, name, shape, dt_)
    io["wout1"] = dram_in(nc, "wout1", [1024, 256], F32)
    io["outs"] = dram_out(nc, "outs", [SEQ, 256], F32)
    y0loc = [nc.dram_tensor(f"y0loc{q}", [256, QT], BF16) for q in range(NQ)]
    y0all = [nc.dram_tensor(f"y0all{q}", [1024, QT], BF16) for q in range(NQ)]
    y1loc = [nc.dram_tensor(f"y1loc{q}", [256, QT], BF16) for q in range(NQ)]
    y1all = [nc.dram_tensor(f"y1all{q}", [1024, QT], BF16) for q in range(NQ)]
    io["x1s"] = nc.dram_tensor("x1s_scr", [SEQ, 256], F32).ap()
    y0buf = [Buf(f"y0all{q}") for q in range(NQ)]
    y1buf = [Buf(f"y1all{q}") for q in range(NQ)]
    ccsem = S.new_sem("cc")

    def gather(loc, dst, buf, out_toks):
        pq = S.q["pool"]
        mx = {}
        for (sem, val) in out_toks:
            mx[sem] = max(mx.get(sem, 0), val)
        for sem, val in mx.items():
            S._wait(pq, sem, val)
        ins = nc.gpsimd.collective_compute("AllGather", ALU.bypass, replica_groups=GROUPS, ins=[loc.ap().opt()], outs=[dst.ap().opt()])
        ccsem.v += 1
        ins.then_inc(ccsem.h, 1)
        buf.w = (ccsem, ccsem.v)

    def after_A(j, out_toks):
        if j % QB == QB - 1:
            q = j // QB
            gather(y0loc[q], y0all[q], y0buf[q], out_toks)

    def after_B(j, out_toks):
        if j % QB == QB - 1:
            q = j // QB
            gather(y1loc[q], y1all[q], y1buf[q], out_toks)

    io["y0dst"] = lambda j, which: y0loc[j // QB][which * 128:(which + 1) * 128, (j % QB) * 128:(j % QB + 1) * 128]
    io["y0src"] = lambda j: y0all[j // QB].ap().rearrange("(c p) t -> p c t", p=128)[:, :, (j % QB) * 128:(j % QB + 1) * 128]
    io["y0buf"] = lambda j: y0buf[j // QB]
    io["y1dst"] = lambda j, hd: y1loc[j // QB][hd * 128:(hd + 1) * 128, (j % QB) * 128:(j % QB + 1) * 128]
    io["y1src"] = lambda j: y1all[j // QB].ap().rearrange("(c p) t -> p c t", p=128)[:, :, (j % QB) * 128:(j % QB + 1) * 128]
    io["y1buf"] = lambda j: y1buf[j // QB]
    io["y1src4"] = lambda g: y1all[(4 * g) // QB].ap().rearrange("(c p) t -> p c t", p=128)[:, :, ((4 * g) % QB) * 128:((4 * g) % QB + 4) * 128]

    wout0_bf = S.sbuf("wout0_bf", [128, 8, 1024], BF16)
    win1_bf = S.sbuf("win1_bf", [128, 8, 1024], BF16)
    normw1_t = S.sbuf("g_normw1", [128, 8], F32)
    S.op("sp", lambda e: e.dma_start(out=normw1_t[:, :], in_=io["normw1"][:, :]), w=[normw1_t], dma=normw1_t)
    io["bc_weights"] = (wout0_bf, win1_bf)

    def bc_preload_step(i, stage):
        k = i - 2
        if not (0 <= k < 16):
            return
        wsrc, wdst, sc = (io["wout0"], wout0_bf, None) if k < 8 else (io["win1"], win1_bf, normw1_t)
        kc = k % 8
        st = stage[k % 2]
        wv = wsrc.rearrange("(c p) n -> p c n", p=128)
        S.op("sp", lambda e: e.dma_start(out=st[:, 0:1024], in_=wv[:, kc, :]), w=[st], dma=st)
        if sc is None:
            S.op("pool", lambda e: e.tensor_copy(out=wdst[:, kc, :], in_=st[:, 0:1024]), r=[st], w=[wdst])
        else:
            S.op("pool", lambda e: e.tensor_scalar(out=wdst[:, kc, :], in0=st[:, 0:1024], scalar1=sc[:, kc:kc + 1], scalar2=None, op0=ALU.mult), r=[st, sc], w=[wdst])

    io["bc_preload_step"] = bc_preload_step
    io["after_block"] = after_A
    S.push_scope()
    build_A(S, io, nblk, True, True)
    S.barrier()
    S.pop_scope()
    io["after_block"] = after_B
    S.push_scope()
    io["x1s_sb"] = S.sbuf("x1s_sb", [128, nblk, 256], F32)
    S.push_scope()
    build_BC(S, io, nblk)
    S.barrier()
    S.pop_scope()
    del io["after_block"]
    S.push_scope()
    toks = build_D(S, io, nblk)
    S.finish(toks)
    S.pop_scope()
    S.pop_scope()
    S.close()
    return nc, S


def consts_BC():
    s = np.arange(CS)[:, None]
    t = np.arange(CS)[None, :]
    m = (s <= t).astype(np.float32)
    return {"mask32": np.ascontiguousarray(np.tile(m, (1, NCH)))}


def colT(v, n):
    return np.ascontiguousarray(np.asarray(v, dtype=np.float32).reshape(n, 128).T)


def inputs_fused(inp):
    mapsA = inputs_A(inp, NB)
    cb = consts_BC()
    perm = np.concatenate([np.concatenate([np.arange(r * 128, r * 128 + 128), np.arange(512 + r * 128, 512 + r * 128 + 128)]) for r in range(4)])
    wout0_g = inp["ev_w_out"][0][perm, :]
    maps = []
    for core in range(8):
        b, s = core // 4, core % 4
        r = 256 * s
        h0, h1 = 2 * s, 2 * s + 1
        cols = []
        for base in (0, 2048, 1024, 3072):
            for h in (h0, h1):
                cols.append(np.arange(base + h * 128, base + (h + 1) * 128))
        cols = np.concatenate(cols)
        win = np.roll(inp["od_w_in"][0], -r, axis=0)[:, cols]
        lbl = np.stack([inp["hgrn_lb_logits"][sl, h * 128:(h + 1) * 128] for h in (h0, h1) for sl in (0, 1)], axis=1)
        hn = np.stack([inp["hgrn_head_norm"][0][h * 128:(h + 1) * 128] for h in (h0, h1)], axis=1)
        m = dict(mapsA[core])
        m.update(cb)
        m["xr"] = np.ascontiguousarray(np.roll(inp["x"][b], -r, axis=1))
        m["wout0"] = np.ascontiguousarray(np.roll(wout0_g, -r, axis=1))
        m["normw1"] = colT(np.roll(inp["norm_w"][1], -r), 8)
        m["win1"] = np.ascontiguousarray(win)
        m["bcol1"] = colT(inp["od_b_in"][0][cols], 8)
        m["lbl"] = np.ascontiguousarray(lbl.astype(np.float32))
        m["hn1"] = np.ascontiguousarray(hn.astype(np.float32))
        m["wout1"] = np.ascontiguousarray(inp["od_w_out"][0][:, s * 256:(s + 1) * 256])
        maps.append(m)
    return maps


def kernel(**inputs):
    inp = {k: np.asarray(v, dtype=np.float32) for k, v in inputs.items()}
    B = inp["x"].shape[0]
    nc, _ = make_nc_fused()
    res = run_bass_kernel_spmd(nc, inputs_fused(inp), core_ids=list(range(8)))
    out = np.zeros((B, SEQ, D), dtype=np.float32)
    for core in range(8):
        b, s = core // 4, core % 4
        out[b][:, s * 256:(s + 1) * 256] = res.results[core]["outs"]
    return out
```

```python
import numpy as np
import ml_dtypes
from contextlib import ExitStack
import threading
import concourse.bass as bass
import concourse.mybir as mybir
from concourse.bass_utils import run_bass_kernel_spmd

F32 = mybir.dt.float32
BF16 = mybir.dt.bfloat16
AF = mybir.ActivationFunctionType
ALU = mybir.AluOpType
AX = mybir.AxisListType
NPBF = ml_dtypes.bfloat16

SEM_LIMIT = 30000
FUSE_WAIT = True
NO_SES = set()
AW = 0.75


class Sem:
    __slots__ = ("h", "v")

    def __init__(self, h):
        self.h = h
        self.v = 0


class Buf:
    __slots__ = ("name", "w", "r", "dsem", "excl")

    def __init__(self, name=""):
        self.name = name
        self.excl = False
        self.w = None
        self.r = {}
        self.dsem = None


class Tile:
    def __init__(self, t, b):
        self.t = t
        self.b = b

    def __getitem__(self, k):
        return self.t[k]


class Queue:
    def __init__(self, name, eng):
        self.name = name
        self.eng = eng
        self.sem = None
        self.known = {}
        self.own = set()


class Sched:
    def __init__(self, nc, same_engine_sync=True):
        self.nc = nc
        self.es = ExitStack()
        self.nsem = 0
        self.same_engine_sync = same_engine_sync
        self.q = {}
        for name, eng in [("pe", nc.tensor), ("dve", nc.vector), ("act", nc.scalar),
                          ("pool", nc.gpsimd), ("sp", nc.sync)]:
            q = Queue(name, eng)
            self.q[name] = q
        for name in ("pe", "dve", "act", "pool"):
            q = self.q[name]
            q.sem = self.new_sem(name)
            q.own.add(q.sem)
        self.ninst = 0
        self.nwait = 0
        self.out_toks = []
        self.scopes = []
        self.dsems = []
        self._coop = None
        self._flags = set()
        self._tls = threading.local()

    def new_sem(self, name):
        self.nsem += 1
        h = self.es.enter_context(self.nc.semaphore(f"s{self.nsem}_{name}"))
        return Sem(h)

    def push_scope(self):
        self.scopes.append(ExitStack())

    def pop_scope(self):
        self.scopes.pop().close()

    def _stack(self):
        return self.scopes[-1] if self.scopes else self.es

    def barrier(self):
        toks = []
        for qn in ("pe", "dve", "act", "pool"):
            for sem in self.q[qn].own:
                if sem.v > 0:
                    toks.append((sem, sem.v))
        for sem in self.dsems:
            if sem.v > 0:
                toks.append((sem, sem.v))
        for qn in ("pe", "dve", "act", "pool", "sp"):
            q = self.q[qn]
            for (sem, val) in toks:
                self._wait(q, sem, val)

    def sbuf(self, name, shape, dtype):
        t = self._stack().enter_context(self.nc.sbuf_tensor(name, list(shape), dtype))
        return Tile(t, Buf(name))

    def psum(self, name, shape, dtype):
        t = self._stack().enter_context(self.nc.psum_tensor(name, list(shape), dtype))
        b = Buf(name)
        b.excl = True
        return Tile(t, b)

    @staticmethod
    def _bufs(lst):
        out = []
        for x in lst:
            if x is None:
                continue
            out.append(x.b if isinstance(x, Tile) else x)
        return out

    def _wait(self, q, sem, val):
        if q.known.get(sem, 0) >= val:
            return
        q.eng.wait_ge(sem.h, val)
        q.known[sem] = val
        self.nwait += 1

    def op(self, qname, fn, r=(), w=(), dma=None):
        q = self.q[qname]
        rb = self._bufs(r)
        wb = self._bufs(w)
        ex = [b for b in rb if b.excl and b not in wb]
        if ex:
            wb = wb + ex
            rb = [b for b in rb if not b.excl]
        deps = []
        for b in rb:
            if b.w is not None:
                deps.append(b.w)
        for b in wb:
            if b.w is not None:
                deps.append(b.w)
            deps.extend(b.r.items())
        need = {}
        for (sem, val) in deps:
            if sem in q.own and dma is None:
                if qname == "pe" or not self.same_engine_sync or qname in NO_SES:
                    continue
            if q.known.get(sem, 0) >= val:
                continue
            if need.get(sem, 0) < val:
                need[sem] = val
        need = list(need.items())
        fused = None
        if FUSE_WAIT and need and dma is None:
            fused = need.pop()
        for (sem, val) in need:
            self._wait(q, sem, val)
        ins = fn(q.eng)
        if fused is not None:
            ins._wait_ge(fused[0].h, fused[1])
            q.known[fused[0]] = fused[1]
        self.ninst += 1
        if dma is not None:
            ob = dma.b if isinstance(dma, Tile) else dma
            if ob.dsem is None or ob.dsem.v + 16 > SEM_LIMIT:
                ob.dsem = self.new_sem("d_" + ob.name)
                self.dsems.append(ob.dsem)
            ob.dsem.v += 16
            ins.then_inc(ob.dsem.h, 16)
            tok = (ob.dsem, ob.dsem.v)
        else:
            if q.sem.v + 1 > SEM_LIMIT:
                q.sem = self.new_sem(qname)
                q.own.add(q.sem)
            q.sem.v += 1
            ins.then_inc(q.sem.h, 1)
            tok = (q.sem, q.sem.v)
            q.known[q.sem] = max(q.known.get(q.sem, 0), 0)
        for b in wb:
            b.w = tok
            b.r = {}
        for b in rb:
            if b in wb:
                continue
            if b.r.get(tok[0], 0) < tok[1]:
                b.r[tok[0]] = tok[1]
        if self._coop is not None:
            st = self._coop
            i = self._tls.idx
            st["cnt"][i] += 1
            if st["cnt"][i] >= st["w"][i]:
                st["cnt"][i] = 0
                self._yield()
        return tok

    def parallel(self, fns, weights=None):
        assert self._coop is None
        n = len(fns)
        if n == 1:
            fns[0]()
            return
        st = {"turn": 0, "alive": [True] * n, "cv": threading.Condition(), "err": None,
              "w": list(weights) if weights else [1] * n, "cnt": [0] * n}
        self._coop = st

        def nxt_alive(i):
            for k in range(1, n + 1):
                c = (i + k) % n
                if st["alive"][c]:
                    return c
            return None

        def runner(i, fn):
            self._tls.idx = i
            with st["cv"]:
                while st["turn"] != i:
                    st["cv"].wait()
            try:
                if st["err"] is None:
                    fn()
            except BaseException as e:
                st["err"] = e
            finally:
                with st["cv"]:
                    st["alive"][i] = False
                    c = nxt_alive(i)
                    st["turn"] = c if c is not None else -1
                    st["cv"].notify_all()

        st["nxt"] = nxt_alive
        ths = [threading.Thread(target=runner, args=(i, f)) for i, f in enumerate(fns)]
        for t in ths:
            t.start()
        for t in ths:
            t.join()
        self._coop = None
        if st["err"] is not None:
            raise st["err"]

    def flag_set(self, name):
        self._flags.add(name)

    def flag_wait(self, name):
        assert self._coop is not None, "flag_wait outside parallel section would deadlock"
        while name not in self._flags:
            if self._coop["err"] is not None:
                raise RuntimeError("sibling emitter failed")
            self._yield()

    def _yield(self):
        st = self._coop
        i = self._tls.idx
        if st["err"] is not None:
            raise RuntimeError("sibling emitter failed")
        with st["cv"]:
            c = st["nxt"](i)
            if c is not None and c != i:
                st["turn"] = c
                st["cv"].notify_all()
                while st["turn"] != i:
                    st["cv"].wait()

    def finish(self, toks):
        q = self.q["sp"]
        for (sem, val) in toks:
            self._wait(q, sem, val)

    def close(self):
        self.es.close()


D = 1024
SEQ = 8192
CS = 64
NCH = 128 // CS
NB = SEQ // 128
EPS = 1e-6


def dram_in(nc, name, shape, dtype):
    return nc.dram_tensor(name, list(shape), dtype, kind="ExternalInput").ap()


def dram_out(nc, name, shape, dtype):
    return nc.dram_tensor(name, list(shape), dtype, kind="ExternalOutput").ap()


def load_cast_weight(S, w_dram, w_bf, ncols, stage, scale_cols=None, nk=8, qname="dve"):
    wv = w_dram.rearrange("(c p) n -> p c n", p=128)
    for kc in range(nk):
        st = stage[kc % len(stage)]
        S.op("sp", lambda e: e.dma_start(out=st[:, 0:ncols], in_=wv[:, kc, :]), w=[st], dma=st)
        if scale_cols is not None:
            S.op(qname, lambda e: e.tensor_scalar(out=w_bf[:, kc, :], in0=st[:, 0:ncols], scalar1=scale_cols[:, kc:kc + 1],
                                                  scalar2=None, op0=ALU.mult), r=[st, scale_cols], w=[w_bf])
        else:
            S.op(qname, lambda e: e.tensor_copy(out=w_bf[:, kc, :], in_=st[:, 0:ncols]), r=[st], w=[w_bf])


def rms_rstd(S, ss, n, npart, ncol, tmp):
    S.op("act", lambda e: e.activation(out=tmp[0:npart, 0:ncol], in_=ss[0:npart, 0:ncol], func=AF.Ln, scale=1.0 / n, bias=EPS),
         r=[ss], w=[tmp])
    S.op("act", lambda e: e.activation(out=ss[0:npart, 0:ncol], in_=tmp[0:npart, 0:ncol], func=AF.Exp, scale=-0.5),
         r=[tmp], w=[ss])


def build_BC(S, io, nblk=NB):
    x, wout0, normw1, win1, bcol1, lbl, hn1 = (io[k] for k in ("xr", "wout0", "normw1", "win1", "bcol1", "lbl", "hn1"))
    identd, mask32d = io["ident"], io["mask32"]
    x1s = io["x1s"]
    out_toks = []

    ident = S.sbuf("c_ident", [128, 128], BF16)
    mask32 = S.sbuf("c_mask32", [CS, 128], F32)
    normw = S.sbuf("c_normw", [128, 8], F32)
    bcol = S.sbuf("c_bcol", [128, 8], F32)
    nbcol = S.sbuf("c_nbcol", [128, 8], F32)
    lblt = S.sbuf("c_lbl", [128, 4], F32)
    lb = S.sbuf("c_lb", [128, 2], F32)
    oml = S.sbuf("c_oml", [128, 2], F32)
    hn = S.sbuf("c_hn", [128, 2], F32)
    ones = S.sbuf("c_ones", [128, 128], F32)
    for (t, d) in ((ident, identd), (mask32, mask32d), (normw, normw1), (bcol, bcol1), (lblt, lbl), (hn, hn1)):
        S.op("sp", lambda e: e.dma_start(out=t[:, :], in_=d[:, :]), w=[t], dma=t)
    S.op("pool", lambda e: e.memset(ones[:, :], 1.0), w=[ones])
    S.op("dve", lambda e: e.tensor_scalar(out=nbcol[:, :], in0=bcol[:, :], scalar1=-1.0, scalar2=None, op0=ALU.mult), r=[bcol], w=[nbcol])
    for hd in range(2):
        S.op("dve", lambda e: e.tensor_tensor(out=lb[:, hd:hd + 1], in0=lblt[:, 2 * hd + 1:2 * hd + 2], in1=lblt[:, 2 * hd:2 * hd + 1], op=ALU.subtract),
             r=[lblt], w=[lb])
    S.op("act", lambda e: e.activation(out=lb[:, :], in_=lb[:, :], func=AF.Exp), r=[lb], w=[lb])
    S.op("dve", lambda e: e.tensor_scalar(out=lb[:, :], in0=lb[:, :], scalar1=1.0, scalar2=None, op0=ALU.add), r=[lb], w=[lb])
    S.op("dve", lambda e: e.reciprocal(out=lb[:, :], in_=lb[:, :]), r=[lb], w=[lb])
    S.op("dve", lambda e: e.tensor_scalar(out=oml[:, :], in0=lb[:, :], scalar1=-1.0, scalar2=1.0, op0=ALU.mult, op1=ALU.add), r=[lb], w=[oml])

    if "bc_weights" in io:
        wout0_bf, win1_bf = io["bc_weights"]
    else:
        stage = [S.sbuf(f"wstage{i}", [128, 1024], F32) for i in range(2)]
        wout0_bf = S.sbuf("wout0_bf", [128, 8, 1024], BF16)
        win1_bf = S.sbuf("win1_bf", [128, 8, 1024], BF16)
        load_cast_weight(S, wout0, wout0_bf, 1024, stage)
        load_cast_weight(S, win1, win1_bf, 1024, stage, scale_cols=normw)

    x_t = [S.sbuf(f"x_t{i}", [128, 1024], F32) for i in range(2)]
    y0_t = [S.sbuf(f"y0_t{i}", [128, 8, 128], BF16) for i in range(2)]
    x1_t = [S.sbuf(f"x1_t{i}", [128, 1024], F32) for i in range(2)]
    junk = S.sbuf("junk", [128, 1024], BF16)
    ssx = S.sbuf("ssx", [128, 1], F32)
    sstmp = S.sbuf("sstmp", [128, 4], F32)
    h_bf = S.sbuf("h_bf", [128, 1024], BF16)
    hT = S.sbuf("hT", [128, 1024], BF16)
    t1 = [S.sbuf(f"t1_{i}", [128, 128], F32) for i in range(2)]
    t2 = [S.sbuf(f"t2_{i}", [128, 128], F32) for i in range(2)]
    t3 = [S.sbuf(f"t3_{i}", [128, 128], F32) for i in range(2)]
    fT = [S.sbuf(f"fT{i}", [128, 128], F32) for i in range(2)]
    lfT = [S.sbuf(f"lfT{i}", [128, 128], F32) for i in range(2)]
    kT = [S.sbuf(f"kT{i}", [128, 128], F32) for i in range(2)]
    qT = [S.sbuf(f"qT{i}", [128, 128], F32) for i in range(2)]
    szT_l = [[S.sbuf(f"szT{p}_{i}", [128, 128], F32) for i in range(2)] for p in range(2)]
    szT = szT_l[0]
    vT_l = [[S.sbuf(f"vT{p}_{i}", [128, 128], BF16) for i in range(2)] for p in range(2)]
    pre_l = [[S.sbuf(f"pre{p}_{g}", [128, 128], F32) for g in range(8)] for p in range(2)]
    G = [S.sbuf(f"G{i}", [128, 128], F32) for i in range(2)]
    eq = [S.sbuf(f"eq{i}", [128, 128], F32) for i in range(2)]
    ek = [S.sbuf(f"ek{i}", [128, 128], F32) for i in range(2)]
    dec_l = [[S.sbuf(f"dec{p}_{i}", [128, NCH], F32) for i in range(2)] for p in range(2)]
    dec = dec_l[0]
    qt_bf_l = [[S.sbuf(f"qt_bf{p}_{i}", [128, 128], BF16) for i in range(2)] for p in range(2)]
    kt_bf = [S.sbuf(f"kt_bf{i}", [128, 128], BF16) for i in range(2)]
    kh_bf = [S.sbuf(f"kh_bf{i}", [128, 128], BF16) for i in range(2)]
    AT_bf_l = [[S.sbuf(f"AT_bf{p}_{i}", [CS, 128], BF16) for i in range(2)] for p in range(2)]
    kh_tok_l = [[S.sbuf(f"kh_tok{p}_{i}", [CS, NCH * 128], BF16) for i in range(2)] for p in range(2)]
    v_tok_l = [[S.sbuf(f"v_tok{p}_{i}", [CS, NCH * 128], BF16) for i in range(2)] for p in range(2)]
    S32 = [S.sbuf(f"S32_{i}", [128, 128], F32) for i in range(2)]
    S_bf = [S.sbuf(f"S_bf{i}", [128, 128], BF16) for i in range(2)]
    osq_l = [S.sbuf(f"osq{i}", [CS, NCH * 128], F32) for i in range(2)]
    oss_l = [S.sbuf(f"oss{i}", [CS, NCH], F32) for i in range(2)]
    on_bf_l = [S.sbuf(f"on_bf{i}", [CS, NCH * 128], BF16) for i in range(2)]
    sstmp_l = [S.sbuf(f"sstmp_h{i}", [128, 4], F32) for i in range(2)]
    y_bf = [S.sbuf(f"y_bf{i}", [128, 128], BF16) for i in range(4)]

    ps_a = [S.psum(f"ps_a{i}", [128, 512], F32) for i in range(2)]
    ps_hT = S.psum("ps_hT", [128, 1024], BF16)
    ps_m1 = S.psum("ps_m1", [128, 512], F32)
    ps_m2_l = [S.psum(f"ps_m2_{i}", [128, 1024], BF16) for i in range(2)]
    ps_o_l = [S.psum(f"ps_o_{i}", [128, 512], F32) for i in range(2)]

    for hd in range(2):
        S.op("pool", lambda e: e.memset(S32[hd][:, :], 0.0), w=[S32[hd]])
        S.op("pool", lambda e: e.memset(S_bf[hd][:, :], 0.0), w=[S_bf[hd]])

    def load(j):
        sl = j % 2
        S.op("sp", lambda e: e.dma_start(out=x_t[sl][:, :], in_=x[j * 128:(j + 1) * 128, :]), w=[x_t[sl]], dma=x_t[sl])
        S.op("sp", lambda e: e.dma_start(out=y0_t[sl][:, :, :], in_=io["y0src"](j)), r=[io["y0buf"](j)], w=[y0_t[sl]], dma=y0_t[sl])

    load(0)

    def common(j):
        sl = j % 2
        if j + 1 < nblk:
            load(j + 1)
        xt, yt, x1 = x_t[sl], y0_t[sl], x1_t[sl]
        for n in range(2):
            for c in range(8):
                S.op("pe", lambda e: e.matmul(ps_a[n][:, :], lhsT=yt[:, c, :], rhs=wout0_bf[:, c, n * 512:(n + 1) * 512],
                                              start=(c == 0), stop=(c == 7)), r=[yt, wout0_bf], w=[ps_a[n]])
        for n in range(2):
            S.op("dve", lambda e: e.tensor_tensor(out=x1[:, n * 512:(n + 1) * 512], in0=ps_a[n][:, :], in1=xt[:, n * 512:(n + 1) * 512], op=ALU.add),
                 r=[ps_a[n], xt], w=[x1])
        out_toks.append(S.op("pool", lambda e: e.dma_start(out=x1s[j * 128:(j + 1) * 128, :], in_=x1[:, 0:256]), r=[x1], dma=x1))
        S.op("act", lambda e: e.activation(out=junk[:, :], in_=x1[:, :], func=AF.Square, accum_out=ssx[:, 0:1]), r=[x1], w=[junk, ssx])
        rms_rstd(S, ssx, 1024, 128, 1, sstmp)
        S.op("act", lambda e: e.activation(out=h_bf[:, :], in_=x1[:, :], func=AF.Copy, scale=ssx[:, 0:1]), r=[x1, ssx], w=[h_bf])
        for hf in range(2):
            for c in range(4):
                cc = hf * 4 + c
                S.op("pe", lambda e: e.transpose(out=ps_hT[:, c * 128:(c + 1) * 128], in_=h_bf[:, cc * 128:(cc + 1) * 128], identity=ident[:, :]),
                     r=[h_bf, ident], w=[ps_hT])
            S.op("dve", lambda e: e.tensor_copy(out=hT[:, hf * 512:(hf + 1) * 512], in_=ps_hT[:, 0:512]), r=[ps_hT], w=[hT])
        for o in range(8):
            pt = ps_a[o // 4]
            for c in range(8):
                S.op("pe", lambda e: e.matmul(pt[:, (o % 4) * 128:(o % 4 + 1) * 128], lhsT=win1_bf[:, c, o * 128:(o + 1) * 128],
                                              rhs=hT[:, c * 128:(c + 1) * 128], start=(c == 0), stop=(c == 7)), r=[win1_bf, hT], w=[pt])
        for g in range(8):
            dst = vT_l[sl][g - 4] if g in (4, 5) else pre_l[sl][g]
            srcp = ps_a[g // 4][:, (g % 4) * 128:(g % 4 + 1) * 128]
            if g % 2 == 0:
                S.op("act", lambda e: e.activation(out=dst[:, :], in_=srcp, func=AF.Identity, bias=bcol[:, g:g + 1]), r=[ps_a[g // 4], bcol], w=[dst])
            else:
                S.op("dve", lambda e: e.tensor_scalar(out=dst[:, :], in0=srcp, scalar1=bcol[:, g:g + 1], scalar2=None, op0=ALU.add), r=[ps_a[g // 4], bcol], w=[dst])

    if True:
        def head(hd, j, part):
            sl = j % 2
            vT = vT_l[sl]
            pre = pre_l[sl]
            ps_m2, ps_o, osq, oss, on_bf, sstmp_h = ps_m2_l[hd], ps_o_l[hd], osq_l[hd], oss_l[hd], on_bf_l[hd], sstmp_l[hd]
            ma = hd * 256
            ms = hd * 256 + 128
            ps_f = ps_a[0][:, hd * 128:(hd + 1) * 128]
            ps_q = ps_a[0][:, (2 + hd) * 128:(3 + hd) * 128]
            ps_v = ps_a[1][:, hd * 128:(hd + 1) * 128]
            ps_z = ps_a[1][:, (2 + hd) * 128:(3 + hd) * 128]
            T1, FT, LF, KT, QT, SZ, VT, GG, EQ, EK, DEC = t1[hd], fT[hd], lfT[hd], kT[hd], qT[hd], szT_l[sl][hd], vT[hd], G[hd], eq[hd], ek[hd], dec_l[sl][hd]
            qtb, ATb, khk, vtk = qt_bf_l[sl][hd], AT_bf_l[sl][hd], kh_tok_l[sl][hd], v_tok_l[sl][hd]
            T2, T3 = t2[hd], t3[hd]
            if part == 0:
                S.op("act", lambda e: e.activation(out=T1[:, :], in_=pre[hd][:, :], func=AF.Exp, scale=-1.0), r=[pre[hd]], w=[T1])
                S.op("act", lambda e: e.activation(out=T1[:, :], in_=T1[:, :], func=AF.Ln, bias=1.0), r=[T1], w=[T1])
                S.op("act", lambda e: e.activation(out=T1[:, :], in_=T1[:, :], func=AF.Exp, scale=-1.0), r=[T1], w=[T1])
                S.op("dve", lambda e: e.tensor_scalar(out=FT[:, :], in0=T1[:, :], scalar1=oml[:, hd:hd + 1], scalar2=lb[:, hd:hd + 1], op0=ALU.mult, op1=ALU.add),
                     r=[T1, oml, lb], w=[FT])
                S.op("act", lambda e: e.activation(out=LF[:, :], in_=FT[:, :], func=AF.Ln), r=[FT], w=[LF])
                S.op("dve", lambda e: e.tensor_scalar(out=KT[:, :], in0=FT[:, :], scalar1=-1.0, scalar2=1.0, op0=ALU.mult, op1=ALU.add), r=[FT], w=[KT])
                S.op("act", lambda e: e.activation(out=T2[:, :], in_=pre[2 + hd][:, :], func=AF.Exp, scale=-1.0), r=[pre[2 + hd]], w=[T2])
                S.op("act", lambda e: e.activation(out=T2[:, :], in_=T2[:, :], func=AF.Ln, bias=1.0), r=[T2], w=[T2])
                S.op("act", lambda e: e.activation(out=T2[:, :], in_=T2[:, :], func=AF.Exp, scale=-1.0), r=[T2], w=[T2])
                S.op("pool", lambda e: e.tensor_tensor(out=QT[:, :], in0=pre[2 + hd][:, :], in1=T2[:, :], op=ALU.mult), r=[pre[2 + hd], T2], w=[QT])
                S.op("act", lambda e: e.activation(out=T3[:, :], in_=pre[6 + hd][:, :], func=AF.Exp, scale=-1.0), r=[pre[6 + hd]], w=[T3])
                S.op("act", lambda e: e.activation(out=T3[:, :], in_=T3[:, :], func=AF.Ln, bias=1.0), r=[T3], w=[T3])
                S.op("act", lambda e: e.activation(out=T3[:, :], in_=T3[:, :], func=AF.Exp, scale=-1.0), r=[T3], w=[T3])
                S.op("pool", lambda e: e.tensor_tensor(out=SZ[:, :], in0=pre[6 + hd][:, :], in1=T3[:, :], op=ALU.mult), r=[pre[6 + hd], T3], w=[SZ])
                for ch in range(NCH):
                    S.op("pe", lambda e: e.transpose(out=ps_m2[0:CS, 512 + ch * 128:512 + (ch + 1) * 128], in_=VT[:, ch * CS:(ch + 1) * CS], identity=ident[:, :]),
                         r=[VT, ident], w=[ps_m2])
                S.op("act", lambda e: e.copy(out=vtk[:, :], in_=ps_m2[0:CS, 512:512 + NCH * 128]), r=[ps_m2], w=[vtk])
                for ch in range(NCH):
                    S.op("dve", lambda e: e.tensor_tensor_scan(out=GG[:, ch * CS:(ch + 1) * CS], data0=ones[:, 0:CS], data1=LF[:, ch * CS:(ch + 1) * CS],
                                                               initial=0.0, op0=ALU.mult, op1=ALU.add), r=[ones, LF], w=[GG])
                S.op("act", lambda e: e.activation(out=EQ[:, :], in_=GG[:, :], func=AF.Exp), r=[GG], w=[EQ])
                S.op("act", lambda e: e.activation(out=EK[:, :], in_=GG[:, :], func=AF.Exp, scale=-1.0), r=[GG], w=[EK])
                S.op("act", lambda e: e.activation(out=DEC[:, :], in_=GG[:, :].rearrange("p (a b) -> p a b", a=NCH)[:, :, CS - 1], func=AF.Exp), r=[GG], w=[DEC])
                S.op("dve", lambda e: e.tensor_tensor(out=qtb[:, :], in0=QT[:, :], in1=EQ[:, :], op=ALU.mult), r=[QT, EQ], w=[qtb])
                S.op("dve", lambda e: e.tensor_tensor(out=kt_bf[hd][:, :], in0=KT[:, :], in1=EK[:, :], op=ALU.mult), r=[KT, EK], w=[kt_bf[hd]])
                S.op("dve", lambda e: e.tensor_tensor(out=kh_bf[hd][:, :].rearrange("p (a b) -> p a b", a=NCH), in0=kt_bf[hd][:, :].rearrange("p (a b) -> p a b", a=NCH),
                                                      in1=DEC[:, :].unsqueeze(2).to_broadcast([128, NCH, CS]), op=ALU.mult), r=[kt_bf[hd], DEC], w=[kh_bf[hd]])
                for ch in range(NCH):
                    S.op("pe", lambda e: e.matmul(ps_m1[0:CS, ma + ch * CS:ma + (ch + 1) * CS], lhsT=kt_bf[hd][:, ch * CS:(ch + 1) * CS], rhs=qtb[:, ch * CS:(ch + 1) * CS],
                                                  start=True, stop=True), r=[kt_bf[hd], qtb], w=[ps_m1])
                S.op("dve", lambda e: e.tensor_tensor(out=ATb[:, :], in0=ps_m1[0:CS, ma:ma + 128], in1=mask32[:, :], op=ALU.mult), r=[ps_m1, mask32], w=[ATb])
                for ch in range(NCH):
                    S.op("pe", lambda e: e.transpose(out=ps_m2[0:CS, ch * 128:(ch + 1) * 128], in_=kh_bf[hd][:, ch * CS:(ch + 1) * CS], identity=ident[:, :]),
                         r=[kh_bf[hd], ident], w=[ps_m2])
                S.op("act", lambda e: e.copy(out=khk[:, :], in_=ps_m2[0:CS, 0:NCH * 128]), r=[ps_m2], w=[khk])
                return
            for ch in range(NCH):
                S.op("pe", lambda e: e.matmul(ps_o[0:CS, ch * 128:(ch + 1) * 128], lhsT=ATb[:, ch * CS:(ch + 1) * CS], rhs=vtk[:, ch * 128:(ch + 1) * 128],
                                              start=True, stop=False), r=[ATb, vtk], w=[ps_o])
                S.op("pe", lambda e: e.matmul(ps_o[0:CS, ch * 128:(ch + 1) * 128], lhsT=qtb[:, ch * CS:(ch + 1) * CS], rhs=S_bf[hd][:, :],
                                              start=False, stop=True), r=[qtb, S_bf[hd]], w=[ps_o])
                S.op("pe", lambda e: e.matmul(ps_m1[:, ms:ms + 128], lhsT=khk[:, ch * 128:(ch + 1) * 128], rhs=vtk[:, ch * 128:(ch + 1) * 128],
                                              start=True, stop=True), r=[khk, vtk], w=[ps_m1])
                S.op("dve", lambda e: e.scalar_tensor_tensor(out=S32[hd][:, :], in0=S32[hd][:, :], scalar=DEC[:, ch:ch + 1], in1=ps_m1[:, ms:ms + 128],
                                                             op0=ALU.mult, op1=ALU.add), r=[S32[hd], DEC, ps_m1], w=[S32[hd]])
                S.op("act", lambda e: e.copy(out=S_bf[hd][:, :], in_=S32[hd][:, :]), r=[S32[hd]], w=[S_bf[hd]])
            S.op("act", lambda e: e.activation(out=osq[:, :], in_=ps_o[0:CS, 0:NCH * 128], func=AF.Square), r=[ps_o], w=[osq])
            S.op("dve", lambda e: e.tensor_reduce(out=oss[:, :], in_=osq[:, :].rearrange("p (a b) -> p a b", a=NCH), axis=AX.X, op=ALU.add), r=[osq], w=[oss])
            rms_rstd(S, oss, 128, CS, NCH, sstmp_h)
            S.op("dve", lambda e: e.tensor_tensor(out=on_bf[:, :].rearrange("p (a b) -> p a b", a=NCH), in0=ps_o[0:CS, 0:NCH * 128].rearrange("p (a b) -> p a b", a=NCH),
                                                  in1=oss[:, :].unsqueeze(2).to_broadcast([CS, NCH, 128]), op=ALU.mult), r=[ps_o, oss], w=[on_bf])
            for ch in range(NCH):
                S.op("pe", lambda e: e.transpose(out=ps_hT[:, 512 + hd * 128 + ch * CS:512 + hd * 128 + (ch + 1) * CS], in_=on_bf[:, ch * 128:(ch + 1) * 128], identity=ident[0:CS, 0:CS]),
                     r=[on_bf, ident], w=[ps_hT])
            yb = y_bf[(2 * j + hd) % 4]
            S.op("dve", lambda e: e.scalar_tensor_tensor(out=yb[:, :], in0=ps_hT[:, 512 + hd * 128:512 + (hd + 1) * 128], scalar=hn[:, hd:hd + 1], in1=SZ[:, :], op0=ALU.mult, op1=ALU.mult),
                 r=[ps_hT, hn, SZ], w=[yb])
            out_toks.append(S.op("pool", lambda e: e.dma_start(out=io["y1dst"](j, hd), in_=yb[:, :]), r=[yb], dma=yb))

    for i in range(nblk + 2):
        fns = []
        wts = []
        if 2 <= i:
            fns.append(lambda i=i: head(0, i - 2, 1))
            fns.append(lambda i=i: head(1, i - 2, 1))
            wts += [1, 1]
        if 1 <= i <= nblk:
            fns.append(lambda i=i: head(0, i - 1, 0))
            fns.append(lambda i=i: head(1, i - 1, 0))
            wts += [1, 1]
        if i < nblk:
            fns.append(lambda i=i: common(i))
            wts += [3]
        S.parallel(fns, wts)
        if i >= 2 and "after_block" in io:
            io["after_block"](i - 2, out_toks)
    return out_toks


def build_D(S, io, nblk=NB):
    x1s, wout1, outs = io["x1s"], io["wout1"], io["outs"]
    out_toks = []
    stage = [S.sbuf(f"dwstage{i}", [128, 256], F32) for i in range(2)]
    w_bf = S.sbuf("wout1_bf", [128, 8, 256], BF16)
    load_cast_weight(S, wout1, w_bf, 256, stage)
    G = 4
    ng = nblk // G
    y_t = [S.sbuf(f"dy_t{i}", [128, 8, G * 128], BF16) for i in range(2)]
    x_t = [S.sbuf(f"dx_t{i}", [128, G, 256], F32) for i in range(2)]
    o_t = [S.sbuf(f"do_t{i}", [128, G, 256], F32) for i in range(2)]
    ps = [S.psum(f"dps{i}", [128, 512], F32) for i in range(4)]

    def load(g):
        sl = g % 2
        S.op("sp", lambda e: e.dma_start(out=y_t[sl][:, :, :], in_=io["y1src4"](g)), r=[io["y1buf"](g * G)], w=[y_t[sl]], dma=y_t[sl])
        S.op("sp", lambda e: e.dma_start(out=x_t[sl][:, :, :], in_=x1s[g * G * 128:(g + 1) * G * 128, :].rearrange("(b p) c -> p b c", p=128)), w=[x_t[sl]], dma=x_t[sl])

    load(0)
    for g in range(ng):
        sl = g % 2
        if g + 1 < ng:
            load(g + 1)
        for b in range(G):
            pt = ps[b]
            for c in range(8):
                S.op("pe", lambda e: e.matmul(pt[:, 0:256], lhsT=y_t[sl][:, c, b * 128:(b + 1) * 128], rhs=w_bf[:, c, :], start=(c == 0), stop=(c == 7)),
                     r=[y_t[sl], w_bf], w=[pt])
        for b in range(G):
            S.op("dve", lambda e: e.tensor_tensor(out=o_t[sl][:, b, :], in0=ps[b][:, 0:256], in1=x_t[sl][:, b, :], op=ALU.add), r=[ps[b], x_t[sl]], w=[o_t[sl]])
        out_toks.append(S.op("pool", lambda e: e.dma_start(out=outs[g * G * 128:(g + 1) * G * 128, :].rearrange("(b p) c -> p b c", p=128), in_=o_t[sl][:, :, :]), r=[o_t[sl]], dma=o_t[sl]))
    return out_toks


NCOL_A = 1416
WR = 8
BIGV = 1.0e30
NEGM = 30000.0


def sub(tile, name):
    return tile


def build_A(S, io, nblk=NB, do_m=True, do_n=True):
    nc = S.nc
    x = io["x"]
    out_toks = []

    def cload(name, shape, dtype, src):
        t = S.sbuf(name, shape, dtype)
        idx = tuple(slice(None) for _ in shape)
        S.op("sp", lambda e: e.dma_start(out=t[idx], in_=src[idx]), w=[t], dma=t)
        return t

    ident = cload("a_ident", [128, 128], BF16, io["ident"])
    identf = cload("a_identf", [128, 128], F32, io["identf"])
    normw = cload("a_normw", [128, 8], F32, io["normw0"])
    bfm = cload("a_bfm", [128, 2], F32, io["bfm"])
    btm = cload("a_btm", [128, 1160], F32, io["btm"])
    ones = S.sbuf("a_ones", [128, 128], F32)
    S.op("pool", lambda e: e.memset(ones[:, :], 1.0), w=[ones])
    stage = [S.sbuf(f"a_wstage{i}", [128, NCOL_A], F32) for i in range(2)]
    win_bf = S.sbuf("a_win_bf", [128, 8, NCOL_A], BF16)
    load_cast_weight(S, io["win0"], win_bf, NCOL_A, stage, scale_cols=normw)

    if do_m:
        convw = cload("m_convw", [128, 8], F32, io["convw"])
        convb = cload("m_convb", [128, 2], F32, io["convb"])
        fb = cload("m_fb", [1, 2], F32, io["fbias"])
        hnm = cload("m_hnm", [128, 128], F32, io["hnm"])
        mask128 = cload("m_mask128", [128, 128], F32, io["mask128"])
        nfb = S.sbuf("m_nfb", [1, 2], F32)
        S.op("dve", lambda e: e.tensor_scalar(out=nfb[:, :], in0=fb[:, :], scalar1=-1.0, scalar2=None, op0=ALU.mult), r=[fb], w=[nfb])
        qbuf_l = [S.sbuf(f"m_qbuf{i}", [128, 131], F32) for i in range(2)]
        kbuf_l = [S.sbuf(f"m_kbuf{i}", [128, 131], F32) for i in range(2)]
        for i in range(2):
            S.op("pool", lambda e: e.memset(qbuf_l[i][:, :], 0.0), w=[qbuf_l[i]])
            S.op("pool", lambda e: e.memset(kbuf_l[i][:, :], 0.0), w=[kbuf_l[i]])
        qacc = S.sbuf("m_qacc", [128, 128], F32)
        kacc = S.sbuf("m_kacc", [128, 128], F32)
        qsg = S.sbuf("m_qsg", [128, 128], F32)
        ksg = S.sbuf("m_ksg", [128, 128], F32)
        qT_bf_l = [S.sbuf(f"m_qT_bf{i}", [128, 128], BF16) for i in range(2)]
        kT_bf_l = [S.sbuf(f"m_kT_bf{i}", [128, 128], BF16) for i in range(2)]
        kt_tok = S.sbuf("m_kt_tok", [128, 128], BF16)
        v1 = S.sbuf("m_v1", [128, 129], BF16)
        S.op("pool", lambda e: e.memset(v1[:, :], 1.0), w=[v1])
        rows = [S.sbuf(f"m_rows{i}", [1, 1280], F32) for i in range(2)]
        for i in range(2):
            S.op("pool", lambda e: e.memset(rows[i][:, :], 0.0), w=[rows[i]])
        mcols = S.sbuf("m_cols", [128, 8], F32)
        ST_bf = S.sbuf("m_ST_bf", [128, 128], BF16)
        Cn32 = S.sbuf("m_Cn32", [128, 129], F32)
        Cn_bf = S.sbuf("m_Cn_bf", [128, 129], BF16)
        S.op("pool", lambda e: e.memset(Cn32[:, :], 0.0), w=[Cn32])
        S.op("pool", lambda e: e.memset(Cn_bf[:, :], 0.0), w=[Cn_bf])
        hs = S.sbuf("m_hs", [128, 128], F32)
        mss = S.sbuf("m_ss", [128, 2], F32)
        mtmp = S.sbuf("m_tmp", [128, 4], F32)
        eo = S.sbuf("m_eo", [128, 128], F32)
        ez = S.sbuf("m_ez", [128, 128], F32)
        gate = S.sbuf("m_gate", [128, 128], F32)
        ya_bf = S.sbuf("m_ya_bf", [128, 128], BF16)
        yaT = [S.sbuf(f"m_yaT{i}", [128, 128], BF16) for i in range(2)]

    if do_n:
        w6 = cload("n_w6", [128, 384], F32, io["w6"])
        kcw = cload("n_kcw", [8, 64], F32, io["kcw"])
        rcos = cload("n_rcos", [128, nblk, 8], F32, io["rcos"])
        rsin = cload("n_rsin", [128, nblk, 8], F32, io["rsin"])
        rnsin = S.sbuf("n_rnsin", [128, nblk, 8], F32)
        S.op("dve", lambda e: e.tensor_scalar(out=rnsin[:, :, :], in0=rsin[:, :, :], scalar1=-1.0, scalar2=None, op0=ALU.mult), r=[rsin], w=[rnsin])
        ccos = cload("n_ccos", [8, nblk, 8], F32, io["ccos"])
        csin = cload("n_csin", [8, nblk, 8], F32, io["csin"])
        bwide = cload("n_bwide", [128, 254], F32, io["bwide"])
        cmask = cload("n_cmask", [128, 16, 128], BF16, io["cmask"])
        causneg2 = cload("n_causneg2", [128, 256], BF16, io["causneg2"])
        anti2 = cload("n_anti2", [128, 256], BF16, io["anti2"])
        w2kv = S.sbuf("n_w2kv", [128, 128], BF16)
        posT = S.sbuf("n_posT", [128, 32], BF16)
        w1kv = [S.sbuf(f"n_w1kv{i}", [128, 16, 128], BF16) for i in range(2)]
        cnt = 0
        for i, nm in enumerate(("w1k", "w1v")):
            wsrc = io[nm].rearrange("(a l d) h -> a d l h", a=2, d=64)
            for m0 in range(0, 16, 8):
                st = stage[cnt % 2]
                cnt += 1
                stv = st[:, 0:1024].rearrange("p (a b) -> p a b", a=8)
                for a in range(2):
                    S.op("sp", lambda e: e.dma_start(out=stv[64 * a:64 * a + 64], in_=wsrc[a, :, m0:m0 + 8, :]), w=[st], dma=st)
                S.op("dve", lambda e: e.tensor_copy(out=w1kv[i][:, m0:m0 + 8, :], in_=stv), r=[st], w=[w1kv[i]])
        w2st = cload("n_w2st", [128, 128], F32, io["w2kv"])
        S.op("dve", lambda e: e.tensor_copy(out=w2kv[:, :], in_=w2st[:, :]), r=[w2st], w=[w2kv])
        posst = cload("n_posst", [128, 32], F32, io["posT"])
        S.op("dve", lambda e: e.tensor_copy(out=posT[:, :], in_=posst[:, :]), r=[posst], w=[posT])
        cbias = S.sbuf("n_cbias", [128, 2], F32)
        KST = S.sbuf("n_KST", [128, nblk * 128], BF16)
        S.op("sp", lambda e: e.dma_start(out=KST[64:128, :], in_=io["kind"][:, :]), w=[KST], dma=KST)
        KWT = S.sbuf("n_KWT", [64, WR * 128], BF16)
        KSTb = [Buf(f"kst{j}") for j in range(nblk)]
        KWTr = [Buf(f"kwt{j}") for j in range(WR)]
        KWTb = [KWTr[j % WR] for j in range(nblk)]
        VS1 = S.sbuf("n_VS1", [128, nblk, 65], BF16)
        VW1 = S.sbuf("n_VW1", [128, WR, 65], BF16)
        VS1b = [Buf(f"vs{j}") for j in range(nblk)]
        VW1r = [Buf(f"vw{j}") for j in range(WR)]
        VW1b = [VW1r[j % WR] for j in range(nblk)]
        S.op("pool", lambda e: e.memset(VS1[:, :, :], 1.0), w=[VS1] + VS1b)
        S.op("pool", lambda e: e.memset(VW1[:, :, :], 1.0), w=[VW1] + VW1r)
        KCT = S.sbuf("n_KCT", [64, 512], BF16)
        VCT = S.sbuf("n_VCT", [64, 512], BF16)
        S.op("pool", lambda e: e.memset(KCT[:, :], 0.0), w=[KCT])
        S.op("pool", lambda e: e.memset(VCT[:, :], 0.0), w=[VCT])
        VC1 = S.sbuf("n_VC1", [128, 4, 193], BF16)
        S.op("pool", lambda e: e.memset(VC1[:, :, :], 1.0), w=[VC1])
        S.op("pool", lambda e: e.memset(VC1[0:1, 0, 64:65], 0.0), w=[VC1])
        ovst = cload("n_ovst", [128, 4, 128], BF16, io["ov"])
        S.op("pool", lambda e: e.tensor_copy(out=VC1[:, :, 65:193], in_=ovst[:, :, :]), r=[ovst], w=[VC1])
        kcv = S.sbuf("n_kcv", [128, 288], BF16)
        S.op("pool", lambda e: e.memset(kcv[:, :], 0.0), w=[kcv])
        qk6 = S.sbuf("n_qk6", [128, 384], F32)
        nsq = S.sbuf("n_sq", [128, 384], F32)
        nss = S.sbuf("n_ss", [128, 8], F32)
        nstmp = S.sbuf("n_stmp", [128, 8], F32)
        rt = S.sbuf("n_rt", [128, 2, 96], F32)
        qk6_bf = S.sbuf("n_qk6_bf", [128, 384], BF16)
        qT_l = [S.sbuf(f"n_qT{i}", [64, 512], BF16) for i in range(2)]
        kcv_bf = S.sbuf("n_kcv_bf", [128, 256], BF16)
        cu = S.sbuf("n_cu", [128, 16], F32)
        cw = S.sbuf("n_cw", [128, 16], F32)
        cg = S.sbuf("n_cg", [128, 16], BF16)
        c2 = S.sbuf("n_c2", [8, 128], F32)
        c2b = S.sbuf("n_c2b", [8, 128], BF16)
        crt = S.sbuf("n_crt", [8, 8, 4], F32)
        Pc = [S.sbuf(f"n_Pc{i}", [128, 512], BF16) for i in range(4)]
        ocs = S.sbuf("n_ocs", [128, 4], F32)
        imp = S.sbuf("n_imp", [128, 128], F32)
        impw = S.sbuf("n_impw", [128, 128], F32)
        m8 = S.sbuf("n_m8", [128, 16], F32)
        negm = S.sbuf("n_negm", [128, 256], BF16)
        S.op("pool", lambda e: e.memset(negm[:, :], 0.0), w=[negm])
        qA_l = [[S.sbuf(f"n_qA{p}_{i}", [128, 256], BF16) for i in range(2)] for p in range(2)]
        Pa = [S.sbuf(f"n_Pa{i}", [128, 512], BF16) for i in range(3)]
        oT = S.sbuf("n_oT", [65, 512], F32)
        OB_l = [S.sbuf(f"n_OB{i}", [128, 6, 65], F32) for i in range(2)]
        c2n = S.sbuf("n_c2n", [8, 64], F32)
        css = S.sbuf("n_css", [8, 2], F32)
        gts_l = [S.sbuf(f"n_gts{i}", [128, 6], F32) for i in range(2)]
        coef = S.sbuf("n_coef", [128, 6], F32)
        sums = S.sbuf("n_sums", [128, 6], F32)
        ez2_l = [S.sbuf(f"n_ez2{i}", [128, 128], F32) for i in range(2)]
        yacc = S.sbuf("n_yacc", [128, 128], F32)
        yb_bf = S.sbuf("n_yb_bf", [128, 128], BF16)
        ybT = [S.sbuf(f"n_ybT{i}", [128, 128], BF16) for i in range(2)]

    x_t = [S.sbuf(f"a_x_t{i}", [128, 1024], F32) for i in range(2)]
    junk = S.sbuf("a_junk", [128, 1024], BF16)
    ssx = S.sbuf("a_ssx", [128, 1], F32)
    sstmp = S.sbuf("a_sstmp", [128, 4], F32)
    h_bf = S.sbuf("a_h_bf", [128, 1024], BF16)
    hT = S.sbuf("a_hT", [128, 1024], BF16)
    tma_l = [S.sbuf(f"a_tma{i}", [128, 512], F32) for i in range(3)]
    tmb = S.sbuf("a_tmb", [128, 512], F32)
    tmc_l = [S.sbuf(f"a_tmc{i}", [128, 136], F32) for i in range(3)]

    ps_hT = S.psum("a_ps_hT", [128, 1024], BF16)
    ps_bt = S.psum("a_ps_bt", [128, 1024], BF16)
    ps_x = S.psum("a_ps_x", [128, 512], F32)
    ps_y = S.psum("a_ps_y", [128, 512], F32)
    ps_f = S.psum("a_ps_f", [128, 512], F32)
    ps_s = [S.psum(f"a_ps_s{i}", [128, 512], F32) for i in range(2)]
    ps_o = S.psum("a_ps_o", [128, 512], F32)
    bt_q = sub(ps_bt, "bt_q")
    bt_kk = sub(ps_bt, "bt_kk")
    bt_kcv = sub(ps_bt, "bt_kcv")
    bt_m = sub(ps_bt, "bt_m")
    f_st = sub(ps_f, "f_st")
    f_num = sub(ps_f, "f_num")
    f_cn = sub(ps_f, "f_cn")
    f_row = sub(ps_f, "f_row")
    f_misc = sub(ps_f, "f_misc")

    if do_n:
        for kv in range(2):
            for m in range(16):
                S.op("pe", lambda e: e.matmul(ps_s[0][:, kv:kv + 1], lhsT=w1kv[kv][:, m, :], rhs=posT[:, 16 * kv + m:16 * kv + m + 1],
                                              start=(m == 0), stop=(m == 15)), r=[w1kv[kv], posT], w=[ps_s[0]])
        S.op("dve", lambda e: e.tensor_copy(out=cbias[:, :], in_=ps_s[0][:, 0:2]), r=[ps_s[0]], w=[cbias])

    def load(j):
        sl = j % 2
        S.op("sp", lambda e: e.dma_start(out=x_t[sl][:, :], in_=x[j * 128:(j + 1) * 128, :]), w=[x_t[sl]], dma=x_t[sl])

    def sigm_inplace(t, shape_ap):
        S.op("dve", lambda e: e.tensor_scalar(out=shape_ap(t), in0=shape_ap(t), scalar1=1.0, scalar2=None, op0=ALU.add), r=[t], w=[t])
        S.op("dve", lambda e: e.reciprocal(out=shape_ap(t), in_=shape_ap(t)), r=[t], w=[t])

    load(0)

    def common(j):
        sl = j % 2
        if j + 1 < nblk:
            load(j + 1)
        xt = x_t[sl]
        tma, tmc = tma_l[j % 3], tmc_l[j % 3]
        if do_m:
            qbuf, kbuf = qbuf_l[sl], kbuf_l[sl]
        S.op("act", lambda e: e.activation(out=junk[:, :], in_=xt[:, :], func=AF.Square, accum_out=ssx[:, 0:1]), r=[xt], w=[junk, ssx])
        rms_rstd(S, ssx, 1024, 128, 1, sstmp)
        S.op("act", lambda e: e.activation(out=h_bf[:, :], in_=xt[:, :], func=AF.Copy, scale=ssx[:, 0:1]), r=[xt, ssx], w=[h_bf])
        for hf in range(2):
            for c in range(4):
                cc = hf * 4 + c
                S.op("pe", lambda e: e.transpose(out=ps_hT[:, c * 128:(c + 1) * 128], in_=h_bf[:, cc * 128:(cc + 1) * 128], identity=ident[:, :]),
                     r=[h_bf, ident], w=[ps_hT])
            S.op("dve", lambda e: e.tensor_copy(out=hT[:, hf * 512:(hf + 1) * 512], in_=ps_hT[:, 0:512]), r=[ps_hT], w=[hT])

    def common_proj(j):
        sl = j % 2
        tma, tmc = tma_l[j % 3], tmc_l[j % 3]
        if do_m:
            qbuf, kbuf = qbuf_l[sl], kbuf_l[sl]
        for o in range(2):
            for c in range(8):
                S.op("pe", lambda e: e.matmul(ps_x[:, o * 128:(o + 1) * 128], lhsT=win_bf[:, c, o * 128:(o + 1) * 128], rhs=hT[:, c * 128:(c + 1) * 128],
                                              start=(c == 0), stop=(c == 7)), r=[win_bf, hT], w=[ps_x])
        for c in range(8):
            S.op("pe", lambda e: e.matmul(ps_x[:, 256:392], lhsT=hT[:, c * 128:(c + 1) * 128], rhs=win_bf[:, c, 1280:1416],
                                          start=(c == 0), stop=(c == 7)), r=[win_bf, hT], w=[ps_x])
        for c in range(8):
            S.op("pe", lambda e: e.matmul(ps_y[:, :], lhsT=hT[:, c * 128:(c + 1) * 128], rhs=win_bf[:, c, 256:768],
                                          start=(c == 0), stop=(c == 7)), r=[win_bf, hT], w=[ps_y])
        if do_m:
            S.op("act", lambda e: e.activation(out=qbuf[:, 3:131], in_=ps_x[:, 0:128], func=AF.Identity, bias=bfm[:, 0:1]), r=[ps_x, bfm], w=[qbuf])
            S.op("act", lambda e: e.activation(out=kbuf[:, 3:131], in_=ps_x[:, 128:256], func=AF.Identity, bias=bfm[:, 1:2]), r=[ps_x, bfm], w=[kbuf])
        S.op("dve", lambda e: e.tensor_tensor(out=tmc[:, :], in0=ps_x[:, 256:392], in1=btm[:, 1024:1160], op=ALU.add), r=[ps_x, btm], w=[tmc])
        S.op("dve", lambda e: e.tensor_tensor(out=tma[:, :], in0=ps_y[:, :], in1=btm[:, 0:512], op=ALU.add), r=[ps_y, btm], w=[tma])
        if do_n:
            for c in range(8):
                S.op("pe", lambda e: e.matmul(ps_x[:, :], lhsT=hT[:, c * 128:(c + 1) * 128], rhs=win_bf[:, c, 768:1280],
                                              start=(c == 0), stop=(c == 7)), r=[win_bf, hT], w=[ps_x])
            S.op("dve", lambda e: e.tensor_tensor(out=tmb[:, :], in0=ps_x[:, :], in1=btm[:, 512:1024], op=ALU.add), r=[ps_x, btm], w=[tmb])

    def mlstm_block(j, part):
        if True:
            tma, tmc = tma_l[j % 3], tmc_l[j % 3]
            qbuf, kbuf = qbuf_l[j % 2], kbuf_l[j % 2]
            qbuf_n, kbuf_n = qbuf_l[(j + 1) % 2], kbuf_l[(j + 1) % 2]
            qT_bf, kT_bf = qT_bf_l[j % 2], kT_bf_l[j % 2]
            R = rows[j % 2]
            Rp = rows[(j + 1) % 2]
            if part == 0:
                for (eng, buf, bufn, acc, sg, wofs, bcol_, dst, post) in (("dve", qbuf, qbuf_n, qacc, qsg, 0, 0, qT_bf, 1.0), ("dve", kbuf, kbuf_n, kacc, ksg, 4, 1, kT_bf, 128 ** -0.5)):
                    S.op(eng, lambda e: e.tensor_scalar(out=acc[:, :], in0=buf[:, 0:128], scalar1=convw[:, wofs:wofs + 1], scalar2=convb[:, bcol_:bcol_ + 1],
                                                        op0=ALU.mult, op1=ALU.add), r=[buf, convw, convb], w=[acc])
                    for i in range(1, 4):
                        S.op(eng, lambda e: e.scalar_tensor_tensor(out=acc[:, :], in0=buf[:, i:i + 128], scalar=convw[:, wofs + i:wofs + i + 1], in1=acc[:, :],
                                                                   op0=ALU.mult, op1=ALU.add), r=[buf, convw, acc], w=[acc])
                    S.op(eng, lambda e: e.tensor_copy(out=bufn[:, 0:3], in_=buf[:, 128:131]), r=[buf], w=[bufn])
                    S.op("act", lambda e: e.activation(out=sg[:, :], in_=acc[:, :], func=AF.Exp, scale=-1.0), r=[acc], w=[sg])
                    S.op("act", lambda e: e.activation(out=sg[:, :], in_=sg[:, :], func=AF.Ln, bias=1.0), r=[sg], w=[sg])
                    S.op("act", lambda e: e.activation(out=sg[:, :], in_=sg[:, :], func=AF.Exp, scale=-1.0), r=[sg], w=[sg])
                    S.op(eng, lambda e: e.scalar_tensor_tensor(out=dst[:, :], in0=acc[:, :], scalar=post, in1=sg[:, :], op0=ALU.mult, op1=ALU.mult),
                         r=[acc, sg], w=[dst])
                return
            S.op("act", lambda e: e.copy(out=v1[:, 0:128], in_=tma[:, 0:128]), r=[tma], w=[v1])
            S.op("pe", lambda e: e.transpose(out=ps_f[0:1, 0:128], in_=tmc[:, 134:135], identity=identf[:, :]), r=[tmc, identf], w=[ps_f])
            S.op("pe", lambda e: e.transpose(out=ps_f[0:1, 128:256], in_=tmc[:, 135:136], identity=identf[:, :]), r=[tmc, identf], w=[ps_f])
            S.op("act", lambda e: e.copy(out=R[:, 1024:1280], in_=ps_f[0:1, 0:256]), r=[ps_f], w=[R])
            S.op("act", lambda e: e.activation(out=R[:, 896:1024], in_=R[:, 1152:1280], func=AF.Exp, scale=-1.0, bias=nfb[:, 0:1]), r=[R, nfb], w=[R])
            S.op("act", lambda e: e.activation(out=R[:, 0:128], in_=R[:, 896:1024], func=AF.Ln, bias=1.0), r=[R], w=[R])
            S.op("dve", lambda e: e.tensor_tensor_scan(out=R[:, 128:256], data0=ones[0:1, 0:128], data1=R[:, 0:128], initial=Rp[:, 255:256],
                                                       op0=ALU.mult, op1=ALU.add), r=[ones, R, Rp], w=[R])
            S.op("dve", lambda e: e.tensor_tensor(out=R[:, 256:384], in0=R[:, 1024:1152], in1=R[:, 128:256], op=ALU.add), r=[R], w=[R])
            S.op("dve", lambda e: e.tensor_tensor_scan(out=R[:, 384:512], data0=ones[0:1, 0:128], data1=R[:, 256:384], initial=Rp[:, 511:512],
                                                       op0=ALU.mult, op1=ALU.max), r=[ones, R, Rp], w=[R])
            S.op("dve", lambda e: e.tensor_scalar(out=R[:, 896:897], in0=Rp[:, 511:512], scalar1=-1.0, scalar2=None, op0=ALU.mult), r=[Rp], w=[R])
            S.op("act", lambda e: e.activation(out=R[:, 512:640], in_=R[:, 256:384], func=AF.Exp, bias=R[:, 896:897]), r=[R], w=[R])
            S.op("act", lambda e: e.activation(out=R[:, 640:768], in_=R[:, 384:512], func=AF.Exp, scale=-1.0, bias=Rp[:, 511:512]), r=[R, Rp], w=[R])
            S.op("dve", lambda e: e.tensor_tensor(out=R[:, 768:896], in0=R[:, 128:256], in1=R[:, 384:512], op=ALU.subtract), r=[R], w=[R])
            S.op("act", lambda e: e.activation(out=R[:, 768:896], in_=R[:, 768:896], func=AF.Exp), r=[R], w=[R])
            for ci, c0 in enumerate((512, 640, 768)):
                S.op("pe", lambda e: e.matmul(ps_f[:, 386 + ci:387 + ci], lhsT=R[:, c0:c0 + 128], rhs=ones[0:1, 0:1], start=True, stop=True),
                     r=[R, ones], w=[f_misc])
            S.op("pe", lambda e: e.matmul(ps_f[:, 389:390], lhsT=ones[0:1, 0:128], rhs=R[:, 767:768], start=True, stop=True), r=[R, ones], w=[f_misc])
            S.op("dve", lambda e: e.tensor_copy(out=mcols[:, 0:4], in_=ps_f[:, 386:390]), r=[f_misc], w=[mcols])
            S.op("dve", lambda e: e.tensor_tensor(out=mcols[:, 4:5], in0=mcols[:, 0:1], in1=mcols[:, 3:4], op=ALU.mult), r=[mcols], w=[mcols])
            S.op("pe", lambda e: e.matmul(ps_f[:, 0:128], lhsT=kT_bf[:, :], rhs=qT_bf[:, :], start=True, stop=True), r=[kT_bf, qT_bf], w=[f_st])
            S.op("dve", lambda e: e.scalar_tensor_tensor(out=ST_bf[:, :], in0=ps_f[:, 0:128], scalar=mcols[:, 0:1], in1=mask128[:, :], op0=ALU.mult, op1=ALU.mult),
                 r=[f_st, mcols, mask128], w=[ST_bf])
            S.op("pe", lambda e: e.matmul(ps_f[:, 128:257], lhsT=ST_bf[:, :], rhs=v1[:, :], start=True, stop=False), r=[ST_bf, v1], w=[f_num])
            S.op("pe", lambda e: e.matmul(ps_f[:, 128:257], lhsT=qT_bf[:, :], rhs=Cn_bf[:, :], start=False, stop=True), r=[qT_bf, Cn_bf], w=[f_num])
            S.op("pe", lambda e: e.transpose(out=ps_hT[:, 768:896], in_=kT_bf[:, :], identity=ident[:, :]), r=[kT_bf, ident], w=[ps_hT])
            S.op("act", lambda e: e.activation(out=kt_tok[:, :], in_=ps_hT[:, 768:896], func=AF.Copy, scale=mcols[:, 4:5]), r=[ps_hT, mcols], w=[kt_tok])
            S.op("pe", lambda e: e.matmul(ps_f[:, 257:386], lhsT=kt_tok[:, :], rhs=v1[:, :], start=True, stop=True), r=[kt_tok, v1], w=[f_cn])
            S.op("dve", lambda e: e.scalar_tensor_tensor(out=Cn32[:, :], in0=Cn32[:, :], scalar=mcols[:, 3:4], in1=ps_f[:, 257:386], op0=ALU.mult, op1=ALU.add),
                 r=[Cn32, mcols, f_cn], w=[Cn32])
            S.op("act", lambda e: e.copy(out=Cn_bf[:, :], in_=Cn32[:, :]), r=[Cn32], w=[Cn_bf])
            S.op("dve", lambda e: e.tensor_tensor(out=mcols[:, 5:6], in0=ps_f[:, 256:257], in1=mcols[:, 1:2], op=ALU.mult), r=[f_num, mcols], w=[mcols])
            S.op("dve", lambda e: e.tensor_scalar(out=mcols[:, 7:8], in0=mcols[:, 5:6], scalar1=-1.0, scalar2=None, op0=ALU.mult), r=[mcols], w=[mcols])
            S.op("dve", lambda e: e.tensor_tensor(out=mcols[:, 5:6], in0=mcols[:, 5:6], in1=mcols[:, 7:8], op=ALU.max), r=[mcols], w=[mcols])
            S.op("dve", lambda e: e.tensor_tensor(out=mcols[:, 5:6], in0=mcols[:, 5:6], in1=mcols[:, 2:3], op=ALU.max), r=[mcols], w=[mcols])
            S.op("dve", lambda e: e.reciprocal(out=mcols[:, 5:6], in_=mcols[:, 5:6]), r=[mcols], w=[mcols])
            S.op("dve", lambda e: e.tensor_tensor(out=mcols[:, 6:7], in0=mcols[:, 5:6], in1=mcols[:, 1:2], op=ALU.mult), r=[mcols], w=[mcols])
            S.op("act", lambda e: e.activation(out=hs[:, :], in_=ps_f[:, 128:256], func=AF.Copy, scale=mcols[:, 6:7]), r=[f_num, mcols], w=[hs])
            S.op("act", lambda e: e.activation(out=junk[:, 0:128], in_=hs[:, :], func=AF.Square, accum_out=mss[:, 0:1]), r=[hs], w=[junk, mss])
            rms_rstd(S, mss, 128, 128, 1, mtmp)
            S.op("act", lambda e: e.activation(out=eo[:, :], in_=tma[:, 128:256], func=AF.Exp, scale=-1.0), r=[tma], w=[eo])
            S.op("act", lambda e: e.activation(out=ez[:, :], in_=tma[:, 256:384], func=AF.Exp, scale=-1.0), r=[tma], w=[ez])
            S.op("act", lambda e: e.activation(out=eo[:, :], in_=eo[:, :], func=AF.Ln, bias=1.0), r=[eo], w=[eo])
            S.op("act", lambda e: e.activation(out=ez[:, :], in_=ez[:, :], func=AF.Ln, bias=1.0), r=[ez], w=[ez])
            S.op("pool", lambda e: e.tensor_tensor(out=ez[:, :], in0=ez[:, :], in1=eo[:, :], op=ALU.add), r=[ez, eo], w=[ez])
            S.op("act", lambda e: e.activation(out=ez[:, :], in_=ez[:, :], func=AF.Exp, scale=-1.0), r=[ez], w=[ez])
            S.op("pool", lambda e: e.tensor_tensor(out=gate[:, :], in0=tma[:, 256:384], in1=hnm[:, :], op=ALU.mult), r=[tma, hnm], w=[gate])
            S.op("pool", lambda e: e.tensor_tensor(out=gate[:, :], in0=gate[:, :], in1=ez[:, :], op=ALU.mult), r=[gate, ez], w=[gate])
            S.op("dve", lambda e: e.scalar_tensor_tensor(out=ya_bf[:, :], in0=hs[:, :], scalar=mss[:, 0:1], in1=gate[:, :], op0=ALU.mult, op1=ALU.mult),
                 r=[hs, mss, gate], w=[ya_bf])
            S.op("pe", lambda e: e.transpose(out=ps_hT[:, 768:896], in_=ya_bf[:, :], identity=ident[:, :]), r=[ya_bf, ident], w=[ps_hT])
            yT = yaT[j % 2]
            S.op("act", lambda e: e.copy(out=yT[:, :], in_=ps_hT[:, 768:896]), r=[ps_hT], w=[yT])
            out_toks.append(S.op("pool", lambda e: e.dma_start(out=io["y0dst"](j, 0), in_=yT[:, :]), r=[yT], dma=yT))

    L = dict(locals())

    def early(i):
        common(i)
        S.flag_wait(("psfree", i))
        common_proj(i)
        S.flag_set(("c", i))
        if do_n:
            nsa_qk(S, i, L)

    def cmp_chain(i):
        S.flag_wait(("c", i))
        nsa_cmp(S, i, L)

    def topk_chain(i):
        nsa_topk(S, i - 1, L, lambda: S.flag_set(("psfree", i)))
        S.flag_set(("psfree", i))

    lag = 2
    for i in range(nblk + lag):
        if "bc_preload_step" in io:
            io["bc_preload_step"](i, stage)
        fns = []
        cnts = []
        if do_n and i >= 2:
            fns.append(lambda i=i: nsa_attn(S, i - 2, L))
            cnts.append(60 + 2.5 * i)
        if do_m and 2 <= i <= nblk + 1:
            fns.append(lambda i=i: mlstm_block(i - 2, 1))
            cnts.append(70)
        if 1 <= i <= nblk:
            if do_m:
                fns.append(lambda i=i: mlstm_block(i - 1, 0))
                cnts.append(60)
            if do_n:
                fns.append(lambda i=i: topk_chain(i))
                cnts.append(60)
        if not (do_n and 1 <= i <= nblk):
            S.flag_set(("psfree", i))
        if i < nblk:
            fns.append(lambda i=i: early(i))
            cnts.append(100 if do_n else 60)
            if do_n:
                fns.append(lambda i=i: cmp_chain(i))
                cnts.append(65)
        mn = min(cnts)
        wts = [max(1, int(round(AW * c / mn))) for c in cnts] if AW else None
        S.parallel(fns, wts)
        if "after_block" in io and i >= lag:
            io["after_block"](i - lag, out_toks)
    return out_toks


NSA_STOP = 0
NSA_DBG = 0


class _NS:
    def __init__(self, d):
        self.__dict__.update(d)


def nsa_qk(S, j, L):
    V = _NS(L)
    io, nblk = V.io, V.nblk
    tma, tmb, tmc, ident, identf, ones = V.tma_l[j % 3], V.tmb, V.tmc_l[j % 3], V.ident, V.identf, V.ones
    ps_bt, ps_s, ps_x, ps_y, ps_f, ps_o = V.ps_bt, V.ps_s, V.ps_x, V.ps_y, V.ps_f, V.ps_o
    qk6, nsq, nss, nstmp, rt, qk6_bf, w6 = V.qk6, V.nsq, V.nss, V.nstmp, V.rt, V.qk6_bf, V.w6
    qT, qA, OB, gts, ez2 = V.qT_l[j % 2], V.qA_l[j % 2], V.OB_l[j % 2], V.gts_l[j % 2], V.ez2_l[j % 2]
    T0 = j * 128

    def v3(ap, a):
        return ap.rearrange("p (a b) -> p a b", a=a)

    S.op("act", lambda e: e.activation(out=nsq[:, :], in_=tmb[:, 0:384], func=AF.Square), r=[tmb], w=[nsq])
    S.op("dve", lambda e: e.tensor_reduce(out=nss[:, 0:6], in_=v3(nsq[:, :], 6), axis=AX.X, op=ALU.add), r=[nsq], w=[nss])
    S.op("act", lambda e: e.activation(out=nstmp[:, 0:6], in_=nss[:, 0:6], func=AF.Ln, scale=1.0 / 64, bias=EPS), r=[nss], w=[nstmp])
    S.op("act", lambda e: e.activation(out=nss[:, 0:4], in_=nstmp[:, 0:4], func=AF.Exp, scale=-0.5, bias=float(np.log(0.125))), r=[nstmp], w=[nss])
    S.op("act", lambda e: e.activation(out=nss[:, 4:6], in_=nstmp[:, 4:6], func=AF.Exp, scale=-0.5), r=[nstmp], w=[nss])
    S.op("dve", lambda e: e.tensor_tensor(out=v3(qk6[:, :], 6), in0=v3(tmb[:, 0:384], 6), in1=nss[:, 0:6].unsqueeze(2).to_broadcast([128, 6, 64]), op=ALU.mult),
         r=[tmb, nss], w=[qk6])
    S.op("dve", lambda e: e.tensor_tensor(out=qk6[:, :], in0=qk6[:, :], in1=w6[:, :], op=ALU.mult), r=[qk6, w6], w=[qk6])
    S.op("act", lambda e: e.copy(out=qk6_bf[:, :], in_=qk6[:, :]), r=[qk6], w=[qk6_bf])
    x1 = v3(qk6[:, :], 6)[:, :, 0:8]
    x2 = v3(qk6[:, :], 6)[:, :, 8:16]
    cosb = V.rcos[:, j, :].unsqueeze(1).to_broadcast([128, 6, 8])
    sinb = V.rsin[:, j, :].unsqueeze(1).to_broadcast([128, 6, 8])
    r3 = lambda k: rt[:, k, :].rearrange("p (a b) -> p a b", a=6)
    x12 = v3(qk6[:, :], 6)[:, :, 0:16].rearrange("p a (t b) -> p a t b", t=2)
    cos2 = V.rcos[:, j, :].unsqueeze(1).unsqueeze(1).to_broadcast([128, 6, 2, 8])
    ra = rt[:, 0, :].rearrange("p (a t b) -> p a t b", a=6, t=2)
    rb = rt[:, 1, :].rearrange("p (a t b) -> p a t b", a=6, t=2)
    S.op("dve", lambda e: e.tensor_tensor(out=ra, in0=x12, in1=cos2, op=ALU.mult), r=[qk6, V.rcos], w=[rt])
    S.op("pool", lambda e: e.tensor_tensor(out=rb[:, :, 0, :], in0=x2, in1=V.rnsin[:, j, :].unsqueeze(1).to_broadcast([128, 6, 8]), op=ALU.mult), r=[qk6, V.rnsin], w=[rt])
    S.op("pool", lambda e: e.tensor_tensor(out=rb[:, :, 1, :], in0=x1, in1=sinb, op=ALU.mult), r=[qk6, V.rsin], w=[rt])
    S.op("dve", lambda e: e.tensor_tensor(out=v3(qk6_bf[:, :], 6)[:, :, 0:16], in0=rt[:, 0, :].rearrange("p (a c) -> p a c", a=6)[:, :, 0:16],
                                          in1=rt[:, 1, :].rearrange("p (a c) -> p a c", a=6)[:, :, 0:16], op=ALU.add), r=[rt], w=[qk6_bf])
    for h in range(4):
        S.op("pe", lambda e: e.transpose(out=ps_bt[0:64, h * 128:(h + 1) * 128], in_=qk6_bf[:, h * 64:(h + 1) * 64], identity=ident[:, :]),
             r=[qk6_bf, ident], w=[V.bt_q])
    S.op("act", lambda e: e.copy(out=qT[:, :], in_=ps_bt[0:64, 0:512]), r=[V.bt_q], w=[qT])
    S.op("pe", lambda e: e.transpose(out=ps_bt[0:64, 512:640], in_=qk6_bf[:, 256:320], identity=ident[:, :]), r=[qk6_bf, ident], w=[V.bt_kk])
    S.op("pe", lambda e: e.transpose(out=ps_bt[0:64, 640:768], in_=qk6_bf[:, 320:384], identity=ident[:, :]), r=[qk6_bf, ident], w=[V.bt_kk])
    S.op("dve", lambda e: e.tensor_copy(out=V.KST[0:64, T0:T0 + 128], in_=ps_bt[0:64, 512:640]), r=[V.bt_kk], w=[V.KSTb[j]])
    S.op("dve", lambda e: e.tensor_copy(out=V.KWT[:, (j % WR) * 128:(j % WR + 1) * 128], in_=ps_bt[0:64, 640:768]), r=[V.bt_kk], w=[V.KWTb[j]])
    S.op("pool", lambda e: e.tensor_copy(out=V.VS1[:, j, 0:64], in_=tmc[:, 0:64]), r=[tmc], w=[V.VS1b[j]])
    S.op("pool", lambda e: e.tensor_copy(out=V.VW1[:, j % WR, 0:64], in_=tmc[:, 64:128]), r=[tmc], w=[V.VW1b[j]])


def nsa_cmp(S, j, L):
    V = _NS(L)
    io, nblk = V.io, V.nblk
    tma, tmb, tmc, ident, identf, ones = V.tma_l[j % 3], V.tmb, V.tmc_l[j % 3], V.ident, V.identf, V.ones
    ps_bt, ps_s, ps_x, ps_y, ps_f, ps_o = V.ps_bt, V.ps_s, V.ps_x, V.ps_y, V.ps_f, V.ps_o
    qk6, nsq, nss, nstmp, rt, qk6_bf, w6 = V.qk6, V.nsq, V.nss, V.nstmp, V.rt, V.qk6_bf, V.w6
    qT, qA, OB, gts, ez2 = V.qT_l[j % 2], V.qA_l[j % 2], V.OB_l[j % 2], V.gts_l[j % 2], V.ez2_l[j % 2]
    T0 = j * 128

    def v3(ap, a):
        return ap.rearrange("p (a b) -> p a b", a=a)

    kcv, kcv_bf = V.kcv, V.kcv_bf
    for kv in range(2):
        for dup in range(2):
            S.op("act" if dup == 0 else "pool", lambda e: (e.copy if dup == 0 else e.tensor_copy)(out=kcv_bf[:, 128 * kv + 64 * dup:128 * kv + 64 * dup + 64],
                                                                                            in_=tmb[:, 384 + 64 * kv:448 + 64 * kv]), r=[tmb], w=[kcv_bf])
    for kv in range(2):
        S.op("pe", lambda e: e.transpose(out=ps_bt[:, 768 + 128 * kv:896 + 128 * kv], in_=kcv_bf[:, 128 * kv:128 * kv + 128], identity=ident[:, :]),
             r=[kcv_bf, ident], w=[V.bt_kcv])
    for kv in range(2):
        b0 = 144 * kv
        S.op("pool", lambda e: e.tensor_copy(out=kcv[0:64, b0:b0 + 16], in_=kcv[0:64, b0 + 128:b0 + 144]), r=[kcv], w=[kcv])
    for kv in range(2):
        b0 = 144 * kv
        S.op("dve", lambda e: e.tensor_copy(out=kcv[0:64, b0 + 16:b0 + 144], in_=ps_bt[0:64, 768 + 128 * kv:896 + 128 * kv]), r=[V.bt_kcv], w=[kcv])
        S.op("act", lambda e: e.copy(out=kcv[64:128, b0:b0 + 128], in_=ps_bt[64:128, 768 + 128 * kv:896 + 128 * kv]), r=[V.bt_kcv], w=[kcv])
    for kv in range(2):
        kview = kcv[:, 144 * kv:144 * kv + 144].rearrange("p (a b) -> p a b", b=16)
        for m in range(16):
            l = m
            S.op("pe", lambda e: e.matmul(ps_y[:, 8 * kv:8 * kv + 8], lhsT=V.w1kv[kv][:, m, :], rhs=kview[:, (l // 16):(l // 16) + 8, l % 16],
                                          start=(m == 0), stop=(m == 15)), r=[V.w1kv[kv], kcv], w=[ps_y])
    cu, cw, cg, c2, c2b, c2n, css = V.cu, V.cw, V.cg, V.c2, V.c2b, V.c2n, V.css
    for kv in range(2):
        S.op("act", lambda e: e.activation(out=cu[:, 8 * kv:8 * kv + 8], in_=ps_y[:, 8 * kv:8 * kv + 8], func=AF.Identity, bias=V.cbias[:, kv:kv + 1]),
             r=[ps_y, V.cbias], w=[cu])
    S.op("dve", lambda e: e.tensor_tensor(out=cw[:, :], in0=cu[:, :], in1=cu[:, :], op=ALU.mult), r=[cu], w=[cw])
    S.op("dve", lambda e: e.tensor_scalar(out=cw[:, :], in0=cw[:, :], scalar1=0.044715, scalar2=1.0, op0=ALU.mult, op1=ALU.add), r=[cw], w=[cw])
    S.op("dve", lambda e: e.tensor_tensor(out=cw[:, :], in0=cw[:, :], in1=cu[:, :], op=ALU.mult), r=[cw, cu], w=[cw])
    S.op("act", lambda e: e.activation(out=cw[:, :], in_=cw[:, :], func=AF.Exp, scale=-2.0 * 0.7978845608028654), r=[cw], w=[cw])
    S.op("dve", lambda e: e.tensor_scalar(out=cw[:, :], in0=cw[:, :], scalar1=1.0, scalar2=None, op0=ALU.add), r=[cw], w=[cw])
    S.op("dve", lambda e: e.reciprocal(out=cw[:, :], in_=cw[:, :]), r=[cw], w=[cw])
    S.op("dve", lambda e: e.tensor_tensor(out=cg[:, :], in0=cu[:, :], in1=cw[:, :], op=ALU.mult), r=[cu, cw], w=[cg])
    for kv in range(2):
        S.op("pe", lambda e: e.matmul(ps_y[0:8, 16 + 64 * kv:16 + 64 * kv + 64], lhsT=cg[:, 8 * kv:8 * kv + 8], rhs=V.w2kv[:, 64 * kv:64 * kv + 64], start=True, stop=True),
             r=[cg, V.w2kv], w=[ps_y])
    S.op("act", lambda e: e.copy(out=c2[:, :], in_=ps_y[0:8, 16:144]), r=[ps_y], w=[c2])
    S.op("act", lambda e: e.activation(out=c2n[:, :], in_=c2[:, 0:64], func=AF.Square, accum_out=css[:, 0:1]), r=[c2], w=[c2n, css])
    S.op("act", lambda e: e.activation(out=css[:, 1:2], in_=css[:, 0:1], func=AF.Ln, scale=1.0 / 64, bias=EPS), r=[css], w=[css])
    S.op("act", lambda e: e.activation(out=css[:, 0:1], in_=css[:, 1:2], func=AF.Exp, scale=-0.5), r=[css], w=[css])
    S.op("dve", lambda e: e.scalar_tensor_tensor(out=c2n[:, :], in0=c2[:, 0:64], scalar=css[:, 0:1], in1=V.kcw[:, :], op0=ALU.mult, op1=ALU.mult),
         r=[c2, css, V.kcw], w=[c2n])
    S.op("act", lambda e: e.copy(out=c2b[:, 0:64], in_=c2n[:, :]), r=[c2n], w=[c2b])
    S.op("act", lambda e: e.copy(out=c2b[:, 64:128], in_=c2[:, 64:128]), r=[c2], w=[c2b])
    crt = V.crt
    cc, cs_ = V.ccos[:, j, :], V.csin[:, j, :]
    S.op("dve", lambda e: e.tensor_tensor(out=crt[:, :, 0], in0=c2n[:, 0:8], in1=cc, op=ALU.mult), r=[c2n, V.ccos], w=[crt])
    S.op("dve", lambda e: e.tensor_tensor(out=crt[:, :, 1], in0=c2n[:, 8:16], in1=cs_, op=ALU.mult), r=[c2n, V.csin], w=[crt])
    S.op("dve", lambda e: e.tensor_tensor(out=crt[:, :, 2], in0=c2n[:, 0:8], in1=cs_, op=ALU.mult), r=[c2n, V.csin], w=[crt])
    S.op("dve", lambda e: e.tensor_tensor(out=crt[:, :, 3], in0=c2n[:, 8:16], in1=cc, op=ALU.mult), r=[c2n, V.ccos], w=[crt])
    S.op("dve", lambda e: e.tensor_tensor(out=c2b[:, 0:8], in0=crt[:, :, 0], in1=crt[:, :, 1], op=ALU.subtract), r=[crt], w=[c2b])
    S.op("dve", lambda e: e.tensor_tensor(out=c2b[:, 8:16], in0=crt[:, :, 2], in1=crt[:, :, 3], op=ALU.add), r=[crt], w=[c2b])
    for kv in range(2):
        S.op("pe", lambda e: e.transpose(out=ps_bt[0:64, 768 + 8 * kv:776 + 8 * kv], in_=c2b[:, 64 * kv:64 * kv + 64], identity=ident[0:8, 0:8]),
             r=[c2b, ident], w=[V.bt_kcv])
    S.op("dve", lambda e: e.tensor_copy(out=V.KCT[:, 8 * j:8 * j + 8], in_=ps_bt[0:64, 768:776]), r=[V.bt_kcv], w=[V.KCT])
    S.op("dve", lambda e: e.tensor_copy(out=V.VCT[:, 8 * j:8 * j + 8], in_=ps_bt[0:64, 776:784]), r=[V.bt_kcv], w=[V.VCT])
    if j == 0:
        S.op("pool", lambda e: e.memset(V.KCT[:, 0:1], 0.0), w=[V.KCT])
        S.op("pool", lambda e: e.memset(V.VCT[:, 0:1], 0.0), w=[V.VCT])
    ktl = j // 16
    S.op("pe", lambda e: e.transpose(out=ps_bt[:, 784:848], in_=V.VCT[:, ktl * 128:(ktl + 1) * 128], identity=ident[0:64, 0:64]), r=[V.VCT, ident], w=[V.bt_kcv])
    S.op("act", lambda e: e.copy(out=V.VC1[:, ktl, 0:64], in_=ps_bt[:, 784:848]), r=[V.bt_kcv], w=[V.VC1])


def nsa_topk(S, j, L, on_ps_free=None):
    V = _NS(L)
    io, nblk = V.io, V.nblk
    tma, tmb, tmc, ident, identf, ones = V.tma_l[j % 3], V.tmb, V.tmc_l[j % 3], V.ident, V.identf, V.ones
    ps_bt, ps_s, ps_x, ps_y, ps_f, ps_o = V.ps_bt, V.ps_s, V.ps_x, V.ps_y, V.ps_f, V.ps_o
    qk6, nsq, nss, nstmp, rt, qk6_bf, w6 = V.qk6, V.nsq, V.nss, V.nstmp, V.rt, V.qk6_bf, V.w6
    qT, qA, OB, gts, ez2 = V.qT_l[j % 2], V.qA_l[j % 2], V.OB_l[j % 2], V.gts_l[j % 2], V.ez2_l[j % 2]
    T0 = j * 128

    def v3(ap, a):
        return ap.rearrange("p (a b) -> p a b", a=a)

    ktl = j // 16
    nkt = ktl + 1
    Pc = V.Pc
    for kt in range(nkt):
        pt = ps_x
        S.op("pe", lambda e: e.matmul(pt[:, 0:512], lhsT=V.KCT[:, kt * 128:(kt + 1) * 128], rhs=qT[:, 0:512], start=True, stop=True), r=[V.KCT, qT], w=[pt])
        S.op("act", lambda e: e.activation(out=Pc[kt][:, :], in_=pt[:, 0:512], func=AF.Exp), r=[pt], w=[Pc[kt]])
        if kt == nkt - 1:
            S.op("dve", lambda e: e.tensor_tensor(out=v3(Pc[kt][:, :], 4), in0=v3(Pc[kt][:, :], 4),
                                                  in1=V.cmask[:, j % 16, :].unsqueeze(1).to_broadcast([128, 4, 128]), op=ALU.mult), r=[Pc[kt], V.cmask], w=[Pc[kt]])
    ocs, imp, impw, m8 = V.ocs, V.imp, V.impw, V.m8
    for h in range(4):
        bank = ps_x if h < 2 else ps_y
        c0 = (h % 2) * 193
        for kt in range(nkt):
            S.op("pe", lambda e: e.matmul(bank[:, c0:c0 + 193], lhsT=Pc[kt][:, h * 128:(h + 1) * 128], rhs=V.VC1[:, kt, :], start=(kt == 0), stop=(kt == nkt - 1)),
                 r=[Pc[kt], V.VC1], w=[bank])
    for h in range(4):
        bank = ps_x if h < 2 else ps_y
        c0 = (h % 2) * 193
        S.op("dve", lambda e: e.tensor_scalar(out=ocs[:, h:h + 1], in0=bank[:, c0 + 64:c0 + 65], scalar1=1e-30, scalar2=None, op0=ALU.max), r=[bank], w=[ocs])
    S.op("dve", lambda e: e.reciprocal(out=ocs[:, :], in_=ocs[:, :]), r=[ocs], w=[ocs])
    for h in range(4):
        bank = ps_x if h < 2 else ps_y
        c0 = (h % 2) * 193
        if h == 0:
            S.op("dve", lambda e: e.tensor_scalar(out=imp[:, :], in0=bank[:, c0 + 65:c0 + 193], scalar1=ocs[:, 0:1], scalar2=None, op0=ALU.mult), r=[bank, ocs], w=[imp])
        else:
            S.op("dve", lambda e: e.scalar_tensor_tensor(out=imp[:, :], in0=bank[:, c0 + 65:c0 + 193], scalar=ocs[:, h:h + 1], in1=imp[:, :], op0=ALU.mult, op1=ALU.add),
                 r=[bank, ocs, imp], w=[imp])
        if h < 2:
            S.op("act", lambda e: e.copy(out=OB[:, 3 * h, :], in_=bank[:, c0:c0 + 65]), r=[bank], w=[OB])
    if on_ps_free is not None:
        on_ps_free()
    S.op("dve", lambda e: e.tensor_tensor(out=impw[:, :], in0=imp[:, :], in1=V.bwide[:, 126 - 2 * j:254 - 2 * j], op=ALU.add), r=[imp, V.bwide], w=[impw])
    S.op("dve", lambda e: e.memset(impw[:, 0:1], BIGV), w=[impw])
    S.op("dve", lambda e: e.max(out=m8[:, 0:8], in_=impw[:, :]), r=[impw], w=[m8])
    S.op("dve", lambda e: e.match_replace(out=imp[:, :], in_to_replace=m8[:, 0:8], in_values=impw[:, :], imm_value=-3.0e38), r=[m8, impw], w=[imp])
    S.op("dve", lambda e: e.max(out=m8[:, 8:16], in_=imp[:, :]), r=[imp], w=[m8])
    S.op("dve", lambda e: e.tensor_scalar(out=imp[:, :], in0=impw[:, :], scalar1=m8[:, 15:16], scalar2=None, op0=ALU.is_ge), r=[impw, m8], w=[imp])
    S.op("dve", lambda e: e.tensor_scalar(out=V.negm[:, :].rearrange("p (a b) -> p a b", a=2)[:, :, 64:128], in0=imp[:, :].rearrange("p (a b) -> p a b", a=2),
                                          scalar1=NEGM, scalar2=-NEGM, op0=ALU.mult, op1=ALU.add), r=[imp], w=[V.negm])
    nhalf = 2 if j >= 32 else 1
    for hf in range(nhalf):
        S.op("pe", lambda e: e.transpose(out=V.ps_hT[:, 512 + hf * 128:512 + (hf + 1) * 128], in_=V.negm[:, hf * 128:(hf + 1) * 128], identity=ident[:, :]), r=[V.negm, ident], w=[V.ps_hT])
    for hf in range(nhalf):
        S.op("act", lambda e: e.copy(out=qA[hf][64:128, 0:128], in_=V.ps_hT[64:128, 512 + hf * 128:512 + (hf + 1) * 128]), r=[V.ps_hT], w=[qA[hf]])
        S.op("dve", lambda e: e.tensor_copy(out=qA[hf][64:128, 128:256], in_=V.ps_hT[64:128, 512 + hf * 128:512 + (hf + 1) * 128]), r=[V.ps_hT], w=[qA[hf]])
        S.op("pool", lambda e: e.tensor_copy(out=qA[hf][0:64, :], in_=qT[:, 0:256]), r=[qT], w=[qA[hf]])
    S.op("act", lambda e: e.activation(out=gts[:, :], in_=tmc[:, 128:134], func=AF.Exp, scale=-1.0), r=[tmc], w=[gts])
    S.op("dve", lambda e: e.tensor_scalar(out=gts[:, :], in0=gts[:, :], scalar1=1.0, scalar2=None, op0=ALU.add), r=[gts], w=[gts])
    S.op("act", lambda e: e.activation(out=ez2[:, :], in_=tma[:, 384:512], func=AF.Exp, scale=-1.0), r=[tma], w=[ez2])
    S.op("act", lambda e: e.activation(out=ez2[:, :], in_=ez2[:, :], func=AF.Ln, bias=1.0), r=[ez2], w=[ez2])
    S.op("act", lambda e: e.activation(out=ez2[:, :], in_=ez2[:, :], func=AF.Exp, scale=-1.0), r=[ez2], w=[ez2])
    S.op("pool", lambda e: e.tensor_tensor(out=ez2[:, :], in0=ez2[:, :], in1=tma[:, 384:512], op=ALU.mult), r=[ez2, tma], w=[ez2])


def nsa_attn(S, j, L):
    V = _NS(L)
    io, nblk = V.io, V.nblk
    ident, identf = V.ident, V.identf
    ps_s, ps_o = V.ps_s, V.ps_o
    qT, qA, OB, gts, ez2 = V.qT_l[j % 2], V.qA_l[j % 2], V.OB_l[j % 2], V.gts_l[j % 2], V.ez2_l[j % 2]
    T0 = j * 128
    Pa, oT = V.Pa, V.oT
    kts = list(range(max(0, j - 4), j + 1))

    def w_score(idx):
        kt = kts[idx]
        pt = ps_s[idx % 2]
        caus = (kt == j)
        anti = (kt == j - 4)
        S.op("pe", lambda e: e.matmul(pt[:, 0:256], lhsT=V.KWT[:, (kt % WR) * 128:(kt % WR + 1) * 128], rhs=qA[0][0:64, :], start=True, stop=not (caus or anti)),
             r=[V.KWTb[kt], qA[0]], w=[pt])
        if caus:
            S.op("pe", lambda e: e.matmul(pt[:, 0:256], lhsT=ident[:, :], rhs=V.causneg2[:, :], start=False, stop=True), r=[ident, V.causneg2], w=[pt])
        if anti:
            S.op("pe", lambda e: e.matmul(pt[:, 0:256], lhsT=ident[:, :], rhs=V.anti2[:, :], start=False, stop=True), r=[ident, V.anti2], w=[pt])

    w_score(0)
    for idx, kt in enumerate(kts):
        pt = ps_s[idx % 2]
        if idx + 1 < len(kts):
            w_score(idx + 1)
        P = Pa[idx % 3]
        S.op("act", lambda e: e.activation(out=P[:, 0:256], in_=pt[:, 0:256], func=AF.Exp), r=[pt], w=[P])
        S.op("pe", lambda e: e.matmul(ps_o[0:65, 256:512], lhsT=V.VW1[:, kt % WR, :], rhs=P[:, 0:256], start=(idx == 0), stop=(idx == len(kts) - 1)),
             r=[V.VW1b[kt], P], w=[ps_o])
    S.op("act", lambda e: e.copy(out=oT[:, 256:512], in_=ps_o[0:65, 256:512]), r=[ps_o], w=[oT])
    groups = [list(range(g0, min(g0 + 2, j + 1))) for g0 in range(0, j + 1, 2)]

    def s_score(gi):
        pt = ps_s[gi % 2]
        for ti, kt in enumerate(groups[gi]):
            S.op("pe", lambda e: e.matmul(pt[:, ti * 256:(ti + 1) * 256], lhsT=V.KST[:, kt * 128:(kt + 1) * 128], rhs=qA[kt // 32][:, :], start=True, stop=(kt != j)),
                 r=[V.KSTb[kt], V.KST, qA[kt // 32]], w=[pt])
            if kt == j:
                S.op("pe", lambda e: e.matmul(pt[:, ti * 256:(ti + 1) * 256], lhsT=ident[:, :], rhs=V.causneg2[:, :], start=False, stop=True), r=[ident, V.causneg2], w=[pt])

    s_score(0)
    for gi, grp in enumerate(groups):
        pt = ps_s[gi % 2]
        if gi + 1 < len(groups):
            s_score(gi + 1)
        P = Pa[gi % 3]
        w_ = 256 * len(grp)
        S.op("act", lambda e: e.activation(out=P[:, 0:w_], in_=pt[:, 0:w_], func=AF.Exp), r=[pt], w=[P])
        for ti, kt in enumerate(grp):
            S.op("pe", lambda e: e.matmul(ps_o[0:65, 0:256], lhsT=V.VS1[:, kt, :], rhs=P[:, ti * 256:(ti + 1) * 256], start=(kt == 0), stop=(kt == j)), r=[V.VS1b[kt], P], w=[ps_o])
    S.op("act", lambda e: e.copy(out=oT[:, 0:256], in_=ps_o[0:65, 0:256]), r=[ps_o], w=[oT])
    for br in (1, 2):
        for h in range(2):
            src0 = (br - 1) * 256 + h * 128
            pos = (3 * h + br) * 65
            S.op("pe", lambda e: e.transpose(out=ps_s[0][:, pos:pos + 65], in_=oT[0:65, src0:src0 + 128], identity=identf[0:65, 0:65]), r=[oT, identf], w=[ps_s[0]])
    for h in range(2):
        S.op("dve", lambda e: e.tensor_copy(out=OB[:, 3 * h + 1:3 * h + 3, :], in_=ps_s[0][:, (3 * h + 1) * 65:(3 * h + 3) * 65].rearrange("p (a b) -> p a b", a=2)),
             r=[ps_s[0]], w=[OB])
    coef = V.coef
    S.op("dve", lambda e: e.tensor_scalar(out=coef[:, :], in0=OB[:, :, 64], scalar1=1e-30, scalar2=None, op0=ALU.max), r=[OB], w=[coef])
    S.op("dve", lambda e: e.tensor_tensor(out=coef[:, :], in0=coef[:, :], in1=gts[:, :], op=ALU.mult), r=[coef, gts], w=[coef])
    S.op("dve", lambda e: e.reciprocal(out=coef[:, :], in_=coef[:, :]), r=[coef], w=[coef])
    yacc = V.yacc
    for h in range(2):
        ys = yacc[:, h * 64:(h + 1) * 64]
        S.op("dve", lambda e: e.tensor_scalar(out=ys, in0=OB[:, 3 * h, 0:64], scalar1=coef[:, 3 * h:3 * h + 1], scalar2=None, op0=ALU.mult), r=[OB, coef], w=[yacc])
        for br in (1, 2):
            S.op("dve", lambda e: e.scalar_tensor_tensor(out=ys, in0=OB[:, 3 * h + br, 0:64], scalar=coef[:, 3 * h + br:3 * h + br + 1], in1=ys, op0=ALU.mult, op1=ALU.add),
                 r=[OB, coef, yacc], w=[yacc])
    S.op("dve", lambda e: e.tensor_tensor(out=V.yb_bf[:, :], in0=yacc[:, :], in1=ez2[:, :], op=ALU.mult), r=[yacc, ez2], w=[V.yb_bf])
    S.op("pe", lambda e: e.transpose(out=V.ps_hT[:, 896:1024], in_=V.yb_bf[:, :], identity=ident[:, :]), r=[V.yb_bf, ident], w=[V.ps_hT])
    yT = V.ybT[j % 2]
    S.op("act", lambda e: e.copy(out=yT[:, :], in_=V.ps_hT[:, 896:1024]), r=[V.ps_hT], w=[yT])
    V.out_toks.append(S.op("pool", lambda e: e.dma_start(out=io["y0dst"](j, 1), in_=yT[:, :]), r=[yT], dma=yT))


MLSTM_COLS = 2568


def consts_A(nblk=NB):
    c = {}
    c["ident"] = np.eye(128, dtype=np.float32).astype(NPBF)
    c["identf"] = np.eye(128, dtype=np.float32)
    s = np.arange(128)[:, None]
    t = np.arange(128)[None, :]
    c["mask128"] = (s <= t).astype(np.float32)
    cn = np.where(s > t, -NEGM, 0.0).astype(np.float32)
    an = np.where(s <= t, -NEGM, 0.0).astype(np.float32)
    c["causneg2"] = np.ascontiguousarray(np.tile(cn, (1, 2))).astype(NPBF)
    c["anti2"] = np.ascontiguousarray(np.tile(an, (1, 2))).astype(NPBF)
    half = 8
    inv_freq = (np.float32(500000.0) ** (-np.arange(half, dtype=np.float32) / np.float32(half))).astype(np.float32)
    pos = (np.arange(nblk)[None, :] * 128 + np.arange(128)[:, None]).astype(np.float32)
    ang = pos[:, :, None] * inv_freq[None, None, :]
    c["rcos"] = np.cos(ang).astype(np.float32)
    c["rsin"] = np.sin(ang).astype(np.float32)
    n = (8 * np.arange(nblk)[None, :] - 1 + np.arange(8)[:, None])
    cpos = (16 * n + 31).astype(np.float32)
    cang = cpos[:, :, None] * inv_freq[None, None, :]
    c["ccos"] = np.cos(cang).astype(np.float32)
    c["csin"] = np.sin(cang).astype(np.float32)
    u = np.arange(254)[None, :]
    hi = (np.arange(128)[:, None] >= 64).astype(np.int64)
    d = u - 126 - hi
    c["bwide"] = (BIGV * ((d == 0) | (d == -1)) - BIGV * (d > 0)).astype(np.float32)
    cl = np.arange(128)[:, None, None]
    jj = np.arange(16)[None, :, None]
    tl = np.arange(128)[None, None, :]
    c["cmask"] = ((cl <= 8 * jj + 7) & (16 * (cl - 8 * jj) + 15 <= tl)).astype(np.float32).astype(NPBF)
    key = np.arange(nblk * 128)[None, :]
    rr = np.arange(64)[:, None]
    c["kind"] = (rr == ((key // 64) % 64)).astype(np.float32).astype(NPBF)
    col = np.arange(512)
    nn = col - 1
    m = np.arange(128)
    ov = np.clip(np.minimum(16 * nn[:, None] + 32, 64 * m[None, :] + 64) - np.maximum(16 * nn[:, None], 64 * m[None, :]), 0, None).astype(np.float32)
    ov[0, :] = 0.0
    c["ov"] = np.ascontiguousarray(ov.reshape(4, 128, 128).transpose(1, 0, 2)).astype(NPBF)
    return c


def cols_A(s):
    hm = s
    g = s // 2
    ha = 4 * g + 2 * (s % 2)
    hb = ha + 1
    oth = [h for h in range(4 * g, 4 * g + 4) if h not in (ha, hb)]
    N0 = MLSTM_COLS
    ar = np.arange
    fm = [ar(hm * 128, hm * 128 + 128), ar(512 + hm * 128, 512 + hm * 128 + 128)]
    tma = [ar(1024 + hm * 128, 1024 + hm * 128 + 128), ar(1536 + hm * 128, 1536 + hm * 128 + 128), ar(2048 + hm * 128, 2048 + hm * 128 + 128),
           ar(N0 + 1304 + ha * 64, N0 + 1304 + ha * 64 + 128)]
    tmb = [ar(N0 + h * 64, N0 + h * 64 + 64) for h in (ha, hb, oth[0], oth[1])]
    tmb += [ar(N0 + base + g * 64, N0 + base + g * 64 + 64) for base in (768, 1024, 512, 640)]
    tmc = [ar(N0 + 896 + g * 64, N0 + 896 + g * 64 + 64), ar(N0 + 1152 + g * 64, N0 + 1152 + g * 64 + 64),
           ar(N0 + 1280 + ha * 3, N0 + 1280 + ha * 3 + 6), ar(2560 + hm, 2561 + hm), ar(2564 + hm, 2565 + hm)]
    return np.concatenate(fm), np.concatenate(tma + tmb + tmc)


def inputs_A(inp, nblk=NB):
    T = nblk * 128
    cs = consts_A(nblk)
    maps = []
    for core in range(8):
        b, s = core // 4, core % 4
        g = s // 2
        fm, tm = cols_A(s)
        cols = np.concatenate([fm, tm[0:512], tm[512:1024], tm[1024:1160]])
        W = inp["ev_w_in"][0][:, cols]
        bias = inp["ev_b_in"][0]
        m = dict(cs)
        m["x"] = np.ascontiguousarray(inp["x"][b, :T])
        m["win0"] = np.ascontiguousarray(W)
        m["normw0"] = colT(inp["norm_w"][0], 8)
        m["bfm"] = colT(bias[fm], 2)
        m["btm"] = np.ascontiguousarray(np.broadcast_to(bias[tm][None, :], (128, 1160))).astype(np.float32)
        cw = inp["mlstm_conv_w"][0]
        m["convw"] = np.ascontiguousarray(np.concatenate([cw[:, s * 128:(s + 1) * 128].T, cw[:, 512 + s * 128:512 + (s + 1) * 128].T], axis=1)).astype(np.float32)
        cb = inp["mlstm_conv_b"][0]
        m["convb"] = np.ascontiguousarray(np.stack([cb[s * 128:(s + 1) * 128], cb[512 + s * 128:512 + (s + 1) * 128]], axis=1)).astype(np.float32)
        m["fbias"] = np.array([[inp["mlstm_f_bias"][0][s], 0.0]], dtype=np.float32)
        m["hnm"] = np.ascontiguousarray(np.broadcast_to(inp["mlstm_head_norm"][0][s * 128:(s + 1) * 128][None, :], (128, 128))).astype(np.float32)
        qn = inp["nsa_q_norm"][0]
        kn = inp["nsa_k_norm"][0]
        w6 = np.concatenate([qn, qn, qn, qn, kn[1], kn[2]])
        m["w6"] = np.ascontiguousarray(np.broadcast_to(w6[None, :], (128, 384))).astype(np.float32)
        m["kcw"] = np.ascontiguousarray(np.broadcast_to(kn[0][None, :], (8, 64))).astype(np.float32)
        m["w1k"] = np.ascontiguousarray(inp["cmp_k_w1"][0])
        m["w1v"] = np.ascontiguousarray(inp["cmp_v_w1"][0])
        m["w2kv"] = np.ascontiguousarray(np.concatenate([inp["cmp_k_w2"][0], inp["cmp_v_w2"][0]], axis=1))
        pk, pv = inp["cmp_k_pos"][0], inp["cmp_v_pos"][0]
        m["posT"] = np.ascontiguousarray(np.concatenate([np.concatenate([p_[0:16].T, p_[16:32].T], axis=0) for p_ in (pk, pv)], axis=1)).astype(np.float32)
        maps.append(m)
    return maps


A_IN_SPECS = [
    ("ident", [128, 128], BF16), ("identf", [128, 128], F32), ("mask128", [128, 128], F32), ("causneg2", [128, 256], BF16), ("anti2", [128, 256], BF16),
    ("rcos", [128, None, 8], F32), ("rsin", [128, None, 8], F32), ("ccos", [8, None, 8], F32), ("csin", [8, None, 8], F32), ("bwide", [128, 254], F32),
    ("cmask", [128, 16, 128], BF16), ("kind", [64, "T", ], BF16), ("ov", [128, 4, 128], BF16),
    ("win0", [1024, NCOL_A], F32), ("normw0", [128, 8], F32), ("bfm", [128, 2], F32), ("btm", [128, 1160], F32), ("convw", [128, 8], F32),
    ("convb", [128, 2], F32), ("fbias", [1, 2], F32), ("hnm", [128, 128], F32), ("w6", [128, 384], F32), ("kcw", [8, 64], F32),
    ("w1k", [2048, 128], F32), ("w1v", [2048, 128], F32), ("w2kv", [128, 128], F32), ("posT", [128, 32], F32),
]


BC_IN_SPECS = [
    ("xr", [SEQ, 1024], F32), ("wout0", [1024, 1024], F32), ("normw1", [128, 8], F32), ("win1", [1024, 1024], F32),
    ("bcol1", [128, 8], F32), ("lbl", [128, 4], F32), ("hn1", [128, 2], F32), ("mask32", [CS, 128], F32),
]
NQ = 4
QB = NB // NQ
QT = QB * 128
GROUPS = [[0, 1, 2, 3], [4, 5, 6, 7]]


def make_nc_fused():
    nblk = NB
    nc = bass.Bass("TRN2", target_bir_lowering=False)
    S = Sched(nc)
    io = {"x": dram_in(nc, "x", [SEQ, 1024], F32)}
    for (name, shape, dt_) in A_IN_SPECS:
        shape = [nblk if d is None else (nblk * 128 if d == "T" else d) for d in shape]
        io[name] = dram_in(nc, name, shape, dt_)
    for (name, shape, dt_) in BC_IN_SPECS:
        io[name] = dram_in(nc, name, shape, dt_)
    io["wout1"] = dram_in(nc, "wout1", [1024, 256], F32)
    io["outs"] = dram_out(nc, "outs", [SEQ, 256], F32)
    y0loc = [nc.dram_tensor(f"y0loc{q}", [256, QT], BF16) for q in range(NQ)]
    y0all = [nc.dram_tensor(f"y0all{q}", [1024, QT], BF16) for q in range(NQ)]
    y1loc = [nc.dram_tensor(f"y1loc{q}", [256, QT], BF16) for q in range(NQ)]
    y1all = [nc.dram_tensor(f"y1all{q}", [1024, QT], BF16) for q in range(NQ)]
    io["x1s"] = nc.dram_tensor("x1s_scr", [SEQ, 256], F32).ap()
    y0buf = [Buf(f"y0all{q}") for q in range(NQ)]
    y1buf = [Buf(f"y1all{q}") for q in range(NQ)]
    ccsem = S.new_sem("cc")

    def gather(loc, dst, buf, out_toks):
        pq = S.q["pool"]
        mx = {}
        for (sem, val) in out_toks:
            mx[sem] = max(mx.get(sem, 0), val)
        for sem, val in mx.items():
            S._wait(pq, sem, val)
        ins = nc.gpsimd.collective_compute("AllGather", ALU.bypass, replica_groups=GROUPS, ins=[loc.ap().opt()], outs=[dst.ap().opt()])
        ccsem.v += 1
        ins.then_inc(ccsem.h, 1)
        buf.w = (ccsem, ccsem.v)

    def after_A(j, out_toks):
        if j % QB == QB - 1:
            q = j // QB
            gather(y0loc[q], y0all[q], y0buf[q], out_toks)

    def after_B(j, out_toks):
        if j % QB == QB - 1:
            q = j // QB
            gather(y1loc[q], y1all[q], y1buf[q], out_toks)

    io["y0dst"] = lambda j, which: y0loc[j // QB][which * 128:(which + 1) * 128, (j % QB) * 128:(j % QB + 1) * 128]
    io["y0src"] = lambda j: y0all[j // QB].ap().rearrange("(c p) t -> p c t", p=128)[:, :, (j % QB) * 128:(j % QB + 1) * 128]
    io["y0buf"] = lambda j: y0buf[j // QB]
    io["y1dst"] = lambda j, hd: y1loc[j // QB][hd * 128:(hd + 1) * 128, (j % QB) * 128:(j % QB + 1) * 128]
    io["y1src"] = lambda j: y1all[j // QB].ap().rearrange("(c p) t -> p c t", p=128)[:, :, (j % QB) * 128:(j % QB + 1) * 128]
    io["y1buf"] = lambda j: y1buf[j // QB]
    io["y1src4"] = lambda g: y1all[(4 * g) // QB].ap().rearrange("(c p) t -> p c t", p=128)[:, :, ((4 * g) % QB) * 128:((4 * g) % QB + 4) * 128]

    wout0_bf = S.sbuf("wout0_bf", [128, 8, 1024], BF16)
    win1_bf = S.sbuf("win1_bf", [128, 8, 1024], BF16)
    normw1_t = S.sbuf("g_normw1", [128, 8], F32)
    S.op("sp", lambda e: e.dma_start(out=normw1_t[:, :], in_=io["normw1"][:, :]), w=[normw1_t], dma=normw1_t)
    io["bc_weights"] = (wout0_bf, win1_bf)

    def bc_preload_step(i, stage):
        k = i - 2
        if not (0 <= k < 16):
            return
        wsrc, wdst, sc = (io["wout0"], wout0_bf, None) if k < 8 else (io["win1"], win1_bf, normw1_t)
        kc = k % 8
        st = stage[k % 2]
        wv = wsrc.rearrange("(c p) n -> p c n", p=128)
        S.op("sp", lambda e: e.dma_start(out=st[:, 0:1024], in_=wv[:, kc, :]), w=[st], dma=st)
        if sc is None:
            S.op("pool", lambda e: e.tensor_copy(out=wdst[:, kc, :], in_=st[:, 0:1024]), r=[st], w=[wdst])
        else:
            S.op("pool", lambda e: e.tensor_scalar(out=wdst[:, kc, :], in0=st[:, 0:1024], scalar1=sc[:, kc:kc + 1], scalar2=None, op0=ALU.mult), r=[st, sc], w=[wdst])

    io["bc_preload_step"] = bc_preload_step
    io["after_block"] = after_A
    S.push_scope()
    build_A(S, io, nblk, True, True)
    S.barrier()
    S.pop_scope()
    io["after_block"] = after_B
    S.push_scope()
    build_BC(S, io, nblk)
    S.barrier()
    S.pop_scope()
    del io["after_block"]
    S.push_scope()
    toks = build_D(S, io, nblk)
    S.finish(toks)
    S.pop_scope()
    S.close()
    return nc, S


def consts_BC():
    s = np.arange(CS)[:, None]
    t = np.arange(CS)[None, :]
    m = (s <= t).astype(np.float32)
    return {"mask32": np.ascontiguousarray(np.tile(m, (1, NCH)))}


def colT(v, n):
    return np.ascontiguousarray(np.asarray(v, dtype=np.float32).reshape(n, 128).T)


def inputs_fused(inp):
    mapsA = inputs_A(inp, NB)
    cb = consts_BC()
    perm = np.concatenate([np.concatenate([np.arange(r * 128, r * 128 + 128), np.arange(512 + r * 128, 512 + r * 128 + 128)]) for r in range(4)])
    wout0_g = inp["ev_w_out"][0][perm, :]
    maps = []
    for core in range(8):
        b, s = core // 4, core % 4
        r = 256 * s
        h0, h1 = 2 * s, 2 * s + 1
        cols = []
        for base in (0, 2048, 1024, 3072):
            for h in (h0, h1):
                cols.append(np.arange(base + h * 128, base + (h + 1) * 128))
        cols = np.concatenate(cols)
        win = np.roll(inp["od_w_in"][0], -r, axis=0)[:, cols]
        lbl = np.stack([inp["hgrn_lb_logits"][sl, h * 128:(h + 1) * 128] for h in (h0, h1) for sl in (0, 1)], axis=1)
        hn = np.stack([inp["hgrn_head_norm"][0][h * 128:(h + 1) * 128] for h in (h0, h1)], axis=1)
        m = dict(mapsA[core])
        m.update(cb)
        m["xr"] = np.ascontiguousarray(np.roll(inp["x"][b], -r, axis=1))
        m["wout0"] = np.ascontiguousarray(np.roll(wout0_g, -r, axis=1))
        m["normw1"] = colT(np.roll(inp["norm_w"][1], -r), 8)
        m["win1"] = np.ascontiguousarray(win)
        m["bcol1"] = colT(inp["od_b_in"][0][cols], 8)
        m["lbl"] = np.ascontiguousarray(lbl.astype(np.float32))
        m["hn1"] = np.ascontiguousarray(hn.astype(np.float32))
        m["wout1"] = np.ascontiguousarray(inp["od_w_out"][0][:, s * 256:(s + 1) * 256])
        maps.append(m)
    return maps


def kernel(**inputs):
    inp = {k: np.asarray(v, dtype=np.float32) for k, v in inputs.items()}
    B = inp["x"].shape[0]
    nc, _ = make_nc_fused()
    res = run_bass_kernel_spmd(nc, inputs_fused(inp), core_ids=list(range(8)))
    out = np.zeros((B, SEQ, D), dtype=np.float32)
    for core in range(8):
        b, s = core // 4, core % 4
        out[b][:, s * 256:(s + 1) * 256] = res.results[core]["outs"]
    return out
```
